# Optimizing a Trainium2 kernel written in Bass

```python
import jax, jax.numpy as jnp
from jax import lax
import numpy as np

D_MODEL = 2048
BATCH = 8
SEQ = 2048
DEPTH = 4

GRID_W = 64
CTX_LEN = 256
N_MIXERS = 2
N_ATT_LAYERS = (DEPTH + 1) // 2
N_REC_LAYERS = DEPTH // 2
HEAD_DIM = 128
N_HEADS = D_MODEL // HEAD_DIM
N_KV_HEADS = 4
GQA_GROUP = N_HEADS // N_KV_HEADS
KV_DIM = N_KV_HEADS * HEAD_DIM
WINDOW = 128
ATT_BLOCK = 128
ROPE_BASE = 10000.0
REC_DK = 128
REC_HEADS = D_MODEL // REC_DK
REC_DV = D_MODEL // REC_HEADS
REC_CHUNK = 64
D_FF = 4 * D_MODEL
N_MOD = 6
NORM_EPS = 1e-6

kernel_name = "hybrid_swa_hgrn2_dit_trunk"


def rms_norm(x, g):
    x32 = x.astype(jnp.float32)
    y = x32 * lax.rsqrt(jnp.mean(x32 * x32, axis=-1, keepdims=True) + NORM_EPS)
    return (y * g.astype(jnp.float32)).astype(x.dtype)


def modulate(h, shift, scale):
    return h * (1 + scale) + shift


def axial_rope_tables(n_tokens):
    rows = n_tokens // GRID_W
    row = jnp.repeat(jnp.arange(rows), GRID_W).astype(jnp.float32)
    col = jnp.tile(jnp.arange(GRID_W), rows).astype(jnp.float32)
    quarter = HEAD_DIM // 4
    inv_freq = ROPE_BASE ** (-jnp.arange(quarter, dtype=jnp.float32) / quarter)
    ang_r = row[:, None] * inv_freq[None, :]
    ang_c = col[:, None] * inv_freq[None, :]
    ang = jnp.concatenate([ang_r, ang_r, ang_c, ang_c], axis=-1)
    return jnp.cos(ang), jnp.sin(ang)


def apply_axial_rope(x, cos, sin):
    x32 = x.astype(jnp.float32)
    xa = x32.reshape(x.shape[:-1] + (2, 2, HEAD_DIM // 4))
    rot = jnp.stack([-xa[..., 1, :], xa[..., 0, :]], axis=-2).reshape(x.shape)
    bshape = (x.shape[1],) + (1,) * (x.ndim - 3) + (HEAD_DIM,)
    return (x32 * cos.reshape(bshape) + rot * sin.reshape(bshape)).astype(x.dtype)


def windowed_gqa_attention(hx, hc, w_qkv, w_o, sink, cos, sin, need_ctx_out):
    B, T, _ = hx.shape
    L = hc.shape[1]
    nb = T // ATT_BLOCK
    scale = HEAD_DIM ** -0.5

    def split(h):
        p = h @ w_qkv
        n = h.shape[1]
        q = p[..., :D_MODEL].reshape(B, n, N_KV_HEADS, GQA_GROUP, HEAD_DIM)
        k = p[..., D_MODEL:D_MODEL + KV_DIM].reshape(B, n, N_KV_HEADS, HEAD_DIM)
        v = p[..., D_MODEL + KV_DIM:].reshape(B, n, N_KV_HEADS, HEAD_DIM)
        return q, k, v

    qx, kx, vx = split(hx)
    qc, kc, vc = split(hc)
    qx = apply_axial_rope(qx, cos, sin)
    kx = apply_axial_rope(kx, cos, sin)

    qb = qx.reshape(B, nb, ATT_BLOCK, N_KV_HEADS, GQA_GROUP, HEAD_DIM)

    def band(t):
        tp = jnp.pad(t, ((0, 0), (ATT_BLOCK, ATT_BLOCK), (0, 0), (0, 0)))
        tp = tp.reshape(B, nb + 2, ATT_BLOCK, N_KV_HEADS, HEAD_DIM)
        return jnp.concatenate([tp[:, :-2], tp[:, 1:-1], tp[:, 2:]], axis=2)

    kw, vw = band(kx), band(vx)
    qi = jnp.arange(ATT_BLOCK)[:, None]
    kj = jnp.arange(3 * ATT_BLOCK)[None, :]
    in_window = jnp.abs(kj - ATT_BLOCK - qi) <= WINDOW
    kpos = jnp.arange(nb)[:, None] * ATT_BLOCK - ATT_BLOCK + kj
    mask = in_window[None] & ((kpos >= 0) & (kpos < T))[:, None, :]

    s_w = jnp.einsum('bnqhgd,bnkhd->bnhgqk', qb, kw).astype(jnp.float32) * scale
    s_w = jnp.where(mask[None, :, None, None], s_w, -jnp.inf)
    s_c = jnp.einsum('bnqhgd,blhd->bnhgql', qb, kc).astype(jnp.float32) * scale
    sk = sink.astype(jnp.float32).reshape(1, 1, N_KV_HEADS, GQA_GROUP, 1, 1)
    m = jnp.maximum(jnp.maximum(s_w.max(-1, keepdims=True), s_c.max(-1, keepdims=True)), sk)
    p_w = jnp.exp(s_w - m)
    p_c = jnp.exp(s_c - m)
    inv = 1.0 / (p_w.sum(-1, keepdims=True) + p_c.sum(-1, keepdims=True) + jnp.exp(sk - m))
    p_w = (p_w * inv).astype(vw.dtype)
    p_c = (p_c * inv).astype(vc.dtype)
    o = jnp.einsum('bnhgqk,bnkhd->bnqhgd', p_w, vw) + jnp.einsum('bnhgql,blhd->bnqhgd', p_c, vc)
    ox = o.reshape(B, T, D_MODEL) @ w_o

    oc = None
    if need_ctx_out:
        s = jnp.einsum('blhgd,bmhd->bhglm', qc, kc).astype(jnp.float32) * scale
        skc = sink.astype(jnp.float32).reshape(1, N_KV_HEADS, GQA_GROUP, 1, 1)
        mc = jnp.maximum(s.max(-1, keepdims=True), skc)
        pc = jnp.exp(s - mc)
        pc = (pc / (pc.sum(-1, keepdims=True) + jnp.exp(skc - mc))).astype(vc.dtype)
        oc = jnp.einsum('bhglm,bmhd->blhgd', pc, vc).reshape(B, L, D_MODEL) @ w_o
    return ox, oc


def hgrn2_chunk_scan(q, k, v, log_f, s0):
    B, T, H, DK = q.shape
    nc = T // REC_CHUNK

    def to_chunks(a):
        return jnp.moveaxis(a.reshape((B, nc, REC_CHUNK) + a.shape[2:]), 1, 0)

    tril = jnp.tril(jnp.ones((REC_CHUNK, REC_CHUNK), dtype=bool))

    def step(s, inp):
        qc, kc, vc, gc = inp
        b = jnp.cumsum(gc, axis=1)
        o_inter = jnp.einsum('bchk,bhkv->bchv', qc * jnp.exp(b), s)
        diff = b[:, :, None] - b[:, None, :]
        decay = jnp.exp(jnp.where(tril[None, :, :, None, None], diff, -jnp.inf))
        a = jnp.einsum('btshk,bshk->bhts', decay * qc[:, :, None], kc)
        o_intra = jnp.einsum('bhts,bshv->bthv', a, vc)
        b_last = b[:, -1]
        s_new = jnp.exp(b_last)[..., None] * s + jnp.einsum(
            'bshk,bshv->bhkv', kc * jnp.exp(b_last[:, None] - b), vc)
        return s_new, o_inter + o_intra

    s_fin, o = lax.scan(step, s0, (to_chunks(q), to_chunks(k), to_chunks(v), to_chunks(log_f)))
    return jnp.moveaxis(o, 0, 1).reshape(B, T, H, v.shape[-1]), s_fin


def hgrn2_mixer(hx, hc, w_in, w_o, lb, onorm_g, need_ctx_out):
    B = hx.shape[0]

    def project(h):
        n = h.shape[1]
        p = (h @ w_in).astype(jnp.float32)
        q_raw, i_raw, f_fw, f_bw, g_raw = jnp.split(p, 5, axis=-1)
        shp = (B, n, REC_HEADS, REC_DK)
        q = jax.nn.silu(q_raw).reshape(shp)
        v = i_raw.reshape(B, n, REC_HEADS, REC_DV)

        def gates(z, lb_d):
            log_f = jnp.logaddexp(jnp.log(lb_d), jnp.log1p(-lb_d) + jax.nn.log_sigmoid(z))
            return (-jnp.expm1(log_f)).reshape(shp), log_f.reshape(shp)

        return q, v, gates(f_fw, lb[0]), gates(f_bw, lb[1]), g_raw

    def flip(a):
        return jnp.flip(a, axis=1)

    def bidir(q, v, fw, bw, s_fw, s_bw):
        o_f, sf = hgrn2_chunk_scan(q, fw[0], v, fw[1], s_fw)
        o_b, sb = hgrn2_chunk_scan(flip(q), flip(bw[0]), flip(v), flip(bw[1]), s_bw)
        return o_f + flip(o_b), sf, sb

    def readout(o, g, dtype):
        n = o.shape[1]
        o = rms_norm(o, onorm_g) * jax.nn.silu(g).reshape(o.shape)
        return o.reshape(B, n, D_MODEL).astype(dtype) @ w_o

    zero = jnp.zeros((B, REC_HEADS, REC_DK, REC_DV), jnp.float32)
    qc, vc, fwc, bwc, gc = project(hc)
    oc, sc_f, sc_b = bidir(qc, vc, fwc, bwc, zero, zero)
    qx, vx, fwx, bwx, gx = project(hx)
    ox, _, _ = bidir(qx, vx, fwx, bwx, sc_f, sc_b)
    ox = readout(ox, gx, hx.dtype)
    oc = readout(oc, gc, hc.dtype) if need_ctx_out else None
    return ox, oc


def squared_relu_mlp(h, w_up, w_down):
    u = jax.nn.relu(h @ w_up)
    return (u * u) @ w_down


def setup_inputs(seed: int = 0) -> dict:
    key = jax.random.key(seed)
    ks = jax.random.split(key, 18)

    def nrm(k, shape, s):
        return jax.random.normal(k, shape, jnp.float32) * s

    d = D_MODEL
    return {
        "x": nrm(ks[0], (BATCH, SEQ, d), 1.0),
        "c": nrm(ks[1], (BATCH, d), 1.0),
        "ctx": nrm(ks[2], (BATCH, CTX_LEN, d), 1.0),
        "c_ctx": nrm(ks[3], (d,), 1.0),
        "w_ada": nrm(ks[4], (DEPTH, d, N_MOD * d), 0.5 * d ** -0.5),
        "b_ada": nrm(ks[5], (DEPTH, N_MOD * d), 0.01),
        "norm_mix_g": 1.0 + nrm(ks[6], (DEPTH, d), 0.02),
        "norm_mlp_g": 1.0 + nrm(ks[7], (DEPTH, d), 0.02),
        "att_w_qkv": nrm(ks[8], (N_ATT_LAYERS, d, d + 2 * KV_DIM), d ** -0.5),
        "att_w_o": nrm(ks[9], (N_ATT_LAYERS, d, d), d ** -0.5),
        "att_sink": nrm(ks[10], (N_ATT_LAYERS, N_HEADS), 0.5),
        "rec_w_in": nrm(ks[11], (N_REC_LAYERS, d, 5 * d), d ** -0.5),
        "rec_w_o": nrm(ks[12], (N_REC_LAYERS, d, d), d ** -0.5),
        "rec_lb_logits": nrm(ks[13], (N_REC_LAYERS, 2, d), 0.5),
        "rec_onorm_g": 1.0 + nrm(ks[14], (N_REC_LAYERS, REC_DV), 0.02),
        "mlp_w_up": nrm(ks[15], (DEPTH, d, D_FF), d ** -0.5),
        "mlp_w_down": nrm(ks[16], (DEPTH, D_FF, d), D_FF ** -0.5),
        "final_norm_g": 1.0 + nrm(ks[17], (d,), 0.02),
    }


def reference(x, c, ctx, c_ctx, w_ada, b_ada, norm_mix_g, norm_mlp_g, att_w_qkv, att_w_o, att_sink,
              rec_w_in, rec_w_o, rec_lb_logits, rec_onorm_g, mlp_w_up, mlp_w_down, final_norm_g):
    T = x.shape[1]
    cos, sin = axial_rope_tables(T)
    gam = jax.nn.softmax(rec_lb_logits.astype(jnp.float32), axis=0)
    lower_bounds = jnp.cumsum(gam, axis=0) - gam[0:1]
    silu_c = jax.nn.silu(c)
    silu_cc = jax.nn.silu(c_ctx)

    for i in range(DEPTH):
        last = i == DEPTH - 1
        j = i // N_MIXERS
        mx = jnp.split(silu_c @ w_ada[i] + b_ada[i], N_MOD, axis=-1)
        mc = jnp.split(silu_cc @ w_ada[i] + b_ada[i], N_MOD, axis=-1)
        hx = modulate(rms_norm(x, norm_mix_g[i]), mx[0][:, None], mx[1][:, None])
        hc = modulate(rms_norm(ctx, norm_mix_g[i]), mc[0], mc[1])
        if i % N_MIXERS == 0:
            dx, dc = windowed_gqa_attention(hx, hc, att_w_qkv[j], att_w_o[j], att_sink[j], cos, sin, not last)
        else:
            dx, dc = hgrn2_mixer(hx, hc, rec_w_in[j], rec_w_o[j], lower_bounds[j], rec_onorm_g[j], not last)
        x = x + mx[2][:, None] * dx
        hx = modulate(rms_norm(x, norm_mlp_g[i]), mx[3][:, None], mx[4][:, None])
        x = x + mx[5][:, None] * squared_relu_mlp(hx, mlp_w_up[i], mlp_w_down[i])
        if not last:
            ctx = ctx + mc[2] * dc
            hc = modulate(rms_norm(ctx, norm_mlp_g[i]), mc[3], mc[4])
            ctx = ctx + mc[5] * squared_relu_mlp(hc, mlp_w_up[i], mlp_w_down[i])

    return rms_norm(x, final_norm_g)
```

```python
import contextlib
import numpy as np
import concourse.bass as bass
import concourse.mybir as mybir
from concourse.bass_utils import run_bass_kernel_spmd

F32 = mybir.dt.float32
BF16 = mybir.dt.bfloat16
AF = mybir.ActivationFunctionType
ALU = mybir.AluOpType
AX = mybir.AxisListType

ENGS = ("pe", "act", "dve", "pool", "sp")
DMA_SLOTS = 8


class Res:
    __slots__ = ("name", "last_w", "readers", "excl")

    def __init__(self, name="", excl=False):
        self.name = name
        self.last_w = None
        self.readers = []
        self.excl = excl


def PRes():
    return Res("psum", True)


class Op:
    __slots__ = ("eng", "fn", "deps", "needs_inc", "is_dma", "slot", "slot_n", "tok")

    def __init__(self, eng, fn):
        self.eng = eng
        self.fn = fn
        self.deps = []
        self.needs_inc = False
        self.is_dma = False
        self.slot = None
        self.slot_n = 0
        self.tok = None


class Prog:
    def __init__(self):
        self.ops = {e: [] for e in ENGS}
        self.dma_count = {e: 0 for e in ENGS}
        self.slot_last = {}
        self.barrier_deps = {e: [] for e in ENGS}

    def _add_dep(self, op, dep, kind):
        if dep is None or dep is op:
            return
        if not dep.is_dma and dep.eng == op.eng and not op.is_dma:
            if op.eng == "pe" or kind == "war" or kind == "excl":
                return
        for d in op.deps:
            if d is dep:
                return
        op.deps.append(dep)
        if not dep.is_dma:
            dep.needs_inc = True

    def _track(self, op, reads, writes):
        ex = [r for r in reads if r.excl] + [w for w in writes if w.excl]
        if ex:
            reads = [r for r in reads if not r.excl]
            writes = [w for w in writes if not w.excl]
            for x in ex:
                self._add_dep(op, x.last_w, "excl")
                x.last_w = op
        for r in reads:
            self._add_dep(op, r.last_w, "raw")
        for w in writes:
            self._add_dep(op, w.last_w, "waw")
            for rd in w.readers:
                self._add_dep(op, rd, "war")
        for r in reads:
            r.readers.append(op)
        for w in writes:
            w.last_w = op
            w.readers = []
        bd = self.barrier_deps[op.eng]
        if bd:
            for d in bd:
                if d is not op and not (d.eng == op.eng == "pe" and not d.is_dma and not op.is_dma):
                    if d not in op.deps:
                        op.deps.append(d)
                        if not d.is_dma:
                            d.needs_inc = True
            self.barrier_deps[op.eng] = []

    def op(self, eng, fn, reads=(), writes=()):
        o = Op(eng, fn)
        self._track(o, reads, writes)
        self.ops[eng].append(o)
        return o

    def dma(self, eng, out, in_, reads=(), writes=(), **kw):
        def fn(e):
            return e.dma_start(out=out, in_=in_, **kw)

        o = Op(eng, fn)
        o.is_dma = True
        n = self.dma_count[eng]
        self.dma_count[eng] = n + 1
        o.slot = n % DMA_SLOTS
        o.slot_n = n // DMA_SLOTS
        prev = self.slot_last.get((eng, o.slot))
        if prev is not None:
            o.deps.append(prev)
        self.slot_last[(eng, o.slot)] = o
        self._track(o, reads, writes)
        self.ops[eng].append(o)
        return o

    def barrier(self):
        lasts = []
        for e in ENGS:
            for o in reversed(self.ops[e]):
                if not o.is_dma:
                    lasts.append(o)
                    break
        for o in self.slot_last.values():
            lasts.append(o)
        for e in ENGS:
            self.barrier_deps[e] = list(lasts)

    def emit(self, nc):
        engmap = {"pe": "tensor", "act": "scalar", "dve": "vector", "pool": "gpsimd", "sp": "sync"}
        with contextlib.ExitStack() as es:
            sems = {e: es.enter_context(nc.semaphore("s_" + e)) for e in ENGS}
            dsems = {}
            for e in ENGS:
                if self.dma_count[e] > 0:
                    for s in range(DMA_SLOTS):
                        dsems[(e, s)] = es.enter_context(nc.semaphore("d_%s_%d" % (e, s)))
            for e in ENGS:
                cnt = 0
                for o in self.ops[e]:
                    if o.is_dma:
                        o.tok = (dsems[(e, o.slot)], 16 * (o.slot_n + 1))
                    elif o.needs_inc:
                        cnt += 1
                        o.tok = (sems[e], cnt)
            block = es.enter_context(nc.Block())

            def make(e):
                def body(eng):
                    seen = {}
                    for o in self.ops[e]:
                        for d in o.deps:
                            sem, val = d.tok
                            key = id(sem)
                            if seen.get(key, 0) < val:
                                eng.wait_ge(sem, val)
                                seen[key] = val
                        ins = o.fn(eng)
                        if o.is_dma:
                            ins.then_inc(o.tok[0], 16)
                        elif o.needs_inc:
                            ins.then_inc(o.tok[0], 1)

                return body

            for e in ENGS:
                if self.ops[e]:
                    getattr(block, engmap[e])(make(e))


NEG = -30000.0
SCALE = 128 ** -0.5


def make_consts():
    c = {}
    c["identf"] = np.eye(128, dtype=np.float32)
    rm = np.zeros((128, 128), np.float32)
    for m in range(128):
        a, p = (m // 64), (m % 64) // 32
        if p == 0:
            rm[m + 32, m] = -1.0
        else:
            rm[m - 32, m] = 1.0
    c["rotm"] = rm
    qi = np.arange(128)[:, None]
    kj = np.arange(128)[None, :]
    mb = np.zeros((3, 128, 384), np.float32)
    for v in range(3):
        mb[v, :, 0:128] = np.where(kj >= qi, 0.0, NEG)
        mb[v, :, 256:384] = np.where(kj <= qi, 0.0, NEG)
    mb[0, :, 0:128] = NEG
    mb[2, :, 256:384] = NEG
    c["amask"] = mb
    s = np.arange(128)[:, None]
    t = np.arange(128)[None, :]
    same = (s // 64) == (t // 64)
    rmask = np.zeros((2, 128, 128), np.float32)
    rmask[0] = (same & (s <= t)).astype(np.float32)
    rmask[1] = (same & (s >= t)).astype(np.float32)
    c["rmask"] = rmask
    pos = np.arange(2048)
    row = (pos // 64).astype(np.float32)
    col = (pos % 64).astype(np.float32)
    inv = (10000.0 ** (-np.arange(32, dtype=np.float32) / 32)).astype(np.float32)
    ar = row[:, None] * inv[None, :]
    ac = col[:, None] * inv[None, :]
    ang = np.concatenate([ar, ar, ac, ac], axis=-1).astype(np.float32)
    c["cosT"] = np.ascontiguousarray(np.cos(ang).T.astype(np.float32))
    c["sinT"] = np.ascontiguousarray(np.sin(ang).T.astype(np.float32))
    return c


class _Stop(Exception):
    pass


def build(n_layers=4, dbg=False, stop=0):
    nc = bass.Bass("TRN2", target_bir_lowering=False)
    P = Prog()

    _cnt = [0]

    def SBT(name, shape, dt):
        _cnt[0] += 1
        return nc.sbuf_tensor("%s_u%d" % (name, _cnt[0]), shape, dt)

    def PST(name, shape, dt):
        _cnt[0] += 1
        return nc.psum_tensor("%s_u%d" % (name, _cnt[0]), shape, dt)

    halt = [False]
    cur = [-1]

    def CK(n):
        if stop == n and (n <= 2 or cur[0] == n_layers - 1):
            halt[0] = True
        return halt[0]

    def din(name, shape, dt=F32):
        return nc.dram_tensor(name, list(shape), dt, kind="ExternalInput").ap()

    xin = din("xin", [2304, 2048])
    ccl = din("ccl", [128, 32])
    w_ada = din("w_ada", [4, 2048, 12288])
    bada_d = din("bada", [128, 384])
    gmix_d = din("gmix", [128, 64])
    gmlp_d = din("gmlp", [128, 64])
    wqkv = din("att_w_qkv", [2, 2048, 3072])
    wo_att = din("att_w_o", [2, 2048, 2048])
    sink_d = din("att_sink", [1, 32])
    win = din("rec_w_in", [2, 2048, 10240])
    wo_rec = din("rec_w_o", [2, 2048, 2048])
    lbl_d = din("lbl", [128, 64])
    onorm_d = din("rec_onorm_g", [2, 128])
    wup = din("mlp_w_up", [4, 2048, 8192])
    wdn = din("mlp_w_down", [4, 8192, 2048])
    fing_d = din("final_norm_g", [1, 2048])
    identf_d = din("identf", [128, 128])
    rotm_d = din("rotm", [128, 128])
    amask_d = din("amask", [3, 128, 384])
    rmask_d = din("rmask", [2, 128, 128])
    cosT_d = din("cosT", [128, 2048])
    sinT_d = din("sinT", [128, 2048])
    out = nc.dram_tensor("out", [2048, 2048], F32, kind="ExternalOutput").ap()
    res = nc.dram_tensor("res", [2304, 2048], F32).ap()
    yT_d = nc.dram_tensor("yT_d", [16, 128, 2304], BF16).ap()
    hT_d = nc.dram_tensor("hT_d", [16, 128, 2304], BF16).ap()
    if dbg:
        dbg_o = nc.dram_tensor("dbg", [2304, 2048], F32, kind="ExternalOutput").ap()

    r_res = [Res("res%d" % t) for t in range(18)]
    r_yT = [Res("yT%d" % t) for t in range(18)]
    r_hT = [Res("hT%d" % t) for t in range(18)]
    r_out = Res("out")

    r_dump = Res("dump")
    dumped = [False]

    def DUMP(i, ap, ncols, R, bf=False):
        if not dbg:
            return
        dumped[0] = True
        P.dma("pool" if bf else "sp", dbg_o[i * 128:(i + 1) * 128, 0:ncols], ap, reads=R, writes=[r_dump])

    def MM(o, lhsT, rhs, start, stop, R, W):
        P.op("pe", lambda e: e.matmul(o, lhsT=lhsT, rhs=rhs, start=start, stop=stop), R, W)

    def TR(o, i, ident, R, W):
        P.op("pe", lambda e: e.transpose(out=o, in_=i, identity=ident), R, W)

    def ACT(o, i, func, R, W, scale=1.0, bias=0.0, accum=None):
        if accum is None:
            P.op("act", lambda e: e.activation(out=o, in_=i, func=func, bias=bias, scale=scale), R, W)
        else:
            P.op("act", lambda e: e.activation(out=o, in_=i, func=func, bias=bias, scale=scale, accum_out=accum), R, W)

    def TS(eng, o, i, s1, s2, op0, op1, R, W):
        if s2 is None:
            P.op(eng, lambda e: e.tensor_scalar(out=o, in0=i, scalar1=s1, scalar2=None, op0=op0), R, W)
        else:
            P.op(eng, lambda e: e.tensor_scalar(out=o, in0=i, scalar1=s1, scalar2=s2, op0=op0, op1=op1), R, W)

    def TT(eng, o, a, b, op, R, W):
        P.op(eng, lambda e: e.tensor_tensor(out=o, in0=a, in1=b, op=op), R, W)

    def STT(eng, o, a, s, b, op0, op1, R, W):
        P.op(eng, lambda e: e.scalar_tensor_tensor(out=o, in0=a, scalar=s, in1=b, op0=op0, op1=op1), R, W)

    def CP(eng, o, i, R, W):
        if eng == "act":
            P.op("act", lambda e: e.copy(out=o, in_=i), R, W)
        else:
            P.op(eng, lambda e: e.tensor_copy(out=o, in_=i), R, W)

    def MS(eng, o, v, W):
        P.op(eng, lambda e: e.memset(o, v), (), W)

    with contextlib.ExitStack() as gs_:
        def gsb(name, shape, dt):
            return gs_.enter_context(SBT(name, shape, dt))

        identf = gsb("identf", [128, 128], F32)
        identb = gsb("identb", [128, 128], BF16)
        rotm = gsb("rotm", [128, 128], BF16)
        gmix = gsb("gmix", [128, 64], F32)
        gmlp = gsb("gmlp", [128, 64], F32)
        lbl = gsb("lbl", [128, 64], F32)
        lbt = gsb("lbt", [128, 3, 32], F32)
        epsT = gsb("epsT", [128, 1], F32)
        ones1 = gsb("ones1", [128, 1], F32)
        sinkr = gsb("sinkr", [128, 32], F32)
        fmA = gsb("fmA", [128, 4, 96, 2], F32)
        bada = gsb("bada", [128, 384], F32)
        r_fmA = Res("fmA")
        gsm = gsb("gsm", [128, 2, 2, 16], F32)
        r_const = Res("const")
        r_lbt = Res("lbt")
        r_fm = Res("fm")
        r_gate = Res("gate")

        P.dma("sp", identf[:], identf_d, writes=[r_const])
        P.dma("pool", identb[:], identf_d, writes=[r_const])
        P.dma("pool", rotm[:], rotm_d, writes=[r_const])
        P.dma("sp", gmix[:], gmix_d, writes=[r_const])
        P.dma("sp", gmlp[:], gmlp_d, writes=[r_const])
        P.dma("sp", lbl[:], lbl_d, writes=[r_const])
        P.dma("sp", bada[:], bada_d, writes=[r_const])
        P.dma("sp", sinkr[:], sink_d.partition_broadcast(128), writes=[r_const])
        MS("dve", epsT[:], 1e-6, [r_const])
        MS("dve", ones1[:], 1.0, [r_const])
        TS("dve", sinkr[:], sinkr[:], 1.0 / SCALE, None, ALU.mult, None, [r_const], [r_const])
        def main_body():
            if CK(1):
                return

            with contextlib.ExitStack() as ps_:
                def sb(name, shape, dt):
                    return ps_.enter_context(SBT(name, shape, dt))

                sil_f = sb("sil_f", [128, 32], F32)
                sil = sb("sil", [128, 16, 2], BF16)
                wb = [sb("wb%d" % i, [128, 16, 512], BF16) for i in range(3)]
                pm = ps_.enter_context(PST("pm", [128, 512], F32))
                r_sil = Res()
                r_wb = [Res() for _ in range(3)]
                r_pm = PRes()
                P.dma("sp", sil_f[:], ccl, writes=[r_sil])
                ACT(sil[:].rearrange("p c s -> p (c s)"), sil_f[:], AF.Silu, [r_sil], [r_sil])
                i = 0
                for l in range(n_layers):
                    for nb in range(24):
                        k = i % 3
                        i += 1
                        P.dma("pool", wb[k][:], w_ada[l, :, nb * 512:(nb + 1) * 512].rearrange("(c p) n -> p c n", p=128),
                              writes=[r_wb[k]])
                        for jj in range(4):
                            jc = nb * 4 + jj
                            for kc in range(16):
                                MM(pm[:, jc * 2:jc * 2 + 2], wb[k][:, kc, jj * 128:(jj + 1) * 128], sil[:, kc, :], kc == 0, kc == 15,
                                   [r_sil, r_wb[k]], [r_pm])
                    TT("dve", fmA[:, l, :, :], pm[:, 0:192].rearrange("p (j s) -> p j s", s=2),
                       bada[:, l * 96:(l + 1) * 96].rearrange("p (j o) -> p j o", o=1).to_broadcast([128, 96, 2]), ALU.add,
                       [r_pm, r_const], [r_fmA])
            P.barrier()
            if CK(2):
                return

            def stream_of(t):
                return 1 if t < 2 else 0

            def src_tile(l, t, mixer=True):
                s = xin if (l == 0 and mixer) else res
                return s[t * 128:(t + 1) * 128, :]

            class NormCtx:
                def __init__(self, es, tag):
                    def sb(name, shape, dt):
                        return es.enter_context(SBT(tag + name, shape, dt))

                    self.xt = [sb("xt%d" % i, [128, 2048], F32) for i in range(2)]
                    self.xn = [sb("xn%d" % i, [128, 2048], BF16) for i in range(2)]
                    self.ss = [sb("ss%d" % i, [128, 4], F32) for i in range(2)]
                    self.pT = es.enter_context(PST(tag + "pT", [128, 16, 128], BF16))
                    self.r_xt = [Res(), Res()]
                    self.r_xn = [Res(), Res()]
                    self.r_ss = [Res(), Res()]
                    self.r_pT = PRes()
                    self.n = 0

            def norm_tile(N, l, t, which, dst, r_dst):
                mixer = which == 0
                k = N.n % 2
                N.n += 1
                s = stream_of(t)
                xt, xn, ss = N.xt[k], N.xn[k], N.ss[k]
                P.dma("sp", xt[:], src_tile(l, t, mixer), reads=[r_res[t]], writes=[N.r_xt[k]])
                ACT(xn[:], xt[:], AF.Square, [N.r_xt[k]], [N.r_xn[k], N.r_ss[k]], accum=ss[:, 0:1])
                ACT(ss[:, 1:2], ss[:, 0:1], AF.Sqrt, [N.r_ss[k], r_const], [N.r_ss[k]], scale=1.0 / 2048, bias=epsT[:, 0:1])
                P.op("dve", lambda e: e.reciprocal(out=ss[:, 2:3], in_=ss[:, 1:2]), [N.r_ss[k]], [N.r_ss[k]])
                TS("dve", xn[:], xt[:], ss[:, 2:3], None, ALU.mult, None, [N.r_xt[k], N.r_ss[k]], [N.r_xn[k]])
                for kc in range(16):
                    TR(N.pT[:, kc, :], xn[:, kc * 128:(kc + 1) * 128], identb[:], [N.r_xn[k], r_const], [N.r_pT])
                sh = 0 if which == 0 else 48
                for kc in range(16):
                    sc_ap = gsm[:, s, which, kc:kc + 1]
                    bi_ap = fmA[:, l, sh + kc, s:s + 1]
                    if kc % 2 == 0:
                        ACT(dst[:, kc, :], N.pT[:, kc, :], AF.Identity, [N.r_pT, r_fm, r_fmA], [r_dst], scale=sc_ap, bias=bi_ap)
                    else:
                        TS("dve", dst[:, kc, :], N.pT[:, kc, :], sc_ap, bi_ap, ALU.mult, ALU.add, [N.r_pT, r_fm, r_fmA], [r_dst])

            def load_mods(l):
                for s in range(2):
                    STT("dve", gsm[:, s, 0, :], fmA[:, l, 16:32, s], 1.0, gmix[:, l * 16:(l + 1) * 16], ALU.add, ALU.mult,
                        [r_fmA, r_const], [r_fm])
                    STT("dve", gsm[:, s, 1, :], fmA[:, l, 64:80, s], 1.0, gmlp[:, l * 16:(l + 1) * 16], ALU.add, ALU.mult,
                        [r_fmA, r_const], [r_fm])

            def load_gate(es, l, m):
                gate_bc = [es.enter_context(SBT("gate%d" % s, [128, 2048], F32)) for s in range(2)]
                with contextlib.ExitStack() as eg:
                    rep = [eg.enter_context(SBT("rep%d" % i, [128, 128], F32)) for i in range(2)]
                    pg = eg.enter_context(PST("pg", [128, 512], F32))
                    r_rep = [Res(), Res()]
                    r_pg = PRes()
                    n = 0
                    for s in range(2):
                        for c4 in range(4):
                            for ci in range(4):
                                c = c4 * 4 + ci
                                k = n % 2
                                n += 1
                                CP("dve", rep[k][:], fmA[:, l, m * 16 + c, s:s + 1].to_broadcast([128, 128]), [r_fmA], [r_rep[k]])
                                MM(pg[:, ci * 128:(ci + 1) * 128], rep[k][:], identf[:], True, True, [r_rep[k], r_const], [r_pg])
                            CP("act", gate_bc[s][:, c4 * 512:(c4 + 1) * 512], pg[:], [r_pg], [r_gate])
                P.barrier()
                return gate_bc

            def phase_wo(l, wo_src, tiles):
                with contextlib.ExitStack() as es:
                    gate_bc = load_gate(es, l, 2)
                    if CK(60):
                        return
                    def sb(name, shape, dt):
                        return es.enter_context(SBT(name, shape, dt))

                    wo = sb("wo", [128, 16, 2048], BF16)
                    yt = [sb("yt%d" % i, [128, 16, 128], BF16) for i in range(2)]
                    xt = [sb("wxt%d" % i, [128, 2048], F32) for i in range(2)]
                    tmp = [sb("wtmp%d" % i, [128, 512], F32) for i in range(2)]
                    acc = [es.enter_context(PST("woacc%d" % i, [128, 512], F32)) for i in range(4)]
                    r_wo = [Res() for _ in range(4)]
                    r_yt, r_xt, r_tmp = [Res(), Res()], [Res(), Res()], [Res(), Res()]
                    r_acc = [PRes() for _ in range(4)]
                    for nb in range(4):
                        P.dma("pool", wo[:, :, nb * 512:(nb + 1) * 512],
                              wo_src[:, nb * 512:(nb + 1) * 512].rearrange("(c p) n -> p c n", p=128), writes=[r_wo[nb]])
                    if CK(61):
                        return
                    n = 0
                    for it, t in enumerate(tiles):
                        if it == 1 and CK(62):
                            return
                        if it == 2 and CK(63):
                            return
                        if it == 3 and CK(64):
                            return
                        if it == 8 and CK(65):
                            return
                        if it == 13 and CK(66):
                            return
                        if it == 17 and CK(67):
                            return
                        k = it % 2
                        s = stream_of(t)
                        P.dma("sp", yt[k][:], yT_d[:, :, t * 128:(t + 1) * 128].rearrange("c p t -> p c t"),
                              reads=[r_yT[t]], writes=[r_yt[k]])
                        P.dma("sp", xt[k][:], src_tile(l, t), reads=[r_res[t]], writes=[r_xt[k]])
                        for nb in range(4):
                            a = acc[n % 4]
                            ra = r_acc[n % 4]
                            for kc in range(16):
                                MM(a[:], yt[k][:, kc, :], wo[:, kc, nb * 512:(nb + 1) * 512], kc == 0, kc == 15,
                                   [r_yt[k], r_wo[nb]], [ra])
                            tm = tmp[n % 2]
                            TT("dve", tm[:], a[:], gate_bc[s][:, nb * 512:(nb + 1) * 512], ALU.mult, [ra, r_gate], [r_tmp[n % 2]])
                            TT("pool", xt[k][:, nb * 512:(nb + 1) * 512], tm[:], xt[k][:, nb * 512:(nb + 1) * 512], ALU.add,
                               [r_tmp[n % 2], r_xt[k]], [r_xt[k]])
                            n += 1
                        P.dma("sp", res[t * 128:(t + 1) * 128, :], xt[k][:], reads=[r_xt[k]], writes=[r_res[t]])
                    if CK(68) or CK(69) or CK(70):
                        return
                P.barrier()

            def phase_mlp(l, groups, last):
                with contextlib.ExitStack() as es:
                    gate_bc = load_gate(es, l, 5)
                    def sb(name, shape, dt):
                        return es.enter_context(SBT(name, shape, dt))

                    aT = sb("aT", [128, 64, 768], BF16)
                    r_aT = [Res() for _ in range(64)]
                    wu_i = 0
                    for g, tiles in enumerate(groups):
                        T = len(tiles) * 128
                        halves = [(0, T // 2), (T // 2, T)]
                        with contextlib.ExitStack() as es2:
                            def sb2(name, shape, dt):
                                return es2.enter_context(SBT(name, shape, dt))

                            hT = sb2("hT", [128, 16, 768], BF16)
                            r_h = Res()
                            wu = [sb2("wu%d" % i, [128, 16, 256], BF16) for i in range(2)]
                            r_wu = [Res(), Res()]
                            rl = [sb2("rl%d" % i, [128, 384], BF16) for i in range(2)]
                            r_rl = [Res(), Res()]
                            N = NormCtx(es2, "m")
                            up = [es2.enter_context(PST("up%d" % i, [128, 512], F32)) for i in range(4)]
                            r_up = [PRes() for _ in range(4)]
                            for it, t in enumerate(tiles):
                                norm_tile(N, l, t, 1, hT[:, :, it * 128:(it + 1) * 128], r_h)
                            nu = 0
                            for fb in range(32):
                                k = fb % 2
                                P.dma("pool", wu[k][:], wup[l, :, fb * 256:(fb + 1) * 256].rearrange("(c p) n -> p c n", p=128),
                                      writes=[r_wu[k]])
                                for fi in range(2):
                                    fc = fb * 2 + fi
                                    pa = [up[(nu * 2) % 4], up[(nu * 2 + 1) % 4]]
                                    rp = [r_up[(nu * 2) % 4], r_up[(nu * 2 + 1) % 4]]
                                    for kc in range(16):
                                        for hi, (c0, c1) in enumerate(halves):
                                            MM(pa[hi][:, 0:c1 - c0], wu[k][:, kc, fi * 128:(fi + 1) * 128], hT[:, kc, c0:c1],
                                               kc == 0, kc == 15, [r_wu[k], r_h], [rp[hi]])
                                    for hi, (c0, c1) in enumerate(halves):
                                        r_ = rl[(nu * 2 + hi) % 2]
                                        rr = r_rl[(nu * 2 + hi) % 2]
                                        ACT(r_[:, 0:c1 - c0], pa[hi][:, 0:c1 - c0], AF.Relu, [rp[hi]], [rr])
                                        TT("dve", aT[:, fc, c0:c1], r_[:, 0:c1 - c0], r_[:, 0:c1 - c0], ALU.mult, [rr], [r_aT[fc]])
                                    nu += 1
                        P.barrier()
                        with contextlib.ExitStack() as es2:
                            def sb2(name, shape, dt):
                                return es2.enter_context(SBT(name, shape, dt))

                            wd = [sb2("wd%d" % i, [128, 8, 512], BF16) for i in range(3)]
                            r_wd = [Res() for _ in range(3)]
                            xs = [sb2("xs%d" % i, [128, 512], F32) for i in range(6)]
                            r_xs = [Res() for _ in range(6)]
                            tm = [sb2("dtm%d" % i, [128, 512], F32) for i in range(2)]
                            r_tm = [Res(), Res()]
                            acc = [es2.enter_context(PST("dacc%d" % i, [128, 512], F32)) for i in range(6)]
                            r_acc = [PRes() for _ in range(6)]
                            wi = 0
                            ne = 0
                            for nb in range(4):
                                for it, t in enumerate(tiles):
                                    P.dma("sp", xs[it][:], src_tile(l, t, False)[:, nb * 512:(nb + 1) * 512], reads=[r_res[t]],
                                          writes=[r_xs[it]])
                                for fb in range(8):
                                    k = wi % 3
                                    wi += 1
                                    P.dma("pool", wd[k][:],
                                          wdn[l, fb * 1024:(fb + 1) * 1024, nb * 512:(nb + 1) * 512].rearrange("(c p) n -> p c n", p=128),
                                          writes=[r_wd[k]])
                                    for fi in range(8):
                                        fc = fb * 8 + fi
                                        for it in range(len(tiles)):
                                            MM(acc[it][:], aT[:, fc, it * 128:(it + 1) * 128], wd[k][:, fi, :], fc == 0, fc == 63,
                                               [r_aT[fc], r_wd[k]], [r_acc[it]])
                                for it, t in enumerate(tiles):
                                    s = stream_of(t)
                                    tt_ = tm[ne % 2]
                                    rt = r_tm[ne % 2]
                                    ne += 1
                                    TT("dve", tt_[:], acc[it][:], gate_bc[s][:, nb * 512:(nb + 1) * 512], ALU.mult,
                                       [r_acc[it], r_gate], [rt])
                                    TT("pool", xs[it][:], tt_[:], xs[it][:], ALU.add, [rt, r_xs[it]], [r_xs[it]])
                                    P.dma("sp", res[t * 128:(t + 1) * 128, nb * 512:(nb + 1) * 512], xs[it][:],
                                          reads=[r_xs[it]], writes=[r_res[t]])
                        P.barrier()

            def phase_att(l, j, need_ctx):
                with contextlib.ExitStack() as es:
                    def sb(name, shape, dt):
                        return es.enter_context(SBT(name, shape, dt))

                    qT = sb("qT", [128, 16, 2304], BF16)
                    kT = sb("kT", [128, 4, 2560], BF16)
                    V = sb("V", [128, 20, 512], BF16)
                    r_q = [Res() for _ in range(18)]
                    r_k = [Res() for _ in range(20)]
                    r_v = [Res() for _ in range(20)]
                    MS("pool", kT[:, :, 256:384], 0.0, [r_k[2]])
                    MS("pool", kT[:, :, 2432:2560], 0.0, [r_k[19]])
                    MS("pool", V[:, 2, :], 0.0, [r_v[2]])
                    MS("pool", V[:, 19, :], 0.0, [r_v[19]])

                    def kslot(t):
                        return t if t < 2 else t + 1

                    with contextlib.ExitStack() as es2:
                        def sb2(name, shape, dt):
                            return es2.enter_context(SBT(name, shape, dt))

                        hT = sb2("ahT", [128, 16, 768], BF16)
                        wq = [sb2("wq%d" % i, [128, 16, 256], BF16) for i in range(2)]
                        r_wq = [Res(), Res()]
                        cosT = sb2("cosT", [128, 768], F32)
                        sinT = sb2("sinT", [128, 768], F32)
                        r_tab = Res()
                        qb = [sb2("qb%d" % i, [128, 384], BF16) for i in range(2)]
                        t1 = [sb2("t1%d" % i, [128, 384], F32) for i in range(2)]
                        t2 = [sb2("t2%d" % i, [128, 384], F32) for i in range(2)]
                        r_qb, r_t1, r_t2 = [Res(), Res()], [Res(), Res()], [Res(), Res()]
                        N = NormCtx(es2, "a")
                        pq = [es2.enter_context(PST("pq%d" % i, [128, 512], F32)) for i in range(2)]
                        pr = [es2.enter_context(PST("pr%d" % i, [128, 512], F32)) for i in range(2)]
                        pv = [es2.enter_context(PST("pv%d" % i, [128, 512], F32)) for i in range(2)]
                        r_pq, r_pr, r_pv = [PRes(), PRes()], [PRes(), PRes()], [PRes(), PRes()]
                        wi = 0
                        nq = 0
                        nv = 0
                        for g in range(3):
                            tiles = list(range(g * 6, g * 6 + 6))
                            r_h = Res()
                            if CK(40):
                                return
                            for it, t in enumerate(tiles):
                                norm_tile(N, l, t, 0, hT[:, :, it * 128:(it + 1) * 128], r_h)
                                if CK(41):
                                    return
                            if CK(42):
                                return
                            lat0 = g * 768 - 256
                            p0 = max(lat0, 0)
                            ncols = 768 - (p0 - lat0)
                            P.dma("sp", cosT[:, p0 - lat0:768], cosT_d[:, p0:p0 + ncols], writes=[r_tab])
                            P.dma("sp", sinT[:, p0 - lat0:768], sinT_d[:, p0:p0 + ncols], writes=[r_tab])
                            for blk in range(12):
                                if blk == 1 and CK(43):
                                    return
                                if blk == 9 and CK(44):
                                    return
                                if blk == 11 and CK(45):
                                    return
                                k = wi % 2
                                wi += 1
                                P.dma("pool", wq[k][:], wqkv[j, :, blk * 256:(blk + 1) * 256].rearrange("(c p) n -> p c n", p=128),
                                      writes=[r_wq[k]])
                                if blk < 10:
                                    for hh in range(2):
                                        for half in range(2):
                                            c0 = half * 384
                                            a = nq % 2
                                            nq += 1
                                            for kc in range(16):
                                                MM(pq[a][:, 0:384], wq[k][:, kc, hh * 128:(hh + 1) * 128], hT[:, kc, c0:c0 + 384],
                                                   kc == 0, kc == 15, [r_wq[k], r_h], [r_pq[a]])
                                            if CK(46):
                                                return
                                            if blk < 8:
                                                head = blk * 2 + hh

                                                def dst(ca, cb, head=head, g=g):
                                                    return qT[:, head, g * 768 + ca:g * 768 + cb]
                                                wres = [r_q[t] for t in tiles[half * 3:half * 3 + 3]]
                                            else:
                                                kvh = (blk - 8) * 2 + hh

                                                def dst(ca, cb, kvh=kvh, g=g):
                                                    sa = g * 768 + ca
                                                    off = 0 if sa < 256 else 128
                                                    return kT[:, kvh, sa + off:sa + off + (cb - ca)]
                                                wres = [r_k[kslot(t)] for t in tiles[half * 3:half * 3 + 3]]
                                            if g == 0 and half == 0:
                                                segs = [(0, 256, False), (256, 384, True)]
                                            else:
                                                segs = [(c0, c0 + 384, True)]
                                            for (ca, cb, rope) in segs:
                                                la, lb_ = ca - c0, cb - c0
                                                if not rope:
                                                    CP("act", dst(ca, cb), pq[a][:, la:lb_], [r_pq[a]], wres)
                                                    if CK(47):
                                                        return
                                                else:
                                                    CP("act", qb[a][:, la:lb_], pq[a][:, la:lb_], [r_pq[a]], [r_qb[a]])
                                                    if CK(49):
                                                        return
                                                    MM(pr[a][:, la:lb_], rotm[:], qb[a][:, la:lb_], True, True, [r_qb[a], r_const],
                                                       [r_pr[a]])
                                                    if CK(50):
                                                        return
                                                    TT("dve", t1[a][:, la:lb_], pq[a][:, la:lb_], cosT[:, ca:cb], ALU.mult,
                                                       [r_pq[a], r_tab], [r_t1[a]])
                                                    if CK(51):
                                                        return
                                                    TT("dve", t2[a][:, la:lb_], pr[a][:, la:lb_], sinT[:, ca:cb], ALU.mult,
                                                       [r_pr[a], r_tab], [r_t2[a]])
                                                    if CK(48):
                                                        return
                                                    TT("pool", dst(ca, cb), t1[a][:, la:lb_], t2[a][:, la:lb_], ALU.add,
                                                       [r_t1[a], r_t2[a]], wres)
                                else:
                                    vh = blk - 10
                                    for it, t in enumerate(tiles):
                                        a = nv % 2
                                        nv += 1
                                        for kc in range(16):
                                            MM(pv[a][:, 0:256], hT[:, kc, it * 128:(it + 1) * 128], wq[k][:, kc, :], kc == 0, kc == 15,
                                               [r_wq[k], r_h], [r_pv[a]])
                                        CP("act", V[:, kslot(t), vh * 256:(vh + 1) * 256], pv[a][:, 0:256], [r_pv[a]], [r_v[kslot(t)]])
                    P.barrier()
                    if CK(4):
                        return
                    with contextlib.ExitStack() as es2:
                        def sb2(name, shape, dt):
                            return es2.enter_context(SBT(name, shape, dt))

                        amask = sb2("amask", [128, 3, 384], BF16)
                        r_am = Res()
                        P.dma("pool", amask[:], amask_d.rearrange("v q k -> q v k"), writes=[r_am])
                        pb = [sb2("pb%d" % i, [128, 640], BF16) for i in range(2)]
                        pTs = [sb2("pTs%d" % i, [128, 5, 128], BF16) for i in range(2)]
                        st = [sb2("ast%d" % i, [128, 8], F32) for i in range(2)]
                        oTs = [sb2("oTs%d" % i, [128, 16, 128], BF16) for i in range(2)]
                        r_pb, r_pTs, r_st, r_oTs = [Res(), Res()], [Res(), Res()], [Res(), Res()], [Res(), Res()]
                        S = [es2.enter_context(PST("S%d" % i, [128, 1024], F32)) for i in range(2)]
                        pTp = [es2.enter_context(PST("pTp%d" % i, [128, 8, 128], BF16)) for i in range(2)]
                        oTp = [es2.enter_context(PST("oTp%d" % i, [128, 4, 128], F32)) for i in range(2)]
                        r_S, r_pTp, r_oTp = [PRes(), PRes()], [PRes(), PRes()], [PRes(), PRes()]
                        u = 0
                        qtiles = list(range(18)) if need_ctx else list(range(2, 18))
                        for iq, t in enumerate(qtiles):
                            ob = iq % 2
                            for h in range(16):
                                kv = h // 4
                                a = u % 2
                                u += 1
                                qap = qT[:, h, t * 128:(t + 1) * 128]
                                if t < 2:
                                    lo, hi = 512, 768
                                    MM(S[a][:, 512:768], qap, kT[:, kv, 0:256], True, True, [r_q[t], r_k[0], r_k[1]], [r_S[a]])
                                    kslots = [0, 1]
                                else:
                                    b = t - 2
                                    lo, hi = 128, 768
                                    kc0 = 256 + b * 128
                                    var = 0 if b == 0 else (2 if b == 15 else 1)
                                    MM(S[a][:, 128:512], qap, kT[:, kv, kc0:kc0 + 384], True, False,
                                       [r_q[t], r_k[2 + b], r_k[3 + b], r_k[4 + b]], [r_S[a]])
                                    MM(S[a][:, 128:512], identb[:], amask[:, var, :], False, True, [r_am, r_const], [r_S[a]])
                                    MM(S[a][:, 512:768], qap, kT[:, kv, 0:256], True, True, [r_q[t], r_k[0], r_k[1]], [r_S[a]])
                                    kslots = [2 + b, 3 + b, 4 + b, 0, 1]
                                n = hi - lo
                                nk = n // 128
                                sa = st[a]
                                P.op("dve", lambda e, sa=sa, Sa=S[a], lo=lo, hi=hi: e.reduce_max(out=sa[:, 0:1], in_=Sa[:, lo:hi], axis=AX.X),
                                     [r_S[a]], [r_st[a]])
                                TS("dve", sa[:, 1:2], sa[:, 0:1], sinkr[:, j * 16 + h:j * 16 + h + 1], -SCALE, ALU.max, ALU.mult,
                                   [r_st[a], r_const], [r_st[a]])
                                ACT(pb[a][:, 0:n], S[a][:, lo:hi], AF.Exp, [r_S[a], r_st[a]], [r_pb[a], r_st[a]], scale=SCALE,
                                    bias=sa[:, 1:2], accum=sa[:, 2:3])
                                ACT(sa[:, 3:4], sinkr[:, j * 16 + h:j * 16 + h + 1], AF.Exp, [r_st[a], r_const], [r_st[a]], scale=SCALE,
                                    bias=sa[:, 1:2])
                                TT("dve", sa[:, 4:5], sa[:, 2:3], sa[:, 3:4], ALU.add, [r_st[a]], [r_st[a]])
                                P.op("dve", lambda e, sa=sa: e.reciprocal(out=sa[:, 5:6], in_=sa[:, 4:5]), [r_st[a]], [r_st[a]])
                                TS("dve", pb[a][:, 0:n], pb[a][:, 0:n], sa[:, 5:6], None, ALU.mult, None, [r_pb[a], r_st[a]], [r_pb[a]])
                                for jk in range(nk):
                                    TR(pTp[a][:, jk, :], pb[a][:, jk * 128:(jk + 1) * 128], identb[:], [r_pb[a], r_const], [r_pTp[a]])
                                CP("act", pTs[a][:, 0:nk, :], pTp[a][:, 0:nk, :], [r_pTp[a]], [r_pTs[a]])
                                oa = (u - 1) // 4 % 2
                                hh = h % 4
                                for jk in range(nk):
                                    MM(oTp[oa][:, hh, :], V[:, kslots[jk], kv * 128:(kv + 1) * 128], pTs[a][:, jk, :], jk == 0,
                                       jk == nk - 1, [r_pTs[a], r_v[kslots[jk]]], [r_oTp[oa]])
                                if hh == 3:
                                    CP("dve", oTs[ob][:, h - 3:h + 1, :], oTp[oa][:], [r_oTp[oa]], [r_oTs[ob]])
                            P.dma("sp", yT_d[:, :, t * 128:(t + 1) * 128].rearrange("c p t -> p c t"), oTs[ob][:], reads=[r_oTs[ob]],
                                  writes=[r_yT[t]])
                P.barrier()

            def phase_rec(l, j, need_ctx):
                if j == 0:
                    MS("dve", lbt[:, 0, :], 0.0, [r_lbt])
                else:
                    TT("dve", lbt[:, 0, :], lbl[:, 32:64], lbl[:, 0:32], ALU.subtract, [r_const], [r_lbt])
                    ACT(lbt[:, 0, :], lbt[:, 0, :], AF.Sigmoid, [r_lbt], [r_lbt])
                TS("dve", lbt[:, 1, :], lbt[:, 0, :], -1.0, 1.0, ALU.mult, ALU.add, [r_lbt], [r_lbt])
                TS("dve", lbt[:, 2, :], lbt[:, 1, :], -1.0, None, ALU.mult, None, [r_lbt], [r_lbt])
                with contextlib.ExitStack() as es:
                    N = NormCtx(es, "r")
                    hs = [es.enter_context(SBT("hs%d" % i, [128, 16, 128], BF16)) for i in range(2)]
                    r_hs = [Res(), Res()]
                    for t in range(18):
                        k = t % 2
                        norm_tile(N, l, t, 0, hs[k][:], r_hs[k])
                        P.dma("sp", hT_d[:, :, t * 128:(t + 1) * 128].rearrange("c p t -> p c t"), hs[k][:], reads=[r_hs[k]],
                              writes=[r_hT[t]])
                P.barrier()
                with contextlib.ExitStack() as es:
                    def sb(name, shape, dt):
                        return es.enter_context(SBT(name, shape, dt))

                    def ps(name, shape, dt):
                        return es.enter_context(PST(name, shape, dt))

                    wh = [sb("wh%d" % i, [128, 16, 5, 128], BF16) for i in range(2)]
                    r_wh = [Res(), Res()]
                    hp = [sb("hp%d" % i, [128, 16, 384], BF16) for i in range(2)]
                    r_hp = [Res(), Res()]
                    rmask = sb("rmask", [128, 2, 128], BF16)
                    onb = sb("onb", [128, 128], F32)
                    r_rc = Res()
                    P.dma("pool", rmask[:], rmask_d.rearrange("v s t -> s v t"), writes=[r_rc])
                    P.dma("sp", onb[:], onorm_d[j:j + 1, :].partition_broadcast(128), writes=[r_rc])
                    qsp = [sb("qsp%d" % d, [128, 36, 128], BF16) for d in range(2)]
                    ksp = [sb("ksp%d" % d, [128, 36, 128], BF16) for d in range(2)]
                    r_sp = Res()
                    for d in range(2):
                        MS("pool", qsp[d][:], 0.0, [r_sp])
                        MS("pool", ksp[d][:], 0.0, [r_sp])
                    Dd = [sb("Dd%d" % d, [128, 36], F32) for d in range(2)]
                    Hm = [sb("Hm%d" % d, [128, 36], F32) for d in range(2)]
                    Gm = [sb("Gm%d" % d, [128, 36], F32) for d in range(2)]
                    utmp = [[sb("utmp%d_%d" % (d, i), [128, 128], F32) for i in range(2)] for d in range(2)]
                    r_ut = [[Res(), Res()], [Res(), Res()]]
                    atmp = [sb("atmp%d" % i, [128, 4, 128], BF16) for i in range(2)]
                    r_at = [Res(), Res()]
                    Sbf = [sb("Sbf%d" % d, [128, 36, 128], BF16) for d in range(2)]
                    ATm = [sb("ATm%d" % d, [128, 18, 128], BF16) for d in range(2)]
                    vtok = sb("vtok", [128, 18, 128], BF16)
                    gg = sb("gg", [128, 18, 128], F32)
                    yTh = sb("yTh", [128, 2304], BF16)
                    Tst = [[sb("Tst%d_%d" % (d, i), [128, 128], F32) for i in range(2)] for d in range(2)]
                    ktok = [sb("ktok%d" % i, [128, 4, 128], BF16) for i in range(2)]
                    r_ktok = [Res(), Res()]
                    r_D, r_Sbf, r_AT = [Res(), Res()], [Res(), Res()], [Res(), Res()]
                    r_T = [[Res(), Res()], [Res(), Res()]]
                    r_vt, r_gg, r_yTh = Res(), Res(), Res()
                    qs = sb("qs", [128, 384], F32)
                    tF = [sb("tF%d" % d, [128, 384], F32) for d in range(2)]
                    tK = [sb("tK%d" % d, [128, 384], F32) for d in range(2)]
                    Bz = [sb("Bz%d" % d, [128, 385], F32) for d in range(2)]
                    tE = [sb("tE%d" % d, [128, 384], F32) for d in range(2)]
                    tX = [sb("tX%d" % d, [128, 384], F32) for d in range(2)]
                    dD = [sb("dD%d" % d, [128, 18], F32) for d in range(2)]
                    r_qs = Res()
                    r_tF, r_tK, r_Bz, r_tE, r_tX, r_dD = ([Res(), Res()] for _ in range(6))
                    for d in range(2):
                        MS("dve", Bz[d][:, 0:1], 0.0, [r_Bz[d]])
                    ost = sb("ost", [128, 4, 4], F32)
                    ojunk = sb("ojunk", [128, 128], BF16)
                    yb = [sb("yb%d" % i, [128, 128], BF16) for i in range(4)]
                    r_ost, r_oj = Res(), Res()
                    r_yb = [Res() for _ in range(4)]
                    zq_ = ps("zq", [128, 512], F32)
                    zq = zq_[:, 0:384]
                    zf_ = [ps("zf%d" % d, [128, 512], F32) for d in range(2)]
                    zf = [z[:, 0:384] for z in zf_]
                    vg = ps("vg", [128, 2, 256], F32)
                    pA = ps("pA", [128, 4, 128], F32)
                    pTr_ = ps("pTr", [128, 2, 4, 128], BF16)
                    pTr = [pTr_[:, i] for i in range(2)]
                    pU = [ps("pU%d" % i, [128, 4, 128], F32) for i in range(2)]
                    r_zq, r_pA = PRes(), PRes()
                    _rp = PRes()
                    r_pTr = [_rp, _rp]
                    r_zf = [PRes(), PRes()]
                    _rv = PRes()
                    r_vg = [_rv, _rv]
                    r_pU = [PRes(), PRes()]

                    nwh = 0
                    nhp = 0
                    nvg = 0
                    for h in range(16):
                        k = nwh % 2
                        nwh += 1
                        for si, part in enumerate((0, 2, 3, 1, 4)):
                            c0 = part * 2048 + h * 128
                            P.dma("pool", wh[k][:, :, si, :], win[j, :, c0:c0 + 128].rearrange("(c p) n -> p c n", p=128),
                                  writes=[r_wh[k]])
                        for pc in range(6):
                            kh = nhp % 2
                            nhp += 1
                            tiles = [pc * 3 + i for i in range(3)]
                            P.dma("sp", hp[kh][:], hT_d[:, :, pc * 384:(pc + 1) * 384].rearrange("c p t -> p c t"),
                                  reads=[r_hT[t] for t in tiles], writes=[r_hp[kh]])
                            for kc in range(16):
                                MM(zq, wh[k][:, kc, 0, :], hp[kh][:, kc, :], kc == 0, kc == 15, [r_wh[k], r_hp[kh]], [r_zq])
                            for d in range(2):
                                for kc in range(16):
                                    MM(zf[d], wh[k][:, kc, 1 + d, :], hp[kh][:, kc, :], kc == 0, kc == 15, [r_wh[k], r_hp[kh]],
                                       [r_zf[d]])
                            for i, t in enumerate(tiles):
                                a = nvg % 2
                                nvg += 1
                                for kc in range(16):
                                    MM(vg[:, a, :], hp[kh][:, kc, i * 128:(i + 1) * 128], wh[k][:, kc, 3:5, :], kc == 0, kc == 15,
                                       [r_wh[k], r_hp[kh]], [r_vg[a]])
                                CP("act", vtok[:, t, :], vg[:, a, 0:128], [r_vg[a]], [r_vt])
                                ACT(gg[:, t, :], vg[:, a, 128:256], AF.Silu, [r_vg[a]], [r_gg])
                                TT("pool", gg[:, t, :], gg[:, t, :], onb[:], ALU.mult, [r_gg, r_rc], [r_gg])
                            ACT(qs[:], zq, AF.Silu, [r_zq], [r_qs])
                            ch0 = pc * 6
                            for d in range(2):
                                col = d * 16 + h
                                ACT(tF[d][:], zf[d], AF.Sigmoid, [r_zf[d]], [r_tF[d]])
                                TS("dve", tK[d][:], tF[d][:], lbt[:, 2, col:col + 1], lbt[:, 1, col:col + 1], ALU.mult, ALU.add,
                                   [r_tF[d], r_lbt], [r_tK[d]])
                                TS("dve", tF[d][:], tF[d][:], lbt[:, 1, col:col + 1], lbt[:, 0, col:col + 1], ALU.mult, ALU.add,
                                   [r_tF[d], r_lbt], [r_tF[d]])
                                ACT(tF[d][:], tF[d][:], AF.Ln, [r_tF[d]], [r_tF[d]])
                                P.op("dve", lambda e, d=d: e.tensor_tensor_scan(out=Bz[d][:, 1:385], data0=ones1[:].to_broadcast([128, 384]),
                                                                               data1=tF[d][:], initial=0.0, op0=ALU.mult, op1=ALU.add),
                                     [r_tF[d], r_const], [r_Bz[d]])
                                bzc = Bz[d][:, 0:384].rearrange("p (c j) -> p c j", j=64)
                                bze = Bz[d][:, 1:385].rearrange("p (c j) -> p c j", j=64)
                                in0 = bze if d == 0 else bzc
                                in1 = bzc[:, :, 32:33].to_broadcast([128, 6, 64])
                                TT("dve", tE[d][:].rearrange("p (c j) -> p c j", j=64), in0, in1, ALU.subtract, [r_Bz[d]], [r_tE[d]])
                                TT("dve", dD[d][:, 0:6].rearrange("p (c o) -> p c o", o=1), bzc[:, :, 32:33], bzc[:, :, 0:1],
                                   ALU.subtract, [r_Bz[d]], [r_dD[d]])
                                TT("dve", dD[d][:, 6:12].rearrange("p (c o) -> p c o", o=1), bze[:, :, 63:64], bzc[:, :, 32:33],
                                   ALU.subtract, [r_Bz[d]], [r_dD[d]])
                                TT("dve", dD[d][:, 12:18].rearrange("p (c o) -> p c o", o=1), bze[:, :, 63:64], bzc[:, :, 0:1],
                                   ALU.subtract, [r_Bz[d]], [r_dD[d]])
                                ha, ga = (0, 6) if d == 0 else (6, 0)
                                ACT(Hm[d][:, ch0:ch0 + 6], dD[d][:, ha:ha + 6], AF.Exp, [r_dD[d]], [r_D[d]])
                                ACT(Gm[d][:, ch0:ch0 + 6], dD[d][:, ga:ga + 6], AF.Exp, [r_dD[d]], [r_D[d]])
                                ACT(Dd[d][:, ch0:ch0 + 6], dD[d][:, 12:18], AF.Exp, [r_dD[d]], [r_D[d]])
                                ACT(tX[d][:], tE[d][:], AF.Exp, [r_tE[d]], [r_tX[d]])
                                ACT(tE[d][:], tE[d][:], AF.Exp, [r_tE[d]], [r_tE[d]], scale=-1.0)
                                qfac, kfac = (tX[d], tE[d]) if d == 0 else (tE[d], tX[d])
                                for par in range(2):
                                    o_q = qsp[d][:, ch0 + par:ch0 + 6:2, par * 64:par * 64 + 64]
                                    o_k = ksp[d][:, ch0 + par:ch0 + 6:2, par * 64:par * 64 + 64]
                                    v = lambda tl: tl[:].rearrange("p (c j) -> p c j", j=64)[:, par:6:2, :]
                                    TT("pool", o_q, v(qs), v(qfac), ALU.mult, [r_qs, r_tX[d], r_tE[d]], [r_sp])
                                    TT("pool" if par else "dve", o_k, v(tK[d]), v(kfac), ALU.mult, [r_tK[d], r_tX[d], r_tE[d]], [r_sp])
                        if stop == 80 and h == 0:
                            DUMP(0, qs[:], 384, [r_qs])
                            for d in range(2):
                                DUMP(1 + d * 5, tF[d][:], 384, [r_tF[d]])
                                DUMP(2 + d * 5, tK[d][:], 384, [r_tK[d]])
                                DUMP(3 + d * 5, Bz[d][:], 385, [r_Bz[d]])
                                DUMP(4 + d * 5, tE[d][:], 384, [r_tE[d]])
                                DUMP(5 + d * 5, tX[d][:], 384, [r_tX[d]])
                            DUMP(11, Dd[0][:], 36, [r_D[0]])
                            DUMP(12, Dd[1][:], 36, [r_D[1]])
                            DUMP(13, gg[:].rearrange("p t v -> p (t v)")[:, 0:2048], 2048, [r_gg])
                            DUMP(14, qsp[0][:].rearrange("p c j -> p (c j)")[:, 0:2048], 2048, [r_sp], bf=True)
                            DUMP(15, ksp[0][:].rearrange("p c j -> p (c j)")[:, 0:2048], 2048, [r_sp], bf=True)
                            DUMP(16, vtok[:].rearrange("p t v -> p (t v)")[:, 0:2048], 2048, [r_vt], bf=True)
                        ctiles = list(range(18)) if need_ctx else list(range(2, 18))
                        seqs = [list(range(36)), [3, 2, 1, 0] + list(range(35, 3, -1))]
                        for d in range(2):
                            MS("pool", Sbf[d][:, seqs[d][0], :], 0.0, [r_Sbf[d]])
                        for g4 in range(9):
                            for d in range(2):
                                cs = seqs[d][g4 * 4:(g4 + 1) * 4]
                                for i, c in enumerate(cs):
                                    TR(pTr[d][:, i, :], ksp[d][:, c, :], identb[:], [r_sp, r_const], [r_pTr[d]])
                                CP("act", ktok[d][:], pTr[d], [r_pTr[d]], [r_ktok[d]])
                                for i, c in enumerate(cs):
                                    MM(pU[d][:, i, :], ktok[d][:, i, :], vtok[:, c // 2, :], True, True, [r_ktok[d], r_vt],
                                       [r_pU[d]])
                            for i in range(4):
                                n = g4 * 4 + i
                                for d in range(2):
                                    c = seqs[d][n]
                                    Sc, Sn = Tst[d][n % 2], Tst[d][(n + 1) % 2]
                                    rc_, rn = r_T[d][n % 2], r_T[d][(n + 1) % 2]
                                    if n == 0:
                                        TS("dve", Sn[:], pU[d][:, i, :], Gm[d][:, c:c + 1], None, ALU.mult, None, [r_pU[d], r_D[d]], [rn])
                                    else:
                                        ut, rut = utmp[d][n % 2], r_ut[d][n % 2]
                                        TS("dve", ut[:], pU[d][:, i, :], Gm[d][:, c:c + 1], None, ALU.mult, None, [r_pU[d], r_D[d]], [rut])
                                        ACT(Sbf[d][:, c, :], Sc[:], AF.Identity, [rc_, r_D[d]], [r_Sbf[d]], scale=Hm[d][:, c:c + 1])
                                        if n < 35:
                                            STT("dve", Sn[:], Sc[:], Dd[d][:, c:c + 1], ut[:], ALU.mult, ALU.add, [rc_, r_D[d], rut], [rn])
                        for d in range(2):
                            for g4 in range(0, 18, 4):
                                ts_ = list(range(g4, min(g4 + 4, 18)))
                                for i, t in enumerate(ts_):
                                    for c in (2 * t, 2 * t + 1):
                                        MM(pA[:, i, :], ksp[d][:, c, :], qsp[d][:, c, :], c == 2 * t, c == 2 * t + 1, [r_sp], [r_pA])
                                n_ = len(ts_)
                                ka = (g4 // 4) % 2
                                CP("act", atmp[ka][:, 0:n_, :], pA[:, 0:n_, :], [r_pA], [r_at[ka]])
                                cm = -1 if d == 0 else 1
                                P.op("pool", lambda e, o=ATm[d][:, g4:g4 + n_, :], i_=atmp[ka][:, 0:n_, :], n_=n_, cm=cm:
                                     e.affine_select(out=o, in_=i_, pattern=[[0, n_], [-cm, 128]], compare_op=ALU.is_ge, fill=0.0,
                                                     base=0, channel_multiplier=cm), [r_at[ka]], [r_AT[d]])
                        for g4 in range(0, 18, 4):
                            ts_ = [t for t in range(g4, min(g4 + 4, 18))]
                            for i, t in enumerate(ts_):
                                if t not in ctiles:
                                    continue
                                mms = []
                                for d in range(2):
                                    mms.append((ATm[d][:, t, :], vtok[:, t, :], [r_AT[d], r_vt]))
                                    for c in (2 * t, 2 * t + 1):
                                        mms.append((qsp[d][:, c, :], Sbf[d][:, c, :], [r_sp, r_Sbf[d]]))
                                for mi, (lh, rh, rr) in enumerate(mms):
                                    MM(pA[:, i, :], lh, rh, mi == 0, mi == len(mms) - 1, rr, [r_pA])
                            for i, t in enumerate(ts_):
                                if t not in ctiles:
                                    continue
                                so = ost[:, i, :]
                                ACT(ojunk[:], pA[:, i, :], AF.Square, [r_pA], [r_oj, r_ost], accum=so[:, 0:1])
                                ACT(so[:, 1:2], so[:, 0:1], AF.Sqrt, [r_ost, r_const], [r_ost], scale=1.0 / 128, bias=epsT[:, 0:1])
                                P.op("dve", lambda e, so=so: e.reciprocal(out=so[:, 2:3], in_=so[:, 1:2]), [r_ost], [r_ost])
                                STT("dve", yb[i][:], pA[:, i, :], so[:, 2:3], gg[:, t, :], ALU.mult, ALU.mult, [r_pA, r_ost, r_gg],
                                    [r_yb[i]])
                            for i, t in enumerate(ts_):
                                if t not in ctiles:
                                    continue
                                TR(pTr[0][:, i, :], yb[i][:], identb[:], [r_yb[i], r_const], [r_pTr[0]])
                            live = [i for i, t in enumerate(ts_) if t in ctiles]
                            if live:
                                i0, i1 = live[0], live[-1] + 1
                                t0 = ts_[i0]
                                CP("act", yTh[:, t0 * 128:(t0 + i1 - i0) * 128].rearrange("p (i t) -> p i t", t=128), pTr[0][:, i0:i1, :],
                                   [r_pTr[0]], [r_yTh])
                        if stop == 80 and h == 0:
                            DUMP(17, yTh[:, 0:2048], 2048, [r_yTh], bf=True)
                            halt[0] = True
                            return
                        c_lo = ctiles[0] * 128
                        P.dma("sp", yT_d[h, :, c_lo:2304], yTh[:, c_lo:2304], reads=[r_yTh], writes=[r_yT[t] for t in ctiles])
                P.barrier()

            for l in range(n_layers):
                last = l == 3
                j = l // 2
                cur[0] = l
                load_mods(l)
                if CK(3):
                    return
                if l % 2 == 0:
                    phase_att(l, j, not last)
                    wo_src = wo_att[j]
                else:
                    phase_rec(l, j, not last)
                    wo_src = wo_rec[j]
                if CK(5):
                    return
                tiles = list(range(2, 18)) if last else list(range(18))
                if stop == 69:
                    tiles = [17]
                if stop == 70:
                    tiles = list(range(17, -1, -1))
                phase_wo(l, wo_src, tiles)
                if CK(6):
                    return
                if last:
                    groups = [[2, 3, 4, 5], list(range(6, 12)), list(range(12, 18))]
                else:
                    groups = [list(range(0, 6)), list(range(6, 12)), list(range(12, 18))]
                phase_mlp(l, groups, last)

            if n_layers == 4:
                with contextlib.ExitStack() as es:
                    fg = es.enter_context(SBT("fg", [128, 2048], F32))
                    r_fg = Res()
                    P.dma("sp", fg[:], fing_d.partition_broadcast(128), writes=[r_fg])
                    xf = [es.enter_context(SBT("xf%d" % i, [128, 2048], F32)) for i in range(2)]
                    fj = [es.enter_context(SBT("fj%d" % i, [128, 2048], BF16)) for i in range(2)]
                    fss = [es.enter_context(SBT("fss%d" % i, [128, 4], F32)) for i in range(2)]
                    r_xf = [Res(), Res()]
                    for t in range(2, 18):
                        k = t % 2
                        ss = fss[k]
                        P.dma("sp", xf[k][:], res[t * 128:(t + 1) * 128, :], reads=[r_res[t]], writes=[r_xf[k]])
                        ACT(fj[k][:], xf[k][:], AF.Square, [r_xf[k]], [r_xf[k]], accum=ss[:, 0:1])
                        ACT(ss[:, 1:2], ss[:, 0:1], AF.Sqrt, [r_xf[k], r_const], [r_xf[k]], scale=1.0 / 2048, bias=epsT[:, 0:1])
                        P.op("dve", lambda e, ss=ss: e.reciprocal(out=ss[:, 2:3], in_=ss[:, 1:2]), [r_xf[k]], [r_xf[k]])
                        STT("dve", xf[k][:], xf[k][:], ss[:, 2:3], fg[:], ALU.mult, ALU.mult, [r_xf[k], r_fg], [r_xf[k]])
                        P.dma("sp", out[(t - 2) * 128:(t - 1) * 128, :], xf[k][:], reads=[r_xf[k]], writes=[r_out])
        try:
            main_body()
        except _Stop:
            pass
        fin_reads = [r_out]
        if dbg and stop == 2:
            r_dbg = Res()
            P.dma("sp", dbg_o[0:128, 0:768], fmA[:].rearrange("p l j s -> p (l j s)"), reads=[r_fmA], writes=[r_dbg])
            fin_reads.append(r_dbg)
        elif dbg and dumped[0]:
            fin_reads.append(r_dump)
        elif dbg:
            r_dbg = Res()
            for t in range(18):
                P.dma("sp", dbg_o[t * 128:(t + 1) * 128, :], (xin if n_layers == 0 else res)[t * 128:(t + 1) * 128, :],
                      reads=[r_res[t]], writes=[r_dbg])
            fin_reads.append(r_dbg)
        P.op("sp", lambda e: e.nop(), fin_reads, ())
        P.emit(nc)
    return nc


_CONSTS = None


def make_in_maps(inputs, cores):
    global _CONSTS
    if _CONSTS is None:
        _CONSTS = make_consts()
    f = lambda a: np.ascontiguousarray(np.asarray(a, dtype=np.float32))
    x, c, ctx, c_ctx = f(inputs["x"]), f(inputs["c"]), f(inputs["ctx"]), f(inputs["c_ctx"])

    def fm(v):
        v = v.reshape(-1, 16, 128)
        return np.ascontiguousarray(v.transpose(2, 0, 1).reshape(128, -1))

    shared = {
        "w_ada": f(inputs["w_ada"]),
        "bada": np.ascontiguousarray(f(inputs["b_ada"]).reshape(4, 96, 128).transpose(2, 0, 1).reshape(128, 384)),
        "gmix": fm(f(inputs["norm_mix_g"])), "gmlp": fm(f(inputs["norm_mlp_g"])),
        "att_w_qkv": f(inputs["att_w_qkv"]), "att_w_o": f(inputs["att_w_o"]),
        "att_sink": f(inputs["att_sink"]).reshape(1, 32),
        "rec_w_in": f(inputs["rec_w_in"]), "rec_w_o": f(inputs["rec_w_o"]),
        "lbl": fm(f(inputs["rec_lb_logits"])), "rec_onorm_g": f(inputs["rec_onorm_g"]),
        "mlp_w_up": f(inputs["mlp_w_up"]), "mlp_w_down": f(inputs["mlp_w_down"]),
        "final_norm_g": f(inputs["final_norm_g"]).reshape(1, 2048),
    }
    shared.update(_CONSTS)
    maps = []
    for b in cores:
        m = dict(shared)
        m["xin"] = np.ascontiguousarray(np.concatenate([ctx[b], x[b]], axis=0))
        cc = np.stack([c[b], c_ctx], axis=0)
        m["ccl"] = np.ascontiguousarray(cc.reshape(2, 16, 128).transpose(2, 1, 0).reshape(128, 32))
        maps.append(m)
    return maps


def kernel(**inputs):
    nc = build(4, False)
    maps = make_in_maps(inputs, list(range(8)))
    r = run_bass_kernel_spmd(nc, maps, core_ids=list(range(8)))
    return np.stack([np.asarray(r.results[b]["out"], dtype=np.float32) for b in range(8)], axis=0)
```

```python
import contextlib
import numpy as np
import concourse.bass as bass
import concourse.mybir as mybir
from concourse.bass_utils import run_bass_kernel_spmd

F32 = mybir.dt.float32
BF16 = mybir.dt.bfloat16
AF = mybir.ActivationFunctionType
ALU = mybir.AluOpType
AX = mybir.AxisListType

ENGS = ("pe", "act", "dve", "pool", "sp")
DMA_SLOTS = 8


class Res:
    __slots__ = ("name", "last_w", "readers", "excl")

    def __init__(self, name="", excl=False):
        self.name = name
        self.last_w = None
        self.readers = []
        self.excl = excl


def PRes():
    return Res("psum", True)


class Op:
    __slots__ = ("eng", "fn", "deps", "needs_inc", "is_dma", "slot", "slot_n", "tok")

    def __init__(self, eng, fn):
        self.eng = eng
        self.fn = fn
        self.deps = []
        self.needs_inc = False
        self.is_dma = False
        self.slot = None
        self.slot_n = 0
        self.tok = None


class Prog:
    def __init__(self):
        self.ops = {e: [] for e in ENGS}
        self.dma_count = {e: 0 for e in ENGS}
        self.slot_last = {}
        self.barrier_deps = {e: [] for e in ENGS}

    def _add_dep(self, op, dep, kind):
        if dep is None or dep is op:
            return
        if not dep.is_dma and dep.eng == op.eng and not op.is_dma:
            if op.eng == "pe" or kind == "war" or kind == "excl":
                return
        for d in op.deps:
            if d is dep:
                return
        op.deps.append(dep)
        if not dep.is_dma:
            dep.needs_inc = True

    def _track(self, op, reads, writes):
        ex = [r for r in reads if r.excl] + [w for w in writes if w.excl]
        if ex:
            reads = [r for r in reads if not r.excl]
            writes = [w for w in writes if not w.excl]
            for x in ex:
                self._add_dep(op, x.last_w, "excl")
                x.last_w = op
        for r in reads:
            self._add_dep(op, r.last_w, "raw")
        for w in writes:
            self._add_dep(op, w.last_w, "waw")
            for rd in w.readers:
                self._add_dep(op, rd, "war")
        for r in reads:
            r.readers.append(op)
        for w in writes:
            w.last_w = op
            w.readers = []
        bd = self.barrier_deps[op.eng]
        if bd:
            for d in bd:
                if d is not op and not (d.eng == op.eng == "pe" and not d.is_dma and not op.is_dma):
                    if d not in op.deps:
                        op.deps.append(d)
                        if not d.is_dma:
                            d.needs_inc = True
            self.barrier_deps[op.eng] = []

    def op(self, eng, fn, reads=(), writes=()):
        o = Op(eng, fn)
        self._track(o, reads, writes)
        self.ops[eng].append(o)
        return o

    def dma(self, eng, out, in_, reads=(), writes=(), **kw):
        def fn(e):
            return e.dma_start(out=out, in_=in_, **kw)

        o = Op(eng, fn)
        o.is_dma = True
        n = self.dma_count[eng]
        self.dma_count[eng] = n + 1
        o.slot = n % DMA_SLOTS
        o.slot_n = n // DMA_SLOTS
        prev = self.slot_last.get((eng, o.slot))
        if prev is not None:
            o.deps.append(prev)
        self.slot_last[(eng, o.slot)] = o
        self._track(o, reads, writes)
        self.ops[eng].append(o)
        return o

    def barrier(self):
        lasts = []
        for e in ENGS:
            for o in reversed(self.ops[e]):
                if not o.is_dma:
                    lasts.append(o)
                    break
        for o in self.slot_last.values():
            lasts.append(o)
        for e in ENGS:
            self.barrier_deps[e] = list(lasts)

    def emit(self, nc):
        engmap = {"pe": "tensor", "act": "scalar", "dve": "vector", "pool": "gpsimd", "sp": "sync"}
        with contextlib.ExitStack() as es:
            sems = {e: es.enter_context(nc.semaphore("s_" + e)) for e in ENGS}
            dsems = {}
            for e in ENGS:
                if self.dma_count[e] > 0:
                    for s in range(DMA_SLOTS):
                        dsems[(e, s)] = es.enter_context(nc.semaphore("d_%s_%d" % (e, s)))
            for e in ENGS:
                cnt = 0
                for o in self.ops[e]:
                    if o.is_dma:
                        o.tok = (dsems[(e, o.slot)], 16 * (o.slot_n + 1))
                    elif o.needs_inc:
                        cnt += 1
                        o.tok = (sems[e], cnt)
            block = es.enter_context(nc.Block())

            def make(e):
                def body(eng):
                    seen = {}
                    for o in self.ops[e]:
                        for d in o.deps:
                            sem, val = d.tok
                            key = id(sem)
                            if seen.get(key, 0) < val:
                                eng.wait_ge(sem, val)
                                seen[key] = val
                        ins = o.fn(eng)
                        if o.is_dma:
                            ins.then_inc(o.tok[0], 16)
                        elif o.needs_inc:
                            ins.then_inc(o.tok[0], 1)

                return body

            for e in ENGS:
                if self.ops[e]:
                    getattr(block, engmap[e])(make(e))


NEG = -30000.0
SCALE = 128 ** -0.5


def make_consts():
    c = {}
    c["identf"] = np.eye(128, dtype=np.float32)
    rm = np.zeros((128, 128), np.float32)
    for m in range(128):
        a, p = (m // 64), (m % 64) // 32
        if p == 0:
            rm[m + 32, m] = -1.0
        else:
            rm[m - 32, m] = 1.0
    c["rotm"] = rm
    qi = np.arange(128)[:, None]
    kj = np.arange(128)[None, :]
    mb = np.zeros((3, 128, 384), np.float32)
    for v in range(3):
        mb[v, :, 0:128] = np.where(kj >= qi, 0.0, NEG)
        mb[v, :, 256:384] = np.where(kj <= qi, 0.0, NEG)
    mb[0, :, 0:128] = NEG
    mb[2, :, 256:384] = NEG
    c["amask"] = mb
    s = np.arange(128)[:, None]
    t = np.arange(128)[None, :]
    same = (s // 64) == (t // 64)
    rmask = np.zeros((2, 128, 128), np.float32)
    rmask[0] = (same & (s <= t)).astype(np.float32)
    rmask[1] = (same & (s >= t)).astype(np.float32)
    c["rmask"] = rmask
    pos = np.arange(2048)
    row = (pos // 64).astype(np.float32)
    col = (pos % 64).astype(np.float32)
    inv = (10000.0 ** (-np.arange(32, dtype=np.float32) / 32)).astype(np.float32)
    ar = row[:, None] * inv[None, :]
    ac = col[:, None] * inv[None, :]
    ang = np.concatenate([ar, ar, ac, ac], axis=-1).astype(np.float32)
    c["cosT"] = np.ascontiguousarray(np.cos(ang).T.astype(np.float32))
    c["sinT"] = np.ascontiguousarray(np.sin(ang).T.astype(np.float32))
    return c


class _Stop(Exception):
    pass


def build(n_layers=4, dbg=False, stop=0):
    nc = bass.Bass("TRN2", target_bir_lowering=False)
    P = Prog()

    _cnt = [0]

    def SBT(name, shape, dt):
        _cnt[0] += 1
        return nc.sbuf_tensor("%s_u%d" % (name, _cnt[0]), shape, dt)

    def PST(name, shape, dt):
        _cnt[0] += 1
        return nc.psum_tensor("%s_u%d" % (name, _cnt[0]), shape, dt)

    halt = [False]
    cur = [-1]

    def CK(n):
        if stop == n and (n <= 2 or cur[0] == n_layers - 1):
            halt[0] = True
        return halt[0]

    def din(name, shape, dt=F32):
        return nc.dram_tensor(name, list(shape), dt, kind="ExternalInput").ap()

    xin = din("xin", [2304, 2048])
    ccl = din("ccl", [128, 32])
    w_ada = din("w_ada", [4, 2048, 12288])
    bada_d = din("bada", [128, 384])
    gmix_d = din("gmix", [128, 64])
    gmlp_d = din("gmlp", [128, 64])
    wqkv = din("att_w_qkv", [2, 2048, 3072])
    wo_att = din("att_w_o", [2, 2048, 2048])
    sink_d = din("att_sink", [1, 32])
    win = din("rec_w_in", [2, 2048, 10240])
    wo_rec = din("rec_w_o", [2, 2048, 2048])
    lbl_d = din("lbl", [128, 64])
    onorm_d = din("rec_onorm_g", [2, 128])
    wup = din("mlp_w_up", [4, 2048, 8192])
    wdn = din("mlp_w_down", [4, 8192, 2048])
    fing_d = din("final_norm_g", [1, 2048])
    identf_d = din("identf", [128, 128])
    rotm_d = din("rotm", [128, 128])
    amask_d = din("amask", [3, 128, 384])
    rmask_d = din("rmask", [2, 128, 128])
    cosT_d = din("cosT", [128, 2048])
    sinT_d = din("sinT", [128, 2048])
    out = nc.dram_tensor("out", [2048, 2048], F32, kind="ExternalOutput").ap()
    res = nc.dram_tensor("res", [2304, 2048], F32).ap()
    yT_d = nc.dram_tensor("yT_d", [16, 128, 2304], BF16).ap()
    hT_d = nc.dram_tensor("hT_d", [16, 128, 2304], BF16).ap()
    if dbg:
        dbg_o = nc.dram_tensor("dbg", [2304, 2048], F32, kind="ExternalOutput").ap()

    r_res = [Res("res%d" % t) for t in range(18)]
    r_yT = [Res("yT%d" % t) for t in range(18)]
    r_hT = [Res("hT%d" % t) for t in range(18)]
    r_out = Res("out")

    r_dump = Res("dump")
    dumped = [False]

    def DUMP(i, ap, ncols, R, bf=False):
        if not dbg:
            return
        dumped[0] = True
        P.dma("pool" if bf else "sp", dbg_o[i * 128:(i + 1) * 128, 0:ncols], ap, reads=R, writes=[r_dump])

    def MM(o, lhsT, rhs, start, stop, R, W):
        P.op("pe", lambda e: e.matmul(o, lhsT=lhsT, rhs=rhs, start=start, stop=stop), R, W)

    def TR(o, i, ident, R, W):
        P.op("pe", lambda e: e.transpose(out=o, in_=i, identity=ident), R, W)

    def ACT(o, i, func, R, W, scale=1.0, bias=0.0, accum=None):
        if accum is None:
            P.op("act", lambda e: e.activation(out=o, in_=i, func=func, bias=bias, scale=scale), R, W)
        else:
            P.op("act", lambda e: e.activation(out=o, in_=i, func=func, bias=bias, scale=scale, accum_out=accum), R, W)

    def TS(eng, o, i, s1, s2, op0, op1, R, W):
        if s2 is None:
            P.op(eng, lambda e: e.tensor_scalar(out=o, in0=i, scalar1=s1, scalar2=None, op0=op0), R, W)
        else:
            P.op(eng, lambda e: e.tensor_scalar(out=o, in0=i, scalar1=s1, scalar2=s2, op0=op0, op1=op1), R, W)

    def TT(eng, o, a, b, op, R, W):
        P.op(eng, lambda e: e.tensor_tensor(out=o, in0=a, in1=b, op=op), R, W)

    def STT(eng, o, a, s, b, op0, op1, R, W):
        P.op(eng, lambda e: e.scalar_tensor_tensor(out=o, in0=a, scalar=s, in1=b, op0=op0, op1=op1), R, W)

    def CP(eng, o, i, R, W):
        if eng == "act":
            P.op("act", lambda e: e.copy(out=o, in_=i), R, W)
        else:
            P.op(eng, lambda e: e.tensor_copy(out=o, in_=i), R, W)

    def MS(eng, o, v, W):
        P.op(eng, lambda e: e.memset(o, v), (), W)

    with contextlib.ExitStack() as gs_:
        def gsb(name, shape, dt):
            return gs_.enter_context(SBT(name, shape, dt))

        identf = gsb("identf", [128, 128], F32)
        identb = gsb("identb", [128, 128], BF16)
        rotm = gsb("rotm", [128, 128], BF16)
        gmix = gsb("gmix", [128, 64], F32)
        gmlp = gsb("gmlp", [128, 64], F32)
        lbl = gsb("lbl", [128, 64], F32)
        lbt = gsb("lbt", [128, 3, 32], F32)
        epsT = gsb("epsT", [128, 1], F32)
        ones1 = gsb("ones1", [128, 1], F32)
        sinkr = gsb("sinkr", [128, 32], F32)
        fmA = gsb("fmA", [128, 4, 96, 2], F32)
        bada = gsb("bada", [128, 384], F32)
        r_fmA = Res("fmA")
        gsm = gsb("gsm", [128, 2, 2, 16], F32)
        r_const = Res("const")
        r_lbt = Res("lbt")
        r_fm = Res("fm")
        r_gate = Res("gate")

        P.dma("sp", identf[:], identf_d, writes=[r_const])
        P.dma("pool", identb[:], identf_d, writes=[r_const])
        P.dma("pool", rotm[:], rotm_d, writes=[r_const])
        P.dma("sp", gmix[:], gmix_d, writes=[r_const])
        P.dma("sp", gmlp[:], gmlp_d, writes=[r_const])
        P.dma("sp", lbl[:], lbl_d, writes=[r_const])
        P.dma("sp", bada[:], bada_d, writes=[r_const])
        P.dma("sp", sinkr[:], sink_d.partition_broadcast(128), writes=[r_const])
        MS("dve", epsT[:], 1e-6, [r_const])
        MS("dve", ones1[:], 1.0, [r_const])
        TS("dve", sinkr[:], sinkr[:], 1.0 / SCALE, None, ALU.mult, None, [r_const], [r_const])
        def main_body():
            if CK(1):
                return

            with contextlib.ExitStack() as ps_:
                def sb(name, shape, dt):
                    return ps_.enter_context(SBT(name, shape, dt))

                sil_f = sb("sil_f", [128, 32], F32)
                sil = sb("sil", [128, 16, 2], BF16)
                wb = [sb("wb%d" % i, [128, 16, 512], BF16) for i in range(3)]
                pm = ps_.enter_context(PST("pm", [128, 512], F32))
                r_sil = Res()
                r_wb = [Res() for _ in range(3)]
                r_pm = PRes()
                P.dma("sp", sil_f[:], ccl, writes=[r_sil])
                ACT(sil[:].rearrange("p c s -> p (c s)"), sil_f[:], AF.Silu, [r_sil], [r_sil])
                i = 0
                for l in range(n_layers):
                    for nb in range(24):
                        k = i % 3
                        i += 1
                        P.dma("pool", wb[k][:], w_ada[l, :, nb * 512:(nb + 1) * 512].rearrange("(c p) n -> p c n", p=128),
                              writes=[r_wb[k]])
                        for jj in range(4):
                            jc = nb * 4 + jj
                            for kc in range(16):
                                MM(pm[:, jc * 2:jc * 2 + 2], wb[k][:, kc, jj * 128:(jj + 1) * 128], sil[:, kc, :], kc == 0, kc == 15,
                                   [r_sil, r_wb[k]], [r_pm])
                    TT("dve", fmA[:, l, :, :], pm[:, 0:192].rearrange("p (j s) -> p j s", s=2),
                       bada[:, l * 96:(l + 1) * 96].rearrange("p (j o) -> p j o", o=1).to_broadcast([128, 96, 2]), ALU.add,
                       [r_pm, r_const], [r_fmA])
            P.barrier()
            if CK(2):
                return

            def stream_of(t):
                return 1 if t < 2 else 0

            def src_tile(l, t, mixer=True):
                s = xin if (l == 0 and mixer) else res
                return s[t * 128:(t + 1) * 128, :]

            class NormCtx:
                def __init__(self, es, tag):
                    def sb(name, shape, dt):
                        return es.enter_context(SBT(tag + name, shape, dt))

                    self.xt = [sb("xt%d" % i, [128, 2048], F32) for i in range(2)]
                    self.xn = [sb("xn%d" % i, [128, 2048], BF16) for i in range(2)]
                    self.ss = [sb("ss%d" % i, [128, 4], F32) for i in range(2)]
                    self.pT = es.enter_context(PST(tag + "pT", [128, 16, 128], BF16))
                    self.r_xt = [Res(), Res()]
                    self.r_xn = [Res(), Res()]
                    self.r_ss = [Res(), Res()]
                    self.r_pT = PRes()
                    self.n = 0

            def norm_tile(N, l, t, which, dst, r_dst):
                mixer = which == 0
                k = N.n % 2
                N.n += 1
                s = stream_of(t)
                xt, xn, ss = N.xt[k], N.xn[k], N.ss[k]
                P.dma("sp", xt[:], src_tile(l, t, mixer), reads=[r_res[t]], writes=[N.r_xt[k]])
                ACT(xn[:], xt[:], AF.Square, [N.r_xt[k]], [N.r_xn[k], N.r_ss[k]], accum=ss[:, 0:1])
                ACT(ss[:, 1:2], ss[:, 0:1], AF.Sqrt, [N.r_ss[k], r_const], [N.r_ss[k]], scale=1.0 / 2048, bias=epsT[:, 0:1])
                P.op("dve", lambda e: e.reciprocal(out=ss[:, 2:3], in_=ss[:, 1:2]), [N.r_ss[k]], [N.r_ss[k]])
                TS("dve", xn[:], xt[:], ss[:, 2:3], None, ALU.mult, None, [N.r_xt[k], N.r_ss[k]], [N.r_xn[k]])
                for kc in range(16):
                    TR(N.pT[:, kc, :], xn[:, kc * 128:(kc + 1) * 128], identb[:], [N.r_xn[k], r_const], [N.r_pT])
                sh = 0 if which == 0 else 48
                for kc in range(16):
                    sc_ap = gsm[:, s, which, kc:kc + 1]
                    bi_ap = fmA[:, l, sh + kc, s:s + 1]
                    if kc % 2 == 0:
                        ACT(dst[:, kc, :], N.pT[:, kc, :], AF.Identity, [N.r_pT, r_fm, r_fmA], [r_dst], scale=sc_ap, bias=bi_ap)
                    else:
                        TS("dve", dst[:, kc, :], N.pT[:, kc, :], sc_ap, bi_ap, ALU.mult, ALU.add, [N.r_pT, r_fm, r_fmA], [r_dst])

            def load_mods(l):
                for s in range(2):
                    STT("dve", gsm[:, s, 0, :], fmA[:, l, 16:32, s], 1.0, gmix[:, l * 16:(l + 1) * 16], ALU.add, ALU.mult,
                        [r_fmA, r_const], [r_fm])
                    STT("dve", gsm[:, s, 1, :], fmA[:, l, 64:80, s], 1.0, gmlp[:, l * 16:(l + 1) * 16], ALU.add, ALU.mult,
                        [r_fmA, r_const], [r_fm])

            def load_gate(es, l, m):
                gate_bc = [es.enter_context(SBT("gate%d" % s, [128, 2048], F32)) for s in range(2)]
                with contextlib.ExitStack() as eg:
                    rep = [eg.enter_context(SBT("rep%d" % i, [128, 128], F32)) for i in range(2)]
                    pg = eg.enter_context(PST("pg", [128, 512], F32))
                    r_rep = [Res(), Res()]
                    r_pg = PRes()
                    n = 0
                    for s in range(2):
                        for c4 in range(4):
                            for ci in range(4):
                                c = c4 * 4 + ci
                                k = n % 2
                                n += 1
                                CP("dve", rep[k][:], fmA[:, l, m * 16 + c, s:s + 1].to_broadcast([128, 128]), [r_fmA], [r_rep[k]])
                                MM(pg[:, ci * 128:(ci + 1) * 128], rep[k][:], identf[:], True, True, [r_rep[k], r_const], [r_pg])
                            CP("act", gate_bc[s][:, c4 * 512:(c4 + 1) * 512], pg[:], [r_pg], [r_gate])
                P.barrier()
                return gate_bc

            def phase_wo(l, wo_src, tiles):
                with contextlib.ExitStack() as es:
                    gate_bc = load_gate(es, l, 2)
                    if CK(60):
                        return
                    def sb(name, shape, dt):
                        return es.enter_context(SBT(name, shape, dt))

                    wo = sb("wo", [128, 16, 2048], BF16)
                    yt = [sb("yt%d" % i, [128, 16, 128], BF16) for i in range(2)]
                    xt = [sb("wxt%d" % i, [128, 2048], F32) for i in range(2)]
                    tmp = [sb("wtmp%d" % i, [128, 512], F32) for i in range(2)]
                    acc = [es.enter_context(PST("woacc%d" % i, [128, 512], F32)) for i in range(4)]
                    r_wo = [Res() for _ in range(4)]
                    r_yt, r_xt, r_tmp = [Res(), Res()], [Res(), Res()], [Res(), Res()]
                    r_acc = [PRes() for _ in range(4)]
                    for nb in range(4):
                        P.dma("pool", wo[:, :, nb * 512:(nb + 1) * 512],
                              wo_src[:, nb * 512:(nb + 1) * 512].rearrange("(c p) n -> p c n", p=128), writes=[r_wo[nb]])
                    if CK(61):
                        return
                    n = 0
                    for it, t in enumerate(tiles):
                        if it == 1 and CK(62):
                            return
                        if it == 2 and CK(63):
                            return
                        if it == 3 and CK(64):
                            return
                        if it == 8 and CK(65):
                            return
                        if it == 13 and CK(66):
                            return
                        if it == 17 and CK(67):
                            return
                        k = it % 2
                        s = stream_of(t)
                        P.dma("sp", yt[k][:], yT_d[:, :, t * 128:(t + 1) * 128].rearrange("c p t -> p c t"),
                              reads=[r_yT[t]], writes=[r_yt[k]])
                        P.dma("sp", xt[k][:], src_tile(l, t), reads=[r_res[t]], writes=[r_xt[k]])
                        for nb in range(4):
                            a = acc[n % 4]
                            ra = r_acc[n % 4]
                            for kc in range(16):
                                MM(a[:], yt[k][:, kc, :], wo[:, kc, nb * 512:(nb + 1) * 512], kc == 0, kc == 15,
                                   [r_yt[k], r_wo[nb]], [ra])
                            tm = tmp[n % 2]
                            TT("dve", tm[:], a[:], gate_bc[s][:, nb * 512:(nb + 1) * 512], ALU.mult, [ra, r_gate], [r_tmp[n % 2]])
                            TT("pool", xt[k][:, nb * 512:(nb + 1) * 512], tm[:], xt[k][:, nb * 512:(nb + 1) * 512], ALU.add,
                               [r_tmp[n % 2], r_xt[k]], [r_xt[k]])
                            n += 1
                        P.dma("sp", res[t * 128:(t + 1) * 128, :], xt[k][:], reads=[r_xt[k]], writes=[r_res[t]])
                    if CK(68) or CK(69) or CK(70):
                        return
                P.barrier()

            def phase_mlp(l, groups, last):
                with contextlib.ExitStack() as es:
                    gate_bc = load_gate(es, l, 5)
                    def sb(name, shape, dt):
                        return es.enter_context(SBT(name, shape, dt))

                    aT = sb("aT", [128, 64, 768], BF16)
                    r_aT = [Res() for _ in range(64)]
                    wu_i = 0
                    for g, tiles in enumerate(groups):
                        T = len(tiles) * 128
                        halves = [(0, T // 2), (T // 2, T)]
                        with contextlib.ExitStack() as es2:
                            def sb2(name, shape, dt):
                                return es2.enter_context(SBT(name, shape, dt))

                            hT = sb2("hT", [128, 16, 768], BF16)
                            r_h = Res()
                            wu = [sb2("wu%d" % i, [128, 16, 256], BF16) for i in range(2)]
                            r_wu = [Res(), Res()]
                            rl = [sb2("rl%d" % i, [128, 384], BF16) for i in range(2)]
                            r_rl = [Res(), Res()]
                            N = NormCtx(es2, "m")
                            up = [es2.enter_context(PST("up%d" % i, [128, 512], F32)) for i in range(4)]
                            r_up = [PRes() for _ in range(4)]
                            for it, t in enumerate(tiles):
                                norm_tile(N, l, t, 1, hT[:, :, it * 128:(it + 1) * 128], r_h)
                            nu = 0
                            for fb in range(32):
                                k = fb % 2
                                P.dma("pool", wu[k][:], wup[l, :, fb * 256:(fb + 1) * 256].rearrange("(c p) n -> p c n", p=128),
                                      writes=[r_wu[k]])
                                for fi in range(2):
                                    fc = fb * 2 + fi
                                    pa = [up[(nu * 2) % 4], up[(nu * 2 + 1) % 4]]
                                    rp = [r_up[(nu * 2) % 4], r_up[(nu * 2 + 1) % 4]]
                                    for kc in range(16):
                                        for hi, (c0, c1) in enumerate(halves):
                                            MM(pa[hi][:, 0:c1 - c0], wu[k][:, kc, fi * 128:(fi + 1) * 128], hT[:, kc, c0:c1],
                                               kc == 0, kc == 15, [r_wu[k], r_h], [rp[hi]])
                                    for hi, (c0, c1) in enumerate(halves):
                                        r_ = rl[(nu * 2 + hi) % 2]
                                        rr = r_rl[(nu * 2 + hi) % 2]
                                        ACT(r_[:, 0:c1 - c0], pa[hi][:, 0:c1 - c0], AF.Relu, [rp[hi]], [rr])
                                        TT("dve", aT[:, fc, c0:c1], r_[:, 0:c1 - c0], r_[:, 0:c1 - c0], ALU.mult, [rr], [r_aT[fc]])
                                    nu += 1
                        P.barrier()
                        with contextlib.ExitStack() as es2:
                            def sb2(name, shape, dt):
                                return es2.enter_context(SBT(name, shape, dt))

                            wd = [sb2("wd%d" % i, [128, 8, 512], BF16) for i in range(3)]
                            r_wd = [Res() for _ in range(3)]
                            xs = [sb2("xs%d" % i, [128, 512], F32) for i in range(6)]
                            r_xs = [Res() for _ in range(6)]
                            tm = [sb2("dtm%d" % i, [128, 512], F32) for i in range(2)]
                            r_tm = [Res(), Res()]
                            acc = [es2.enter_context(PST("dacc%d" % i, [128, 512], F32)) for i in range(6)]
                            r_acc = [PRes() for _ in range(6)]
                            wi = 0
                            ne = 0
                            for nb in range(4):
                                for it, t in enumerate(tiles):
                                    P.dma("sp", xs[it][:], src_tile(l, t, False)[:, nb * 512:(nb + 1) * 512], reads=[r_res[t]],
                                          writes=[r_xs[it]])
                                for fb in range(8):
                                    k = wi % 3
                                    wi += 1
                                    P.dma("pool", wd[k][:],
                                          wdn[l, fb * 1024:(fb + 1) * 1024, nb * 512:(nb + 1) * 512].rearrange("(c p) n -> p c n", p=128),
                                          writes=[r_wd[k]])
                                    for fi in range(8):
                                        fc = fb * 8 + fi
                                        for it in range(len(tiles)):
                                            MM(acc[it][:], aT[:, fc, it * 128:(it + 1) * 128], wd[k][:, fi, :], fc == 0, fc == 63,
                                               [r_aT[fc], r_wd[k]], [r_acc[it]])
                                for it, t in enumerate(tiles):
                                    s = stream_of(t)
                                    tt_ = tm[ne % 2]
                                    rt = r_tm[ne % 2]
                                    ne += 1
                                    TT("dve", tt_[:], acc[it][:], gate_bc[s][:, nb * 512:(nb + 1) * 512], ALU.mult,
                                       [r_acc[it], r_gate], [rt])
                                    TT("pool", xs[it][:], tt_[:], xs[it][:], ALU.add, [rt, r_xs[it]], [r_xs[it]])
                                    P.dma("sp", res[t * 128:(t + 1) * 128, nb * 512:(nb + 1) * 512], xs[it][:],
                                          reads=[r_xs[it]], writes=[r_res[t]])
                        P.barrier()

            def phase_att(l, j, need_ctx):
                with contextlib.ExitStack() as es:
                    def sb(name, shape, dt):
                        return es.enter_context(SBT(name, shape, dt))

                    qT = sb("qT", [128, 16, 2304], BF16)
                    kT = sb("kT", [128, 4, 2560], BF16)
                    V = sb("V", [128, 20, 512], BF16)
                    r_q = [Res() for _ in range(18)]
                    r_k = [Res() for _ in range(20)]
                    r_v = [Res() for _ in range(20)]
                    MS("pool", kT[:, :, 256:384], 0.0, [r_k[2]])
                    MS("pool", kT[:, :, 2432:2560], 0.0, [r_k[19]])
                    MS("pool", V[:, 2, :], 0.0, [r_v[2]])
                    MS("pool", V[:, 19, :], 0.0, [r_v[19]])

                    def kslot(t):
                        return t if t < 2 else t + 1

                    with contextlib.ExitStack() as es2:
                        def sb2(name, shape, dt):
                            return es2.enter_context(SBT(name, shape, dt))

                        hT = sb2("ahT", [128, 16, 768], BF16)
                        wq = [sb2("wq%d" % i, [128, 16, 256], BF16) for i in range(2)]
                        r_wq = [Res(), Res()]
                        cosT = sb2("cosT", [128, 768], F32)
                        sinT = sb2("sinT", [128, 768], F32)
                        r_tab = Res()
                        qb = [sb2("qb%d" % i, [128, 384], BF16) for i in range(2)]
                        t1 = [sb2("t1%d" % i, [128, 384], F32) for i in range(2)]
                        t2 = [sb2("t2%d" % i, [128, 384], F32) for i in range(2)]
                        r_qb, r_t1, r_t2 = [Res(), Res()], [Res(), Res()], [Res(), Res()]
                        N = NormCtx(es2, "a")
                        pq = [es2.enter_context(PST("pq%d" % i, [128, 512], F32)) for i in range(2)]
                        pr = [es2.enter_context(PST("pr%d" % i, [128, 512], F32)) for i in range(2)]
                        pv = [es2.enter_context(PST("pv%d" % i, [128, 512], F32)) for i in range(2)]
                        r_pq, r_pr, r_pv = [PRes(), PRes()], [PRes(), PRes()], [PRes(), PRes()]
                        wi = 0
                        nq = 0
                        nv = 0
                        for g in range(3):
                            tiles = list(range(g * 6, g * 6 + 6))
                            r_h = Res()
                            if CK(40):
                                return
                            for it, t in enumerate(tiles):
                                norm_tile(N, l, t, 0, hT[:, :, it * 128:(it + 1) * 128], r_h)
                                if CK(41):
                                    return
                            if CK(42):
                                return
                            lat0 = g * 768 - 256
                            p0 = max(lat0, 0)
                            ncols = 768 - (p0 - lat0)
                            P.dma("sp", cosT[:, p0 - lat0:768], cosT_d[:, p0:p0 + ncols], writes=[r_tab])
                            P.dma("sp", sinT[:, p0 - lat0:768], sinT_d[:, p0:p0 + ncols], writes=[r_tab])
                            for blk in range(12):
                                if blk == 1 and CK(43):
                                    return
                                if blk == 9 and CK(44):
                                    return
                                if blk == 11 and CK(45):
                                    return
                                k = wi % 2
                                wi += 1
                                P.dma("pool", wq[k][:], wqkv[j, :, blk * 256:(blk + 1) * 256].rearrange("(c p) n -> p c n", p=128),
                                      writes=[r_wq[k]])
                                if blk < 10:
                                    for hh in range(2):
                                        for half in range(2):
                                            c0 = half * 384
                                            a = nq % 2
                                            nq += 1
                                            for kc in range(16):
                                                MM(pq[a][:, 0:384], wq[k][:, kc, hh * 128:(hh + 1) * 128], hT[:, kc, c0:c0 + 384],
                                                   kc == 0, kc == 15, [r_wq[k], r_h], [r_pq[a]])
                                            if CK(46):
                                                return
                                            if blk < 8:
                                                head = blk * 2 + hh

                                                def dst(ca, cb, head=head, g=g):
                                                    return qT[:, head, g * 768 + ca:g * 768 + cb]
                                                wres = [r_q[t] for t in tiles[half * 3:half * 3 + 3]]
                                            else:
                                                kvh = (blk - 8) * 2 + hh

                                                def dst(ca, cb, kvh=kvh, g=g):
                                                    sa = g * 768 + ca
                                                    off = 0 if sa < 256 else 128
                                                    return kT[:, kvh, sa + off:sa + off + (cb - ca)]
                                                wres = [r_k[kslot(t)] for t in tiles[half * 3:half * 3 + 3]]
                                            if g == 0 and half == 0:
                                                segs = [(0, 256, False), (256, 384, True)]
                                            else:
                                                segs = [(c0, c0 + 384, True)]
                                            for (ca, cb, rope) in segs:
                                                la, lb_ = ca - c0, cb - c0
                                                if not rope:
                                                    CP("act", dst(ca, cb), pq[a][:, la:lb_], [r_pq[a]], wres)
                                                    if CK(47):
                                                        return
                                                else:
                                                    CP("act", qb[a][:, la:lb_], pq[a][:, la:lb_], [r_pq[a]], [r_qb[a]])
                                                    if CK(49):
                                                        return
                                                    MM(pr[a][:, la:lb_], rotm[:], qb[a][:, la:lb_], True, True, [r_qb[a], r_const],
                                                       [r_pr[a]])
                                                    if CK(50):
                                                        return
                                                    TT("dve", t1[a][:, la:lb_], pq[a][:, la:lb_], cosT[:, ca:cb], ALU.mult,
                                                       [r_pq[a], r_tab], [r_t1[a]])
                                                    if CK(51):
                                                        return
                                                    TT("dve", t2[a][:, la:lb_], pr[a][:, la:lb_], sinT[:, ca:cb], ALU.mult,
                                                       [r_pr[a], r_tab], [r_t2[a]])
                                                    if CK(48):
                                                        return
                                                    TT("pool", dst(ca, cb), t1[a][:, la:lb_], t2[a][:, la:lb_], ALU.add,
                                                       [r_t1[a], r_t2[a]], wres)
                                else:
                                    vh = blk - 10
                                    for it, t in enumerate(tiles):
                                        a = nv % 2
                                        nv += 1
                                        for kc in range(16):
                                            MM(pv[a][:, 0:256], hT[:, kc, it * 128:(it + 1) * 128], wq[k][:, kc, :], kc == 0, kc == 15,
                                               [r_wq[k], r_h], [r_pv[a]])
                                        CP("act", V[:, kslot(t), vh * 256:(vh + 1) * 256], pv[a][:, 0:256], [r_pv[a]], [r_v[kslot(t)]])
                    P.barrier()
                    if CK(4):
                        return
                    with contextlib.ExitStack() as es2:
                        def sb2(name, shape, dt):
                            return es2.enter_context(SBT(name, shape, dt))

                        amask = sb2("amask", [128, 3, 384], BF16)
                        r_am = Res()
                        P.dma("pool", amask[:], amask_d.rearrange("v q k -> q v k"), writes=[r_am])
                        pb = [sb2("pb%d" % i, [128, 640], BF16) for i in range(2)]
                        pTs = [sb2("pTs%d" % i, [128, 5, 128], BF16) for i in range(2)]
                        st = [sb2("ast%d" % i, [128, 8], F32) for i in range(2)]
                        oTs = [sb2("oTs%d" % i, [128, 16, 128], BF16) for i in range(2)]
                        r_pb, r_pTs, r_st, r_oTs = [Res(), Res()], [Res(), Res()], [Res(), Res()], [Res(), Res()]
                        S = [es2.enter_context(PST("S%d" % i, [128, 1024], F32)) for i in range(2)]
                        pTp = [es2.enter_context(PST("pTp%d" % i, [128, 8, 128], BF16)) for i in range(2)]
                        oTp = [es2.enter_context(PST("oTp%d" % i, [128, 4, 128], F32)) for i in range(2)]
                        r_S, r_pTp, r_oTp = [PRes(), PRes()], [PRes(), PRes()], [PRes(), PRes()]
                        qtiles = list(range(18)) if need_ctx else list(range(2, 18))
                        units = [(iq, t, h) for iq, t in enumerate(qtiles) for h in range(16)]

                        def stage_a(u):
                            iq, t, h = units[u]
                            kv = h // 4
                            a = u % 2
                            qap = qT[:, h, t * 128:(t + 1) * 128]
                            if t < 2:
                                lo, hi = 512, 768
                                MM(S[a][:, 512:768], qap, kT[:, kv, 0:256], True, True, [r_q[t], r_k[0], r_k[1]], [r_S[a]])
                            else:
                                b = t - 2
                                lo, hi = 128, 768
                                kc0 = 256 + b * 128
                                var = 0 if b == 0 else (2 if b == 15 else 1)
                                MM(S[a][:, 128:512], qap, kT[:, kv, kc0:kc0 + 384], True, False,
                                   [r_q[t], r_k[2 + b], r_k[3 + b], r_k[4 + b]], [r_S[a]])
                                MM(S[a][:, 128:512], identb[:], amask[:, var, :], False, True, [r_am, r_const], [r_S[a]])
                                MM(S[a][:, 512:768], qap, kT[:, kv, 0:256], True, True, [r_q[t], r_k[0], r_k[1]], [r_S[a]])
                            n = hi - lo
                            sa = st[a]
                            P.op("dve", lambda e, sa=sa, Sa=S[a], lo=lo, hi=hi: e.reduce_max(out=sa[:, 0:1], in_=Sa[:, lo:hi], axis=AX.X),
                                 [r_S[a]], [r_st[a]])
                            TS("dve", sa[:, 1:2], sa[:, 0:1], sinkr[:, j * 16 + h:j * 16 + h + 1], -SCALE, ALU.max, ALU.mult,
                               [r_st[a], r_const], [r_st[a]])
                            ACT(pb[a][:, 0:n], S[a][:, lo:hi], AF.Exp, [r_S[a], r_st[a]], [r_pb[a], r_st[a]], scale=SCALE,
                                bias=sa[:, 1:2], accum=sa[:, 2:3])
                            ACT(sa[:, 3:4], sinkr[:, j * 16 + h:j * 16 + h + 1], AF.Exp, [r_st[a], r_const], [r_st[a]], scale=SCALE,
                                bias=sa[:, 1:2])
                            TT("dve", sa[:, 4:5], sa[:, 2:3], sa[:, 3:4], ALU.add, [r_st[a]], [r_st[a]])
                            P.op("dve", lambda e, sa=sa: e.reciprocal(out=sa[:, 5:6], in_=sa[:, 4:5]), [r_st[a]], [r_st[a]])
                            TS("dve", pb[a][:, 0:n], pb[a][:, 0:n], sa[:, 5:6], None, ALU.mult, None, [r_pb[a], r_st[a]], [r_pb[a]])

                        def stage_b(u):
                            iq, t, h = units[u]
                            kv = h // 4
                            a = u % 2
                            ob = iq % 2
                            if t < 2:
                                kslots = [0, 1]
                            else:
                                b = t - 2
                                kslots = [2 + b, 3 + b, 4 + b, 0, 1]
                            nk = len(kslots)
                            for jk in range(nk):
                                TR(pTp[a][:, jk, :], pb[a][:, jk * 128:(jk + 1) * 128], identb[:], [r_pb[a], r_const], [r_pTp[a]])
                            CP("act", pTs[a][:, 0:nk, :], pTp[a][:, 0:nk, :], [r_pTp[a]], [r_pTs[a]])
                            oa = (u // 4) % 2
                            hh = h % 4
                            for jk in range(nk):
                                MM(oTp[oa][:, hh, :], V[:, kslots[jk], kv * 128:(kv + 1) * 128], pTs[a][:, jk, :], jk == 0,
                                   jk == nk - 1, [r_pTs[a], r_v[kslots[jk]]], [r_oTp[oa]])
                            if hh == 3:
                                CP("dve", oTs[ob][:, h - 3:h + 1, :], oTp[oa][:], [r_oTp[oa]], [r_oTs[ob]])
                            if h == 15:
                                P.dma("sp", yT_d[:, :, t * 128:(t + 1) * 128].rearrange("c p t -> p c t"), oTs[ob][:],
                                      reads=[r_oTs[ob]], writes=[r_yT[t]])

                        for u in range(len(units) + 1):
                            if u < len(units):
                                stage_a(u)
                            if u >= 1:
                                stage_b(u - 1)
                P.barrier()

            def phase_rec(l, j, need_ctx):
                if j == 0:
                    MS("dve", lbt[:, 0, :], 0.0, [r_lbt])
                else:
                    TT("dve", lbt[:, 0, :], lbl[:, 32:64], lbl[:, 0:32], ALU.subtract, [r_const], [r_lbt])
                    ACT(lbt[:, 0, :], lbt[:, 0, :], AF.Sigmoid, [r_lbt], [r_lbt])
                TS("dve", lbt[:, 1, :], lbt[:, 0, :], -1.0, 1.0, ALU.mult, ALU.add, [r_lbt], [r_lbt])
                TS("dve", lbt[:, 2, :], lbt[:, 1, :], -1.0, None, ALU.mult, None, [r_lbt], [r_lbt])
                with contextlib.ExitStack() as es:
                    N = NormCtx(es, "r")
                    hs = [es.enter_context(SBT("hs%d" % i, [128, 16, 128], BF16)) for i in range(2)]
                    r_hs = [Res(), Res()]
                    for t in range(18):
                        k = t % 2
                        norm_tile(N, l, t, 0, hs[k][:], r_hs[k])
                        P.dma("sp", hT_d[:, :, t * 128:(t + 1) * 128].rearrange("c p t -> p c t"), hs[k][:], reads=[r_hs[k]],
                              writes=[r_hT[t]])
                P.barrier()
                with contextlib.ExitStack() as es:
                    def sb(name, shape, dt):
                        return es.enter_context(SBT(name, shape, dt))

                    def ps(name, shape, dt):
                        return es.enter_context(PST(name, shape, dt))

                    wh = [sb("wh%d" % i, [128, 16, 5, 128], BF16) for i in range(2)]
                    r_wh = [Res(), Res()]
                    hp = [sb("hp%d" % i, [128, 16, 384], BF16) for i in range(2)]
                    r_hp = [Res(), Res()]
                    rmask = sb("rmask", [128, 2, 128], BF16)
                    onb = sb("onb", [128, 128], F32)
                    r_rc = Res()
                    P.dma("pool", rmask[:], rmask_d.rearrange("v s t -> s v t"), writes=[r_rc])
                    P.dma("sp", onb[:], onorm_d[j:j + 1, :].partition_broadcast(128), writes=[r_rc])
                    qsp = [sb("qsp%d" % d, [128, 36, 128], BF16) for d in range(2)]
                    ksp = [sb("ksp%d" % d, [128, 36, 128], BF16) for d in range(2)]
                    r_qsp = [[Res() for _ in range(6)] for _ in range(2)]
                    r_ksp = [[Res() for _ in range(6)] for _ in range(2)]
                    for d in range(2):
                        MS("pool", qsp[d][:], 0.0, r_qsp[d])
                        MS("pool", ksp[d][:], 0.0, r_ksp[d])
                    Dd = [sb("Dd%d" % d, [128, 36], F32) for d in range(2)]
                    Hm = [sb("Hm%d" % d, [128, 36], F32) for d in range(2)]
                    Gm = [sb("Gm%d" % d, [128, 36], F32) for d in range(2)]
                    utmp = [[sb("utmp%d_%d" % (d, i), [128, 128], F32) for i in range(2)] for d in range(2)]
                    r_ut = [[Res(), Res()], [Res(), Res()]]
                    atmp = [sb("atmp%d" % i, [128, 4, 128], BF16) for i in range(2)]
                    r_at = [Res(), Res()]
                    Sbf = [sb("Sbf%d" % d, [128, 36, 128], BF16) for d in range(2)]
                    ATm = [sb("ATm%d" % d, [128, 18, 128], BF16) for d in range(2)]
                    vtok = sb("vtok", [128, 18, 128], BF16)
                    gg = sb("gg", [128, 18, 128], F32)
                    yTh = sb("yTh", [128, 2304], BF16)
                    Tst = [[sb("Tst%d_%d" % (d, i), [128, 128], F32) for i in range(2)] for d in range(2)]
                    ktok = [sb("ktok%d" % i, [128, 4, 128], BF16) for i in range(2)]
                    r_ktok = [Res(), Res()]
                    r_D, r_Sbf, r_AT = [Res(), Res()], [Res(), Res()], [Res(), Res()]
                    r_T = [[Res(), Res()], [Res(), Res()]]
                    r_vt, r_gg, r_yTh = Res(), Res(), Res()
                    qs2 = [sb("qs%d" % i, [128, 384], F32) for i in range(2)]
                    tF2 = [[sb("tF%d_%d" % (i, d), [128, 384], F32) for d in range(2)] for i in range(2)]
                    tK2 = [[sb("tK%d_%d" % (i, d), [128, 384], F32) for d in range(2)] for i in range(2)]
                    Bz2 = [[sb("Bz%d_%d" % (i, d), [128, 385], F32) for d in range(2)] for i in range(2)]
                    tE2 = [[sb("tE%d_%d" % (i, d), [128, 384], F32) for d in range(2)] for i in range(2)]
                    tX2 = [[sb("tX%d_%d" % (i, d), [128, 384], F32) for d in range(2)] for i in range(2)]
                    dD2 = [[sb("dD%d_%d" % (i, d), [128, 18], F32) for d in range(2)] for i in range(2)]
                    r_qs2 = [Res(), Res()]
                    r_tF2, r_tK2, r_Bz2, r_tE2, r_tX2, r_dD2 = ([[Res(), Res()], [Res(), Res()]] for _ in range(6))
                    for i in range(2):
                        for d in range(2):
                            MS("dve", Bz2[i][d][:, 0:1], 0.0, [r_Bz2[i][d]])
                    ost = sb("ost", [128, 4, 4], F32)
                    ojunk = sb("ojunk", [128, 128], BF16)
                    yb = [sb("yb%d" % i, [128, 128], BF16) for i in range(4)]
                    r_ost, r_oj = Res(), Res()
                    r_yb = [Res() for _ in range(4)]
                    zq_ = ps("zq", [128, 512], F32)
                    zq = zq_[:, 0:384]
                    zf_ = [ps("zf%d" % d, [128, 512], F32) for d in range(2)]
                    zf = [z[:, 0:384] for z in zf_]
                    vg = ps("vg", [128, 2, 256], F32)
                    pA = ps("pA", [128, 4, 128], F32)
                    pTr_ = ps("pTr", [128, 2, 4, 128], BF16)
                    pTr = [pTr_[:, i] for i in range(2)]
                    pU = [ps("pU%d" % i, [128, 4, 128], F32) for i in range(2)]
                    r_zq, r_pA = PRes(), PRes()
                    _rp = PRes()
                    r_pTr = [_rp, _rp]
                    r_zf = [PRes(), PRes()]
                    _rv = PRes()
                    r_vg = [_rv, _rv]
                    r_pU = [PRes(), PRes()]

                    nwh = 0
                    nhp = 0
                    nvg = 0
                    for h in range(16):
                        k = nwh % 2
                        nwh += 1
                        for si, part in enumerate((0, 2, 3, 1, 4)):
                            c0 = part * 2048 + h * 128
                            P.dma("pool", wh[k][:, :, si, :], win[j, :, c0:c0 + 128].rearrange("(c p) n -> p c n", p=128),
                                  writes=[r_wh[k]])
                        for pc in range(6):
                            kh = nhp % 2
                            nhp += 1
                            tiles = [pc * 3 + i for i in range(3)]
                            P.dma("sp", hp[kh][:], hT_d[:, :, pc * 384:(pc + 1) * 384].rearrange("c p t -> p c t"),
                                  reads=[r_hT[t] for t in tiles], writes=[r_hp[kh]])
                            for kc in range(16):
                                MM(zq, wh[k][:, kc, 0, :], hp[kh][:, kc, :], kc == 0, kc == 15, [r_wh[k], r_hp[kh]], [r_zq])
                            for d in range(2):
                                for kc in range(16):
                                    MM(zf[d], wh[k][:, kc, 1 + d, :], hp[kh][:, kc, :], kc == 0, kc == 15, [r_wh[k], r_hp[kh]],
                                       [r_zf[d]])
                            for i, t in enumerate(tiles):
                                a = nvg % 2
                                nvg += 1
                                for kc in range(16):
                                    MM(vg[:, a, :], hp[kh][:, kc, i * 128:(i + 1) * 128], wh[k][:, kc, 3:5, :], kc == 0, kc == 15,
                                       [r_wh[k], r_hp[kh]], [r_vg[a]])
                                CP("act", vtok[:, t, :], vg[:, a, 0:128], [r_vg[a]], [r_vt])
                                ACT(gg[:, t, :], vg[:, a, 128:256], AF.Silu, [r_vg[a]], [r_gg])
                                TT("pool", gg[:, t, :], gg[:, t, :], onb[:], ALU.mult, [r_gg, r_rc], [r_gg])
                            pp = pc % 2
                            qs, tF, tK, Bz, tE, tX, dD = qs2[pp], tF2[pp], tK2[pp], Bz2[pp], tE2[pp], tX2[pp], dD2[pp]
                            r_qs, r_tF, r_tK, r_Bz, r_tE, r_tX, r_dD = (r_qs2[pp], r_tF2[pp], r_tK2[pp], r_Bz2[pp], r_tE2[pp],
                                                                        r_tX2[pp], r_dD2[pp])
                            ACT(qs[:], zq, AF.Silu, [r_zq], [r_qs])
                            ch0 = pc * 6
                            for d in range(2):
                                col = d * 16 + h
                                ACT(tF[d][:], zf[d], AF.Sigmoid, [r_zf[d]], [r_tF[d]])
                                TS("dve", tK[d][:], tF[d][:], lbt[:, 2, col:col + 1], lbt[:, 1, col:col + 1], ALU.mult, ALU.add,
                                   [r_tF[d], r_lbt], [r_tK[d]])
                                TS("dve", tF[d][:], tF[d][:], lbt[:, 1, col:col + 1], lbt[:, 0, col:col + 1], ALU.mult, ALU.add,
                                   [r_tF[d], r_lbt], [r_tF[d]])
                                ACT(tF[d][:], tF[d][:], AF.Ln, [r_tF[d]], [r_tF[d]])
                                P.op("dve", lambda e, bo=Bz[d][:, 1:385], fi=tF[d][:]: e.tensor_tensor_scan(
                                    out=bo, data0=ones1[:].to_broadcast([128, 384]), data1=fi, initial=0.0, op0=ALU.mult, op1=ALU.add),
                                     [r_tF[d], r_const], [r_Bz[d]])
                                bzc = Bz[d][:, 0:384].rearrange("p (c j) -> p c j", j=64)
                                bze = Bz[d][:, 1:385].rearrange("p (c j) -> p c j", j=64)
                                in0 = bze if d == 0 else bzc
                                in1 = bzc[:, :, 32:33].to_broadcast([128, 6, 64])
                                TT("dve", tE[d][:].rearrange("p (c j) -> p c j", j=64), in0, in1, ALU.subtract, [r_Bz[d]], [r_tE[d]])
                                TT("dve", dD[d][:, 0:6].rearrange("p (c o) -> p c o", o=1), bzc[:, :, 32:33], bzc[:, :, 0:1],
                                   ALU.subtract, [r_Bz[d]], [r_dD[d]])
                                TT("dve", dD[d][:, 6:12].rearrange("p (c o) -> p c o", o=1), bze[:, :, 63:64], bzc[:, :, 32:33],
                                   ALU.subtract, [r_Bz[d]], [r_dD[d]])
                                TT("dve", dD[d][:, 12:18].rearrange("p (c o) -> p c o", o=1), bze[:, :, 63:64], bzc[:, :, 0:1],
                                   ALU.subtract, [r_Bz[d]], [r_dD[d]])
                                ha, ga = (0, 6) if d == 0 else (6, 0)
                                ACT(Hm[d][:, ch0:ch0 + 6], dD[d][:, ha:ha + 6], AF.Exp, [r_dD[d]], [r_D[d]])
                                ACT(Gm[d][:, ch0:ch0 + 6], dD[d][:, ga:ga + 6], AF.Exp, [r_dD[d]], [r_D[d]])
                                ACT(Dd[d][:, ch0:ch0 + 6], dD[d][:, 12:18], AF.Exp, [r_dD[d]], [r_D[d]])
                                ACT(tX[d][:], tE[d][:], AF.Exp, [r_tE[d]], [r_tX[d]])
                                ACT(tE[d][:], tE[d][:], AF.Exp, [r_tE[d]], [r_tE[d]], scale=-1.0)
                                qfac, kfac = (tX[d], tE[d]) if d == 0 else (tE[d], tX[d])
                                for par in range(2):
                                    o_q = qsp[d][:, ch0 + par:ch0 + 6:2, par * 64:par * 64 + 64]
                                    o_k = ksp[d][:, ch0 + par:ch0 + 6:2, par * 64:par * 64 + 64]
                                    v = lambda tl: tl[:].rearrange("p (c j) -> p c j", j=64)[:, par:6:2, :]
                                    TT("pool", o_q, v(qs), v(qfac), ALU.mult, [r_qs, r_tX[d], r_tE[d]], [r_qsp[d][pc]])
                                    TT("pool" if par else "dve", o_k, v(tK[d]), v(kfac), ALU.mult, [r_tK[d], r_tX[d], r_tE[d]], [r_ksp[d][pc]])
                        if stop == 80 and h == 0:
                            DUMP(0, qs[:], 384, [r_qs])
                            for d in range(2):
                                DUMP(1 + d * 5, tF[d][:], 384, [r_tF[d]])
                                DUMP(2 + d * 5, tK[d][:], 384, [r_tK[d]])
                                DUMP(3 + d * 5, Bz[d][:], 385, [r_Bz[d]])
                                DUMP(4 + d * 5, tE[d][:], 384, [r_tE[d]])
                                DUMP(5 + d * 5, tX[d][:], 384, [r_tX[d]])
                            DUMP(11, Dd[0][:], 36, [r_D[0]])
                            DUMP(12, Dd[1][:], 36, [r_D[1]])
                            DUMP(13, gg[:].rearrange("p t v -> p (t v)")[:, 0:2048], 2048, [r_gg])
                            DUMP(14, qsp[0][:].rearrange("p c j -> p (c j)")[:, 0:2048], 2048, r_qsp[0], bf=True)
                            DUMP(15, ksp[0][:].rearrange("p c j -> p (c j)")[:, 0:2048], 2048, r_ksp[0], bf=True)
                            DUMP(16, vtok[:].rearrange("p t v -> p (t v)")[:, 0:2048], 2048, [r_vt], bf=True)
                        ctiles = list(range(18)) if need_ctx else list(range(2, 18))
                        seqs = [list(range(36)), [3, 2, 1, 0] + list(range(35, 3, -1))]
                        for d in range(2):
                            MS("pool", Sbf[d][:, seqs[d][0], :], 0.0, [r_Sbf[d]])
                        for g4 in range(9):
                            for d in range(2):
                                cs = seqs[d][g4 * 4:(g4 + 1) * 4]
                                for i, c in enumerate(cs):
                                    TR(pTr[d][:, i, :], ksp[d][:, c, :], identb[:], [r_ksp[d][c // 6], r_const], [r_pTr[d]])
                                CP("act", ktok[d][:], pTr[d], [r_pTr[d]], [r_ktok[d]])
                                for i, c in enumerate(cs):
                                    MM(pU[d][:, i, :], ktok[d][:, i, :], vtok[:, c // 2, :], True, True, [r_ktok[d], r_vt],
                                       [r_pU[d]])
                            for i in range(4):
                                n = g4 * 4 + i
                                for d in range(2):
                                    c = seqs[d][n]
                                    Sc, Sn = Tst[d][n % 2], Tst[d][(n + 1) % 2]
                                    rc_, rn = r_T[d][n % 2], r_T[d][(n + 1) % 2]
                                    if n == 0:
                                        TS("dve", Sn[:], pU[d][:, i, :], Gm[d][:, c:c + 1], None, ALU.mult, None, [r_pU[d], r_D[d]], [rn])
                                    else:
                                        ut, rut = utmp[d][n % 2], r_ut[d][n % 2]
                                        TS("dve", ut[:], pU[d][:, i, :], Gm[d][:, c:c + 1], None, ALU.mult, None, [r_pU[d], r_D[d]], [rut])
                                        ACT(Sbf[d][:, c, :], Sc[:], AF.Identity, [rc_, r_D[d]], [r_Sbf[d]], scale=Hm[d][:, c:c + 1])
                                        if n < 35:
                                            STT("dve", Sn[:], Sc[:], Dd[d][:, c:c + 1], ut[:], ALU.mult, ALU.add, [rc_, r_D[d], rut], [rn])
                        for d in range(2):
                            for g4 in range(0, 18, 4):
                                ts_ = list(range(g4, min(g4 + 4, 18)))
                                for i, t in enumerate(ts_):
                                    for c in (2 * t, 2 * t + 1):
                                        MM(pA[:, i, :], ksp[d][:, c, :], qsp[d][:, c, :], c == 2 * t, c == 2 * t + 1,
                                           [r_ksp[d][c // 6], r_qsp[d][c // 6]], [r_pA])
                                n_ = len(ts_)
                                ka = (g4 // 4) % 2
                                CP("act", atmp[ka][:, 0:n_, :], pA[:, 0:n_, :], [r_pA], [r_at[ka]])
                                cm = -1 if d == 0 else 1
                                P.op("pool", lambda e, o=ATm[d][:, g4:g4 + n_, :], i_=atmp[ka][:, 0:n_, :], n_=n_, cm=cm:
                                     e.affine_select(out=o, in_=i_, pattern=[[0, n_], [-cm, 128]], compare_op=ALU.is_ge, fill=0.0,
                                                     base=0, channel_multiplier=cm), [r_at[ka]], [r_AT[d]])
                        for g4 in range(0, 18, 4):
                            ts_ = [t for t in range(g4, min(g4 + 4, 18))]
                            for i, t in enumerate(ts_):
                                if t not in ctiles:
                                    continue
                                mms = []
                                for d in range(2):
                                    mms.append((ATm[d][:, t, :], vtok[:, t, :], [r_AT[d], r_vt]))
                                    for c in (2 * t, 2 * t + 1):
                                        mms.append((qsp[d][:, c, :], Sbf[d][:, c, :], [r_qsp[d][c // 6], r_Sbf[d]]))
                                for mi, (lh, rh, rr) in enumerate(mms):
                                    MM(pA[:, i, :], lh, rh, mi == 0, mi == len(mms) - 1, rr, [r_pA])
                            for i, t in enumerate(ts_):
                                if t not in ctiles:
                                    continue
                                so = ost[:, i, :]
                                ACT(ojunk[:], pA[:, i, :], AF.Square, [r_pA], [r_oj, r_ost], accum=so[:, 0:1])
                                ACT(so[:, 1:2], so[:, 0:1], AF.Sqrt, [r_ost, r_const], [r_ost], scale=1.0 / 128, bias=epsT[:, 0:1])
                                P.op("dve", lambda e, so=so: e.reciprocal(out=so[:, 2:3], in_=so[:, 1:2]), [r_ost], [r_ost])
                                STT("dve", yb[i][:], pA[:, i, :], so[:, 2:3], gg[:, t, :], ALU.mult, ALU.mult, [r_pA, r_ost, r_gg],
                                    [r_yb[i]])
                            for i, t in enumerate(ts_):
                                if t not in ctiles:
                                    continue
                                TR(pTr[0][:, i, :], yb[i][:], identb[:], [r_yb[i], r_const], [r_pTr[0]])
                            live = [i for i, t in enumerate(ts_) if t in ctiles]
                            if live:
                                i0, i1 = live[0], live[-1] + 1
                                t0 = ts_[i0]
                                CP("act", yTh[:, t0 * 128:(t0 + i1 - i0) * 128].rearrange("p (i t) -> p i t", t=128), pTr[0][:, i0:i1, :],
                                   [r_pTr[0]], [r_yTh])
                        if stop == 80 and h == 0:
                            DUMP(17, yTh[:, 0:2048], 2048, [r_yTh], bf=True)
                            halt[0] = True
                            return
                        c_lo = ctiles[0] * 128
                        P.dma("sp", yT_d[h, :, c_lo:2304], yTh[:, c_lo:2304], reads=[r_yTh], writes=[r_yT[t] for t in ctiles])
                P.barrier()

            for l in range(n_layers):
                last = l == 3
                j = l // 2
                cur[0] = l
                load_mods(l)
                if CK(3):
                    return
                if l % 2 == 0:
                    phase_att(l, j, not last)
                    wo_src = wo_att[j]
                else:
                    phase_rec(l, j, not last)
                    wo_src = wo_rec[j]
                if CK(5):
                    return
                tiles = list(range(2, 18)) if last else list(range(18))
                if stop == 69:
                    tiles = [17]
                if stop == 70:
                    tiles = list(range(17, -1, -1))
                phase_wo(l, wo_src, tiles)
                if CK(6):
                    return
                if last:
                    groups = [[2, 3, 4, 5], list(range(6, 12)), list(range(12, 18))]
                else:
                    groups = [list(range(0, 6)), list(range(6, 12)), list(range(12, 18))]
                phase_mlp(l, groups, last)

            if n_layers == 4:
                with contextlib.ExitStack() as es:
                    fg = es.enter_context(SBT("fg", [128, 2048], F32))
                    r_fg = Res()
                    P.dma("sp", fg[:], fing_d.partition_broadcast(128), writes=[r_fg])
                    xf = [es.enter_context(SBT("xf%d" % i, [128, 2048], F32)) for i in range(2)]
                    fj = [es.enter_context(SBT("fj%d" % i, [128, 2048], BF16)) for i in range(2)]
                    fss = [es.enter_context(SBT("fss%d" % i, [128, 4], F32)) for i in range(2)]
                    r_xf = [Res(), Res()]
                    for t in range(2, 18):
                        k = t % 2
                        ss = fss[k]
                        P.dma("sp", xf[k][:], res[t * 128:(t + 1) * 128, :], reads=[r_res[t]], writes=[r_xf[k]])
                        ACT(fj[k][:], xf[k][:], AF.Square, [r_xf[k]], [r_xf[k]], accum=ss[:, 0:1])
                        ACT(ss[:, 1:2], ss[:, 0:1], AF.Sqrt, [r_xf[k], r_const], [r_xf[k]], scale=1.0 / 2048, bias=epsT[:, 0:1])
                        P.op("dve", lambda e, ss=ss: e.reciprocal(out=ss[:, 2:3], in_=ss[:, 1:2]), [r_xf[k]], [r_xf[k]])
                        STT("dve", xf[k][:], xf[k][:], ss[:, 2:3], fg[:], ALU.mult, ALU.mult, [r_xf[k], r_fg], [r_xf[k]])
                        P.dma("sp", out[(t - 2) * 128:(t - 1) * 128, :], xf[k][:], reads=[r_xf[k]], writes=[r_out])
        try:
            main_body()
        except _Stop:
            pass
        fin_reads = [r_out]
        if dbg and stop == 2:
            r_dbg = Res()
            P.dma("sp", dbg_o[0:128, 0:768], fmA[:].rearrange("p l j s -> p (l j s)"), reads=[r_fmA], writes=[r_dbg])
            fin_reads.append(r_dbg)
        elif dbg and dumped[0]:
            fin_reads.append(r_dump)
        elif dbg:
            r_dbg = Res()
            for t in range(18):
                P.dma("sp", dbg_o[t * 128:(t + 1) * 128, :], (xin if n_layers == 0 else res)[t * 128:(t + 1) * 128, :],
                      reads=[r_res[t]], writes=[r_dbg])
            fin_reads.append(r_dbg)
        P.op("sp", lambda e: e.nop(), fin_reads, ())
        P.emit(nc)
    return nc


_CONSTS = None


def make_in_maps(inputs, cores):
    global _CONSTS
    if _CONSTS is None:
        _CONSTS = make_consts()
    f = lambda a: np.ascontiguousarray(np.asarray(a, dtype=np.float32))
    x, c, ctx, c_ctx = f(inputs["x"]), f(inputs["c"]), f(inputs["ctx"]), f(inputs["c_ctx"])

    def fm(v):
        v = v.reshape(-1, 16, 128)
        return np.ascontiguousarray(v.transpose(2, 0, 1).reshape(128, -1))

    shared = {
        "w_ada": f(inputs["w_ada"]),
        "bada": np.ascontiguousarray(f(inputs["b_ada"]).reshape(4, 96, 128).transpose(2, 0, 1).reshape(128, 384)),
        "gmix": fm(f(inputs["norm_mix_g"])), "gmlp": fm(f(inputs["norm_mlp_g"])),
        "att_w_qkv": f(inputs["att_w_qkv"]), "att_w_o": f(inputs["att_w_o"]),
        "att_sink": f(inputs["att_sink"]).reshape(1, 32),
        "rec_w_in": f(inputs["rec_w_in"]), "rec_w_o": f(inputs["rec_w_o"]),
        "lbl": fm(f(inputs["rec_lb_logits"])), "rec_onorm_g": f(inputs["rec_onorm_g"]),
        "mlp_w_up": f(inputs["mlp_w_up"]), "mlp_w_down": f(inputs["mlp_w_down"]),
        "final_norm_g": f(inputs["final_norm_g"]).reshape(1, 2048),
    }
    shared.update(_CONSTS)
    maps = []
    for b in cores:
        m = dict(shared)
        m["xin"] = np.ascontiguousarray(np.concatenate([ctx[b], x[b]], axis=0))
        cc = np.stack([c[b], c_ctx], axis=0)
        m["ccl"] = np.ascontiguousarray(cc.reshape(2, 16, 128).transpose(2, 1, 0).reshape(128, 32))
        maps.append(m)
    return maps


def kernel(**inputs):
    nc = build(4, False)
    maps = make_in_maps(inputs, list(range(8)))
    r = run_bass_kernel_spmd(nc, maps, core_ids=list(range(8)))
    return np.stack([np.asarray(r.results[b]["out"], dtype=np.float32) for b in range(8)], axis=0)
```

```python
import contextlib
import numpy as np
import concourse.bass as bass
import concourse.mybir as mybir
from concourse.bass_utils import run_bass_kernel_spmd

F32 = mybir.dt.float32
BF16 = mybir.dt.bfloat16
AF = mybir.ActivationFunctionType
ALU = mybir.AluOpType
AX = mybir.AxisListType

ENGS = ("pe", "act", "dve", "pool", "sp")
DMA_SLOTS = 8


class Res:
    __slots__ = ("name", "last_w", "readers", "excl")

    def __init__(self, name="", excl=False):
        self.name = name
        self.last_w = None
        self.readers = []
        self.excl = excl


def PRes():
    return Res("psum", True)


class Op:
    __slots__ = ("eng", "fn", "deps", "needs_inc", "is_dma", "slot", "slot_n", "tok")

    def __init__(self, eng, fn):
        self.eng = eng
        self.fn = fn
        self.deps = []
        self.needs_inc = False
        self.is_dma = False
        self.slot = None
        self.slot_n = 0
        self.tok = None


class Prog:
    def __init__(self):
        self.ops = {e: [] for e in ENGS}
        self.dma_count = {e: 0 for e in ENGS}
        self.slot_last = {}
        self.barrier_deps = {e: [] for e in ENGS}

    def _add_dep(self, op, dep, kind):
        if dep is None or dep is op:
            return
        if not dep.is_dma and dep.eng == op.eng and not op.is_dma:
            if op.eng == "pe" or kind == "war" or kind == "excl":
                return
        for d in op.deps:
            if d is dep:
                return
        op.deps.append(dep)
        if not dep.is_dma:
            dep.needs_inc = True

    def _track(self, op, reads, writes):
        ex = [r for r in reads if r.excl] + [w for w in writes if w.excl]
        if ex:
            reads = [r for r in reads if not r.excl]
            writes = [w for w in writes if not w.excl]
            for x in ex:
                self._add_dep(op, x.last_w, "excl")
                x.last_w = op
        for r in reads:
            self._add_dep(op, r.last_w, "raw")
        for w in writes:
            self._add_dep(op, w.last_w, "waw")
            for rd in w.readers:
                self._add_dep(op, rd, "war")
        for r in reads:
            r.readers.append(op)
        for w in writes:
            w.last_w = op
            w.readers = []
        bd = self.barrier_deps[op.eng]
        if bd:
            for d in bd:
                if d is not op and not (d.eng == op.eng == "pe" and not d.is_dma and not op.is_dma):
                    if d not in op.deps:
                        op.deps.append(d)
                        if not d.is_dma:
                            d.needs_inc = True
            self.barrier_deps[op.eng] = []

    def op(self, eng, fn, reads=(), writes=()):
        o = Op(eng, fn)
        self._track(o, reads, writes)
        self.ops[eng].append(o)
        return o

    def dma(self, eng, out, in_, reads=(), writes=(), **kw):
        def fn(e):
            return e.dma_start(out=out, in_=in_, **kw)

        o = Op(eng, fn)
        o.is_dma = True
        n = self.dma_count[eng]
        self.dma_count[eng] = n + 1
        o.slot = n % DMA_SLOTS
        o.slot_n = n // DMA_SLOTS
        prev = self.slot_last.get((eng, o.slot))
        if prev is not None:
            o.deps.append(prev)
        self.slot_last[(eng, o.slot)] = o
        self._track(o, reads, writes)
        self.ops[eng].append(o)
        return o

    def barrier(self):
        lasts = []
        for e in ENGS:
            for o in reversed(self.ops[e]):
                if not o.is_dma:
                    lasts.append(o)
                    break
        for o in self.slot_last.values():
            lasts.append(o)
        for e in ENGS:
            self.barrier_deps[e] = list(lasts)

    def emit(self, nc):
        engmap = {"pe": "tensor", "act": "scalar", "dve": "vector", "pool": "gpsimd", "sp": "sync"}
        with contextlib.ExitStack() as es:
            sems = {e: es.enter_context(nc.semaphore("s_" + e)) for e in ENGS}
            dsems = {}
            for e in ENGS:
                if self.dma_count[e] > 0:
                    for s in range(DMA_SLOTS):
                        dsems[(e, s)] = es.enter_context(nc.semaphore("d_%s_%d" % (e, s)))
            for e in ENGS:
                cnt = 0
                for o in self.ops[e]:
                    if o.is_dma:
                        o.tok = (dsems[(e, o.slot)], 16 * (o.slot_n + 1))
                    elif o.needs_inc:
                        cnt += 1
                        o.tok = (sems[e], cnt)
            block = es.enter_context(nc.Block())

            def make(e):
                def body(eng):
                    seen = {}
                    for o in self.ops[e]:
                        for d in o.deps:
                            sem, val = d.tok
                            key = id(sem)
                            if seen.get(key, 0) < val:
                                eng.wait_ge(sem, val)
                                seen[key] = val
                        ins = o.fn(eng)
                        if o.is_dma:
                            ins.then_inc(o.tok[0], 16)
                        elif o.needs_inc:
                            ins.then_inc(o.tok[0], 1)

                return body

            for e in ENGS:
                if self.ops[e]:
                    getattr(block, engmap[e])(make(e))


NEG = -30000.0
SCALE = 128 ** -0.5


def make_consts():
    c = {}
    c["identf"] = np.eye(128, dtype=np.float32)
    rm = np.zeros((128, 128), np.float32)
    for m in range(128):
        a, p = (m // 64), (m % 64) // 32
        if p == 0:
            rm[m + 32, m] = -1.0
        else:
            rm[m - 32, m] = 1.0
    c["rotm"] = rm
    qi = np.arange(128)[:, None]
    kj = np.arange(128)[None, :]
    mb = np.zeros((3, 128, 384), np.float32)
    for v in range(3):
        mb[v, :, 0:128] = np.where(kj >= qi, 0.0, NEG)
        mb[v, :, 256:384] = np.where(kj <= qi, 0.0, NEG)
    mb[0, :, 0:128] = NEG
    mb[2, :, 256:384] = NEG
    c["amask"] = mb
    s = np.arange(128)[:, None]
    t = np.arange(128)[None, :]
    same = (s // 64) == (t // 64)
    rmask = np.zeros((2, 128, 128), np.float32)
    rmask[0] = (same & (s <= t)).astype(np.float32)
    rmask[1] = (same & (s >= t)).astype(np.float32)
    c["rmask"] = rmask
    pos = np.arange(2048)
    row = (pos // 64).astype(np.float32)
    col = (pos % 64).astype(np.float32)
    inv = (10000.0 ** (-np.arange(32, dtype=np.float32) / 32)).astype(np.float32)
    ar = row[:, None] * inv[None, :]
    ac = col[:, None] * inv[None, :]
    ang = np.concatenate([ar, ar, ac, ac], axis=-1).astype(np.float32)
    c["cosT"] = np.ascontiguousarray(np.cos(ang).T.astype(np.float32))
    c["sinT"] = np.ascontiguousarray(np.sin(ang).T.astype(np.float32))
    return c


class _Stop(Exception):
    pass


def build(n_layers=4, dbg=False, stop=0):
    nc = bass.Bass("TRN2", target_bir_lowering=False)
    P = Prog()

    _cnt = [0]

    def SBT(name, shape, dt):
        _cnt[0] += 1
        return nc.sbuf_tensor("%s_u%d" % (name, _cnt[0]), shape, dt)

    def PST(name, shape, dt):
        _cnt[0] += 1
        return nc.psum_tensor("%s_u%d" % (name, _cnt[0]), shape, dt)

    halt = [False]
    cur = [-1]

    def CK(n):
        if stop == n and (n <= 2 or cur[0] == n_layers - 1):
            halt[0] = True
        return halt[0]

    def din(name, shape, dt=F32):
        return nc.dram_tensor(name, list(shape), dt, kind="ExternalInput").ap()

    xin = din("xin", [2304, 2048])
    ccl = din("ccl", [128, 32])
    w_ada = din("w_ada", [4, 2048, 12288])
    bada_d = din("bada", [128, 384])
    gmix_d = din("gmix", [128, 64])
    gmlp_d = din("gmlp", [128, 64])
    wqkv = din("att_w_qkv", [2, 2048, 3072])
    wo_att = din("att_w_o", [2, 2048, 2048])
    sink_d = din("att_sink", [1, 32])
    win = din("rec_w_in", [2, 2048, 10240])
    wo_rec = din("rec_w_o", [2, 2048, 2048])
    lbl_d = din("lbl", [128, 64])
    onorm_d = din("rec_onorm_g", [2, 128])
    wup = din("mlp_w_up", [4, 2048, 8192])
    wdn = din("mlp_w_down", [4, 8192, 2048])
    fing_d = din("final_norm_g", [1, 2048])
    identf_d = din("identf", [128, 128])
    rotm_d = din("rotm", [128, 128])
    amask_d = din("amask", [3, 128, 384])
    rmask_d = din("rmask", [2, 128, 128])
    cosT_d = din("cosT", [128, 2048])
    sinT_d = din("sinT", [128, 2048])
    out = nc.dram_tensor("out", [2048, 2048], F32, kind="ExternalOutput").ap()
    res = nc.dram_tensor("res", [2304, 2048], F32).ap()
    yT_d = nc.dram_tensor("yT_d", [16, 128, 2304], BF16).ap()
    hT_d = nc.dram_tensor("hT_d", [16, 128, 2304], BF16).ap()
    if dbg:
        dbg_o = nc.dram_tensor("dbg", [2304, 2048], F32, kind="ExternalOutput").ap()

    r_res = [Res("res%d" % t) for t in range(18)]
    r_yT = [Res("yT%d" % t) for t in range(18)]
    r_hT = [Res("hT%d" % t) for t in range(18)]
    r_out = Res("out")

    r_dump = Res("dump")
    dumped = [False]

    def DUMP(i, ap, ncols, R, bf=False):
        if not dbg:
            return
        dumped[0] = True
        P.dma("pool" if bf else "sp", dbg_o[i * 128:(i + 1) * 128, 0:ncols], ap, reads=R, writes=[r_dump])

    def MM(o, lhsT, rhs, start, stop, R, W):
        P.op("pe", lambda e: e.matmul(o, lhsT=lhsT, rhs=rhs, start=start, stop=stop), R, W)

    def TR(o, i, ident, R, W):
        P.op("pe", lambda e: e.transpose(out=o, in_=i, identity=ident), R, W)

    def ACT(o, i, func, R, W, scale=1.0, bias=0.0, accum=None):
        if accum is None:
            P.op("act", lambda e: e.activation(out=o, in_=i, func=func, bias=bias, scale=scale), R, W)
        else:
            P.op("act", lambda e: e.activation(out=o, in_=i, func=func, bias=bias, scale=scale, accum_out=accum), R, W)

    def TS(eng, o, i, s1, s2, op0, op1, R, W):
        if s2 is None:
            P.op(eng, lambda e: e.tensor_scalar(out=o, in0=i, scalar1=s1, scalar2=None, op0=op0), R, W)
        else:
            P.op(eng, lambda e: e.tensor_scalar(out=o, in0=i, scalar1=s1, scalar2=s2, op0=op0, op1=op1), R, W)

    def TT(eng, o, a, b, op, R, W):
        P.op(eng, lambda e: e.tensor_tensor(out=o, in0=a, in1=b, op=op), R, W)

    def STT(eng, o, a, s, b, op0, op1, R, W):
        P.op(eng, lambda e: e.scalar_tensor_tensor(out=o, in0=a, scalar=s, in1=b, op0=op0, op1=op1), R, W)

    def CP(eng, o, i, R, W):
        if eng == "act":
            P.op("act", lambda e: e.copy(out=o, in_=i), R, W)
        else:
            P.op(eng, lambda e: e.tensor_copy(out=o, in_=i), R, W)

    def MS(eng, o, v, W):
        P.op(eng, lambda e: e.memset(o, v), (), W)

    with contextlib.ExitStack() as gs_:
        def gsb(name, shape, dt):
            return gs_.enter_context(SBT(name, shape, dt))

        identf = gsb("identf", [128, 128], F32)
        identb = gsb("identb", [128, 128], BF16)
        rotm = gsb("rotm", [128, 128], BF16)
        gmix = gsb("gmix", [128, 64], F32)
        gmlp = gsb("gmlp", [128, 64], F32)
        lbl = gsb("lbl", [128, 64], F32)
        lbt = gsb("lbt", [128, 3, 32], F32)
        epsT = gsb("epsT", [128, 1], F32)
        ones1 = gsb("ones1", [128, 1], F32)
        sinkr = gsb("sinkr", [128, 32], F32)
        fmA = gsb("fmA", [128, 4, 96, 2], F32)
        bada = gsb("bada", [128, 384], F32)
        r_fmA = Res("fmA")
        gsm = gsb("gsm", [128, 2, 2, 16], F32)
        r_const = Res("const")
        r_lbt = Res("lbt")
        r_fm = Res("fm")
        r_gate = Res("gate")

        P.dma("sp", identf[:], identf_d, writes=[r_const])
        P.dma("pool", identb[:], identf_d, writes=[r_const])
        P.dma("pool", rotm[:], rotm_d, writes=[r_const])
        P.dma("sp", gmix[:], gmix_d, writes=[r_const])
        P.dma("sp", gmlp[:], gmlp_d, writes=[r_const])
        P.dma("sp", lbl[:], lbl_d, writes=[r_const])
        P.dma("sp", bada[:], bada_d, writes=[r_const])
        P.dma("sp", sinkr[:], sink_d.partition_broadcast(128), writes=[r_const])
        MS("dve", epsT[:], 1e-6, [r_const])
        MS("dve", ones1[:], 1.0, [r_const])
        TS("dve", sinkr[:], sinkr[:], 1.0 / SCALE, None, ALU.mult, None, [r_const], [r_const])
        def main_body():
            if CK(1):
                return

            with contextlib.ExitStack() as ps_:
                def sb(name, shape, dt):
                    return ps_.enter_context(SBT(name, shape, dt))

                sil_f = sb("sil_f", [128, 32], F32)
                sil = sb("sil", [128, 16, 2], BF16)
                wb = [sb("wb%d" % i, [128, 16, 512], BF16) for i in range(3)]
                pm = ps_.enter_context(PST("pm", [128, 512], F32))
                r_sil = Res()
                r_wb = [Res() for _ in range(3)]
                r_pm = PRes()
                P.dma("sp", sil_f[:], ccl, writes=[r_sil])
                ACT(sil[:].rearrange("p c s -> p (c s)"), sil_f[:], AF.Silu, [r_sil], [r_sil])
                i = 0
                for l in range(n_layers):
                    for nb in range(24):
                        k = i % 3
                        i += 1
                        P.dma("pool", wb[k][:], w_ada[l, :, nb * 512:(nb + 1) * 512].rearrange("(c p) n -> p c n", p=128),
                              writes=[r_wb[k]])
                        for jj in range(4):
                            jc = nb * 4 + jj
                            for kc in range(16):
                                MM(pm[:, jc * 2:jc * 2 + 2], wb[k][:, kc, jj * 128:(jj + 1) * 128], sil[:, kc, :], kc == 0, kc == 15,
                                   [r_sil, r_wb[k]], [r_pm])
                    TT("dve", fmA[:, l, :, :], pm[:, 0:192].rearrange("p (j s) -> p j s", s=2),
                       bada[:, l * 96:(l + 1) * 96].rearrange("p (j o) -> p j o", o=1).to_broadcast([128, 96, 2]), ALU.add,
                       [r_pm, r_const], [r_fmA])
            P.barrier()
            if CK(2):
                return

            def stream_of(t):
                return 1 if t < 2 else 0

            def src_tile(l, t, mixer=True):
                s = xin if (l == 0 and mixer) else res
                return s[t * 128:(t + 1) * 128, :]

            class NormCtx:
                def __init__(self, es, tag):
                    def sb(name, shape, dt):
                        return es.enter_context(SBT(tag + name, shape, dt))

                    self.xt = [sb("xt%d" % i, [128, 2048], F32) for i in range(2)]
                    self.xn = [sb("xn%d" % i, [128, 2048], BF16) for i in range(2)]
                    self.ss = [sb("ss%d" % i, [128, 4], F32) for i in range(2)]
                    self.pT = es.enter_context(PST(tag + "pT", [128, 16, 128], BF16))
                    self.r_xt = [Res(), Res()]
                    self.r_xn = [Res(), Res()]
                    self.r_ss = [Res(), Res()]
                    self.r_pT = PRes()
                    self.n = 0

            def norm_tile(N, l, t, which, dst, r_dst):
                mixer = which == 0
                k = N.n % 2
                N.n += 1
                s = stream_of(t)
                xt, xn, ss = N.xt[k], N.xn[k], N.ss[k]
                P.dma("sp", xt[:], src_tile(l, t, mixer), reads=[r_res[t]], writes=[N.r_xt[k]])
                ACT(xn[:], xt[:], AF.Square, [N.r_xt[k]], [N.r_xn[k], N.r_ss[k]], accum=ss[:, 0:1])
                ACT(ss[:, 1:2], ss[:, 0:1], AF.Sqrt, [N.r_ss[k], r_const], [N.r_ss[k]], scale=1.0 / 2048, bias=epsT[:, 0:1])
                P.op("dve", lambda e: e.reciprocal(out=ss[:, 2:3], in_=ss[:, 1:2]), [N.r_ss[k]], [N.r_ss[k]])
                TS("dve", xn[:], xt[:], ss[:, 2:3], None, ALU.mult, None, [N.r_xt[k], N.r_ss[k]], [N.r_xn[k]])
                for kc in range(16):
                    TR(N.pT[:, kc, :], xn[:, kc * 128:(kc + 1) * 128], identb[:], [N.r_xn[k], r_const], [N.r_pT])
                sh = 0 if which == 0 else 48
                for kc in range(16):
                    sc_ap = gsm[:, s, which, kc:kc + 1]
                    bi_ap = fmA[:, l, sh + kc, s:s + 1]
                    if kc % 2 == 0:
                        ACT(dst[:, kc, :], N.pT[:, kc, :], AF.Identity, [N.r_pT, r_fm, r_fmA], [r_dst], scale=sc_ap, bias=bi_ap)
                    else:
                        TS("dve", dst[:, kc, :], N.pT[:, kc, :], sc_ap, bi_ap, ALU.mult, ALU.add, [N.r_pT, r_fm, r_fmA], [r_dst])

            def load_mods(l):
                for s in range(2):
                    STT("dve", gsm[:, s, 0, :], fmA[:, l, 16:32, s], 1.0, gmix[:, l * 16:(l + 1) * 16], ALU.add, ALU.mult,
                        [r_fmA, r_const], [r_fm])
                    STT("dve", gsm[:, s, 1, :], fmA[:, l, 64:80, s], 1.0, gmlp[:, l * 16:(l + 1) * 16], ALU.add, ALU.mult,
                        [r_fmA, r_const], [r_fm])

            def load_gate(es, l, m):
                gate_bc = [es.enter_context(SBT("gate%d" % s, [128, 2048], F32)) for s in range(2)]
                with contextlib.ExitStack() as eg:
                    rep = [eg.enter_context(SBT("rep%d" % i, [128, 128], F32)) for i in range(2)]
                    pg = eg.enter_context(PST("pg", [128, 512], F32))
                    r_rep = [Res(), Res()]
                    r_pg = PRes()
                    n = 0
                    for s in range(2):
                        for c4 in range(4):
                            for ci in range(4):
                                c = c4 * 4 + ci
                                k = n % 2
                                n += 1
                                CP("dve", rep[k][:], fmA[:, l, m * 16 + c, s:s + 1].to_broadcast([128, 128]), [r_fmA], [r_rep[k]])
                                MM(pg[:, ci * 128:(ci + 1) * 128], rep[k][:], identf[:], True, True, [r_rep[k], r_const], [r_pg])
                            CP("act", gate_bc[s][:, c4 * 512:(c4 + 1) * 512], pg[:], [r_pg], [r_gate])
                P.barrier()
                return gate_bc

            def phase_wo(l, wo_src, tiles):
                with contextlib.ExitStack() as es:
                    gate_bc = load_gate(es, l, 2)
                    if CK(60):
                        return
                    def sb(name, shape, dt):
                        return es.enter_context(SBT(name, shape, dt))

                    wo = sb("wo", [128, 16, 2048], BF16)
                    yt = [sb("yt%d" % i, [128, 16, 128], BF16) for i in range(2)]
                    xt = [sb("wxt%d" % i, [128, 2048], F32) for i in range(2)]
                    tmp = [sb("wtmp%d" % i, [128, 512], F32) for i in range(2)]
                    acc = [es.enter_context(PST("woacc%d" % i, [128, 512], F32)) for i in range(4)]
                    r_wo = [Res() for _ in range(4)]
                    r_yt, r_xt, r_tmp = [Res(), Res()], [Res(), Res()], [Res(), Res()]
                    r_acc = [PRes() for _ in range(4)]
                    for nb in range(4):
                        P.dma("pool", wo[:, :, nb * 512:(nb + 1) * 512],
                              wo_src[:, nb * 512:(nb + 1) * 512].rearrange("(c p) n -> p c n", p=128), writes=[r_wo[nb]])
                    if CK(61):
                        return
                    n = 0
                    for it, t in enumerate(tiles):
                        if it == 1 and CK(62):
                            return
                        if it == 2 and CK(63):
                            return
                        if it == 3 and CK(64):
                            return
                        if it == 8 and CK(65):
                            return
                        if it == 13 and CK(66):
                            return
                        if it == 17 and CK(67):
                            return
                        k = it % 2
                        s = stream_of(t)
                        P.dma("sp", yt[k][:], yT_d[:, :, t * 128:(t + 1) * 128].rearrange("c p t -> p c t"),
                              reads=[r_yT[t]], writes=[r_yt[k]])
                        P.dma("sp", xt[k][:], src_tile(l, t), reads=[r_res[t]], writes=[r_xt[k]])
                        for nb in range(4):
                            a = acc[n % 4]
                            ra = r_acc[n % 4]
                            for kc in range(16):
                                MM(a[:], yt[k][:, kc, :], wo[:, kc, nb * 512:(nb + 1) * 512], kc == 0, kc == 15,
                                   [r_yt[k], r_wo[nb]], [ra])
                            tm = tmp[n % 2]
                            TT("dve", tm[:], a[:], gate_bc[s][:, nb * 512:(nb + 1) * 512], ALU.mult, [ra, r_gate], [r_tmp[n % 2]])
                            TT("pool", xt[k][:, nb * 512:(nb + 1) * 512], tm[:], xt[k][:, nb * 512:(nb + 1) * 512], ALU.add,
                               [r_tmp[n % 2], r_xt[k]], [r_xt[k]])
                            n += 1
                        P.dma("sp", res[t * 128:(t + 1) * 128, :], xt[k][:], reads=[r_xt[k]], writes=[r_res[t]])
                    if CK(68) or CK(69) or CK(70):
                        return
                P.barrier()

            def phase_mlp(l, groups, last):
                with contextlib.ExitStack() as es:
                    gate_bc = load_gate(es, l, 5)
                    def sb(name, shape, dt):
                        return es.enter_context(SBT(name, shape, dt))

                    aT = sb("aT", [128, 64, 768], BF16)
                    r_aT = [Res() for _ in range(64)]
                    wu_i = 0
                    for g, tiles in enumerate(groups):
                        T = len(tiles) * 128
                        halves = [(0, T // 2), (T // 2, T)]
                        with contextlib.ExitStack() as es2:
                            def sb2(name, shape, dt):
                                return es2.enter_context(SBT(name, shape, dt))

                            hT = sb2("hT", [128, 16, 768], BF16)
                            r_h = Res()
                            wu = [sb2("wu%d" % i, [128, 16, 256], BF16) for i in range(2)]
                            r_wu = [Res(), Res()]
                            rl = [sb2("rl%d" % i, [128, 384], BF16) for i in range(2)]
                            r_rl = [Res(), Res()]
                            N = NormCtx(es2, "m")
                            up = [es2.enter_context(PST("up%d" % i, [128, 512], F32)) for i in range(4)]
                            r_up = [PRes() for _ in range(4)]
                            for it, t in enumerate(tiles):
                                norm_tile(N, l, t, 1, hT[:, :, it * 128:(it + 1) * 128], r_h)
                            nu = 0
                            for fb in range(32):
                                k = fb % 2
                                P.dma("pool", wu[k][:], wup[l, :, fb * 256:(fb + 1) * 256].rearrange("(c p) n -> p c n", p=128),
                                      writes=[r_wu[k]])
                                for fi in range(2):
                                    fc = fb * 2 + fi
                                    pa = [up[(nu * 2) % 4], up[(nu * 2 + 1) % 4]]
                                    rp = [r_up[(nu * 2) % 4], r_up[(nu * 2 + 1) % 4]]
                                    for kc in range(16):
                                        for hi, (c0, c1) in enumerate(halves):
                                            MM(pa[hi][:, 0:c1 - c0], wu[k][:, kc, fi * 128:(fi + 1) * 128], hT[:, kc, c0:c1],
                                               kc == 0, kc == 15, [r_wu[k], r_h], [rp[hi]])
                                    for hi, (c0, c1) in enumerate(halves):
                                        r_ = rl[(nu * 2 + hi) % 2]
                                        rr = r_rl[(nu * 2 + hi) % 2]
                                        ACT(r_[:, 0:c1 - c0], pa[hi][:, 0:c1 - c0], AF.Relu, [rp[hi]], [rr])
                                        TT("dve", aT[:, fc, c0:c1], r_[:, 0:c1 - c0], r_[:, 0:c1 - c0], ALU.mult, [rr], [r_aT[fc]])
                                    nu += 1
                        P.barrier()
                        with contextlib.ExitStack() as es2:
                            def sb2(name, shape, dt):
                                return es2.enter_context(SBT(name, shape, dt))

                            wd = [sb2("wd%d" % i, [128, 8, 512], BF16) for i in range(3)]
                            r_wd = [Res() for _ in range(3)]
                            xs = [sb2("xs%d" % i, [128, 512], F32) for i in range(6)]
                            r_xs = [Res() for _ in range(6)]
                            tm = [sb2("dtm%d" % i, [128, 512], F32) for i in range(2)]
                            r_tm = [Res(), Res()]
                            acc = [es2.enter_context(PST("dacc%d" % i, [128, 512], F32)) for i in range(6)]
                            r_acc = [PRes() for _ in range(6)]
                            wi = 0
                            ne = 0
                            for nb in range(4):
                                for it, t in enumerate(tiles):
                                    P.dma("sp", xs[it][:], src_tile(l, t, False)[:, nb * 512:(nb + 1) * 512], reads=[r_res[t]],
                                          writes=[r_xs[it]])
                                for fb in range(8):
                                    k = wi % 3
                                    wi += 1
                                    P.dma("pool", wd[k][:],
                                          wdn[l, fb * 1024:(fb + 1) * 1024, nb * 512:(nb + 1) * 512].rearrange("(c p) n -> p c n", p=128),
                                          writes=[r_wd[k]])
                                    for fi in range(8):
                                        fc = fb * 8 + fi
                                        for it in range(len(tiles)):
                                            MM(acc[it][:], aT[:, fc, it * 128:(it + 1) * 128], wd[k][:, fi, :], fc == 0, fc == 63,
                                               [r_aT[fc], r_wd[k]], [r_acc[it]])
                                for it, t in enumerate(tiles):
                                    s = stream_of(t)
                                    tt_ = tm[ne % 2]
                                    rt = r_tm[ne % 2]
                                    ne += 1
                                    TT("dve", tt_[:], acc[it][:], gate_bc[s][:, nb * 512:(nb + 1) * 512], ALU.mult,
                                       [r_acc[it], r_gate], [rt])
                                    TT("pool", xs[it][:], tt_[:], xs[it][:], ALU.add, [rt, r_xs[it]], [r_xs[it]])
                                    P.dma("sp", res[t * 128:(t + 1) * 128, nb * 512:(nb + 1) * 512], xs[it][:],
                                          reads=[r_xs[it]], writes=[r_res[t]])
                        P.barrier()

            def phase_att(l, j, need_ctx):
                with contextlib.ExitStack() as es:
                    def sb(name, shape, dt):
                        return es.enter_context(SBT(name, shape, dt))

                    qT = sb("qT", [128, 16, 2304], BF16)
                    kT = sb("kT", [128, 4, 2560], BF16)
                    V = sb("V", [128, 20, 512], BF16)
                    r_q = [Res() for _ in range(18)]
                    r_k = [Res() for _ in range(20)]
                    r_v = [Res() for _ in range(20)]
                    MS("pool", kT[:, :, 256:384], 0.0, [r_k[2]])
                    MS("pool", kT[:, :, 2432:2560], 0.0, [r_k[19]])
                    MS("pool", V[:, 2, :], 0.0, [r_v[2]])
                    MS("pool", V[:, 19, :], 0.0, [r_v[19]])

                    def kslot(t):
                        return t if t < 2 else t + 1

                    with contextlib.ExitStack() as es2:
                        def sb2(name, shape, dt):
                            return es2.enter_context(SBT(name, shape, dt))

                        hT = sb2("ahT", [128, 16, 768], BF16)
                        wq = [sb2("wq%d" % i, [128, 16, 256], BF16) for i in range(2)]
                        r_wq = [Res(), Res()]
                        cosT = sb2("cosT", [128, 768], F32)
                        sinT = sb2("sinT", [128, 768], F32)
                        r_tab = Res()
                        qb = [sb2("qb%d" % i, [128, 384], BF16) for i in range(2)]
                        t1 = [sb2("t1%d" % i, [128, 384], F32) for i in range(2)]
                        t2 = [sb2("t2%d" % i, [128, 384], F32) for i in range(2)]
                        r_qb, r_t1, r_t2 = [Res(), Res()], [Res(), Res()], [Res(), Res()]
                        N = NormCtx(es2, "a")
                        pq = [es2.enter_context(PST("pq%d" % i, [128, 512], F32)) for i in range(2)]
                        pr = [es2.enter_context(PST("pr%d" % i, [128, 512], F32)) for i in range(2)]
                        pv = [es2.enter_context(PST("pv%d" % i, [128, 512], F32)) for i in range(2)]
                        r_pq, r_pr, r_pv = [PRes(), PRes()], [PRes(), PRes()], [PRes(), PRes()]
                        wi = 0
                        nq = 0
                        nv = 0
                        for g in range(3):
                            tiles = list(range(g * 6, g * 6 + 6))
                            r_h = Res()
                            if CK(40):
                                return
                            for it, t in enumerate(tiles):
                                norm_tile(N, l, t, 0, hT[:, :, it * 128:(it + 1) * 128], r_h)
                                if CK(41):
                                    return
                            if CK(42):
                                return
                            lat0 = g * 768 - 256
                            p0 = max(lat0, 0)
                            ncols = 768 - (p0 - lat0)
                            P.dma("sp", cosT[:, p0 - lat0:768], cosT_d[:, p0:p0 + ncols], writes=[r_tab])
                            P.dma("sp", sinT[:, p0 - lat0:768], sinT_d[:, p0:p0 + ncols], writes=[r_tab])
                            for blk in range(12):
                                if blk == 1 and CK(43):
                                    return
                                if blk == 9 and CK(44):
                                    return
                                if blk == 11 and CK(45):
                                    return
                                k = wi % 2
                                wi += 1
                                P.dma("pool", wq[k][:], wqkv[j, :, blk * 256:(blk + 1) * 256].rearrange("(c p) n -> p c n", p=128),
                                      writes=[r_wq[k]])
                                if blk < 10:
                                    for hh in range(2):
                                        for half in range(2):
                                            c0 = half * 384
                                            a = nq % 2
                                            nq += 1
                                            for kc in range(16):
                                                MM(pq[a][:, 0:384], wq[k][:, kc, hh * 128:(hh + 1) * 128], hT[:, kc, c0:c0 + 384],
                                                   kc == 0, kc == 15, [r_wq[k], r_h], [r_pq[a]])
                                            if CK(46):
                                                return
                                            if blk < 8:
                                                head = blk * 2 + hh

                                                def dst(ca, cb, head=head, g=g):
                                                    return qT[:, head, g * 768 + ca:g * 768 + cb]
                                                wres = [r_q[t] for t in tiles[half * 3:half * 3 + 3]]
                                            else:
                                                kvh = (blk - 8) * 2 + hh

                                                def dst(ca, cb, kvh=kvh, g=g):
                                                    sa = g * 768 + ca
                                                    off = 0 if sa < 256 else 128
                                                    return kT[:, kvh, sa + off:sa + off + (cb - ca)]
                                                wres = [r_k[kslot(t)] for t in tiles[half * 3:half * 3 + 3]]
                                            if g == 0 and half == 0:
                                                segs = [(0, 256, False), (256, 384, True)]
                                            else:
                                                segs = [(c0, c0 + 384, True)]
                                            for (ca, cb, rope) in segs:
                                                la, lb_ = ca - c0, cb - c0
                                                if not rope:
                                                    CP("act", dst(ca, cb), pq[a][:, la:lb_], [r_pq[a]], wres)
                                                    if CK(47):
                                                        return
                                                else:
                                                    CP("act", qb[a][:, la:lb_], pq[a][:, la:lb_], [r_pq[a]], [r_qb[a]])
                                                    if CK(49):
                                                        return
                                                    MM(pr[a][:, la:lb_], rotm[:], qb[a][:, la:lb_], True, True, [r_qb[a], r_const],
                                                       [r_pr[a]])
                                                    if CK(50):
                                                        return
                                                    TT("dve", t1[a][:, la:lb_], pq[a][:, la:lb_], cosT[:, ca:cb], ALU.mult,
                                                       [r_pq[a], r_tab], [r_t1[a]])
                                                    if CK(51):
                                                        return
                                                    TT("dve", t2[a][:, la:lb_], pr[a][:, la:lb_], sinT[:, ca:cb], ALU.mult,
                                                       [r_pr[a], r_tab], [r_t2[a]])
                                                    if CK(48):
                                                        return
                                                    TT("pool", dst(ca, cb), t1[a][:, la:lb_], t2[a][:, la:lb_], ALU.add,
                                                       [r_t1[a], r_t2[a]], wres)
                                else:
                                    vh = blk - 10
                                    for it, t in enumerate(tiles):
                                        a = nv % 2
                                        nv += 1
                                        for kc in range(16):
                                            MM(pv[a][:, 0:256], hT[:, kc, it * 128:(it + 1) * 128], wq[k][:, kc, :], kc == 0, kc == 15,
                                               [r_wq[k], r_h], [r_pv[a]])
                                        CP("act", V[:, kslot(t), vh * 256:(vh + 1) * 256], pv[a][:, 0:256], [r_pv[a]], [r_v[kslot(t)]])
                    P.barrier()
                    if CK(4):
                        return
                    with contextlib.ExitStack() as es2:
                        def sb2(name, shape, dt):
                            return es2.enter_context(SBT(name, shape, dt))

                        amask = sb2("amask", [128, 3, 384], BF16)
                        r_am = Res()
                        P.dma("pool", amask[:], amask_d.rearrange("v q k -> q v k"), writes=[r_am])
                        pb = [sb2("pb%d" % i, [128, 640], BF16) for i in range(3)]
                        pTs = [sb2("pTs%d" % i, [128, 5, 128], BF16) for i in range(2)]
                        st = [sb2("ast%d" % i, [128, 8], F32) for i in range(3)]
                        oTs = [sb2("oTs%d" % i, [128, 16, 128], BF16) for i in range(2)]
                        r_pb, r_pTs, r_st, r_oTs = [Res(), Res(), Res()], [Res(), Res()], [Res(), Res(), Res()], [Res(), Res()]
                        S = [es2.enter_context(PST("S%d" % i, [128, 1024], F32)) for i in range(2)]
                        pTp = [es2.enter_context(PST("pTp%d" % i, [128, 8, 128], BF16)) for i in range(2)]
                        oTp = [es2.enter_context(PST("oTp%d" % i, [128, 4, 128], F32)) for i in range(2)]
                        r_S, r_pTp, r_oTp = [PRes(), PRes()], [PRes(), PRes()], [PRes(), PRes()]
                        qtiles = list(range(18)) if need_ctx else list(range(2, 18))
                        units = [(iq, t, h) for iq, t in enumerate(qtiles) for h in range(16)]

                        def stage_a(u):
                            iq, t, h = units[u]
                            kv = h // 4
                            a = u % 2
                            a3 = u % 3
                            qap = qT[:, h, t * 128:(t + 1) * 128]
                            if t < 2:
                                lo, hi = 512, 768
                                MM(S[a][:, 512:768], qap, kT[:, kv, 0:256], True, True, [r_q[t], r_k[0], r_k[1]], [r_S[a]])
                            else:
                                b = t - 2
                                lo, hi = 128, 768
                                kc0 = 256 + b * 128
                                var = 0 if b == 0 else (2 if b == 15 else 1)
                                MM(S[a][:, 128:512], qap, kT[:, kv, kc0:kc0 + 384], True, False,
                                   [r_q[t], r_k[2 + b], r_k[3 + b], r_k[4 + b]], [r_S[a]])
                                MM(S[a][:, 128:512], identb[:], amask[:, var, :], False, True, [r_am, r_const], [r_S[a]])
                                MM(S[a][:, 512:768], qap, kT[:, kv, 0:256], True, True, [r_q[t], r_k[0], r_k[1]], [r_S[a]])
                            n = hi - lo
                            sa = st[a3]
                            P.op("dve", lambda e, sa=sa, Sa=S[a], lo=lo, hi=hi: e.reduce_max(out=sa[:, 0:1], in_=Sa[:, lo:hi], axis=AX.X),
                                 [r_S[a]], [r_st[a3]])
                            TS("dve", sa[:, 1:2], sa[:, 0:1], sinkr[:, j * 16 + h:j * 16 + h + 1], -SCALE, ALU.max, ALU.mult,
                               [r_st[a3], r_const], [r_st[a3]])
                            ACT(pb[a3][:, 0:n], S[a][:, lo:hi], AF.Exp, [r_S[a], r_st[a3]], [r_pb[a3], r_st[a3]], scale=SCALE,
                                bias=sa[:, 1:2], accum=sa[:, 2:3])
                            ACT(sa[:, 3:4], sinkr[:, j * 16 + h:j * 16 + h + 1], AF.Exp, [r_st[a3], r_const], [r_st[a3]], scale=SCALE,
                                bias=sa[:, 1:2])

                        def stage_a2(u):
                            iq, t, h = units[u]
                            a3 = u % 3
                            n = 256 if t < 2 else 640
                            sa = st[a3]
                            TT("dve", sa[:, 4:5], sa[:, 2:3], sa[:, 3:4], ALU.add, [r_st[a3]], [r_st[a3]])
                            P.op("dve", lambda e, sa=sa: e.reciprocal(out=sa[:, 5:6], in_=sa[:, 4:5]), [r_st[a3]], [r_st[a3]])
                            TS("dve", pb[a3][:, 0:n], pb[a3][:, 0:n], sa[:, 5:6], None, ALU.mult, None, [r_pb[a3], r_st[a3]], [r_pb[a3]])

                        def stage_b(u):
                            iq, t, h = units[u]
                            kv = h // 4
                            a = u % 2
                            a3 = u % 3
                            ob = iq % 2
                            if t < 2:
                                kslots = [0, 1]
                            else:
                                b = t - 2
                                kslots = [2 + b, 3 + b, 4 + b, 0, 1]
                            nk = len(kslots)
                            for jk in range(nk):
                                TR(pTp[a][:, jk, :], pb[a3][:, jk * 128:(jk + 1) * 128], identb[:], [r_pb[a3], r_const], [r_pTp[a]])
                            CP("act", pTs[a][:, 0:nk, :], pTp[a][:, 0:nk, :], [r_pTp[a]], [r_pTs[a]])
                            oa = (u // 4) % 2
                            hh = h % 4
                            for jk in range(nk):
                                MM(oTp[oa][:, hh, :], V[:, kslots[jk], kv * 128:(kv + 1) * 128], pTs[a][:, jk, :], jk == 0,
                                   jk == nk - 1, [r_pTs[a], r_v[kslots[jk]]], [r_oTp[oa]])
                            if hh == 3:
                                CP("dve", oTs[ob][:, h - 3:h + 1, :], oTp[oa][:], [r_oTp[oa]], [r_oTs[ob]])
                            if h == 15:
                                P.dma("sp", yT_d[:, :, t * 128:(t + 1) * 128].rearrange("c p t -> p c t"), oTs[ob][:],
                                      reads=[r_oTs[ob]], writes=[r_yT[t]])

                        nu_ = len(units)
                        for u in range(nu_ + 2):
                            if u < nu_:
                                stage_a(u)
                            if 1 <= u <= nu_:
                                stage_a2(u - 1)
                            if u >= 2:
                                stage_b(u - 2)
                P.barrier()

            def phase_rec(l, j, need_ctx):
                if j == 0:
                    MS("dve", lbt[:, 0, :], 0.0, [r_lbt])
                else:
                    TT("dve", lbt[:, 0, :], lbl[:, 32:64], lbl[:, 0:32], ALU.subtract, [r_const], [r_lbt])
                    ACT(lbt[:, 0, :], lbt[:, 0, :], AF.Sigmoid, [r_lbt], [r_lbt])
                TS("dve", lbt[:, 1, :], lbt[:, 0, :], -1.0, 1.0, ALU.mult, ALU.add, [r_lbt], [r_lbt])
                TS("dve", lbt[:, 2, :], lbt[:, 1, :], -1.0, None, ALU.mult, None, [r_lbt], [r_lbt])
                with contextlib.ExitStack() as es:
                    N = NormCtx(es, "r")
                    hs = [es.enter_context(SBT("hs%d" % i, [128, 16, 128], BF16)) for i in range(2)]
                    r_hs = [Res(), Res()]
                    for t in range(18):
                        k = t % 2
                        norm_tile(N, l, t, 0, hs[k][:], r_hs[k])
                        P.dma("sp", hT_d[:, :, t * 128:(t + 1) * 128].rearrange("c p t -> p c t"), hs[k][:], reads=[r_hs[k]],
                              writes=[r_hT[t]])
                P.barrier()
                with contextlib.ExitStack() as es:
                    def sb(name, shape, dt):
                        return es.enter_context(SBT(name, shape, dt))

                    def ps(name, shape, dt):
                        return es.enter_context(PST(name, shape, dt))

                    wh = [sb("wh%d" % i, [128, 16, 5, 128], BF16) for i in range(2)]
                    r_wh = [Res(), Res()]
                    hp = [sb("hp%d" % i, [128, 16, 384], BF16) for i in range(2)]
                    r_hp = [Res(), Res()]
                    rmask = sb("rmask", [128, 2, 128], BF16)
                    onb = sb("onb", [128, 128], F32)
                    r_rc = Res()
                    P.dma("pool", rmask[:], rmask_d.rearrange("v s t -> s v t"), writes=[r_rc])
                    P.dma("sp", onb[:], onorm_d[j:j + 1, :].partition_broadcast(128), writes=[r_rc])
                    qsp = [sb("qsp%d" % d, [128, 36, 128], BF16) for d in range(2)]
                    ksp = [sb("ksp%d" % d, [128, 36, 128], BF16) for d in range(2)]
                    r_qsp = [[Res() for _ in range(6)] for _ in range(2)]
                    r_ksp = [[Res() for _ in range(6)] for _ in range(2)]
                    for d in range(2):
                        MS("pool", qsp[d][:], 0.0, r_qsp[d])
                        MS("pool", ksp[d][:], 0.0, r_ksp[d])
                    Dd = [sb("Dd%d" % d, [128, 36], F32) for d in range(2)]
                    Hm = [sb("Hm%d" % d, [128, 36], F32) for d in range(2)]
                    Gm = [sb("Gm%d" % d, [128, 36], F32) for d in range(2)]
                    utmp = [[sb("utmp%d_%d" % (d, i), [128, 128], F32) for i in range(2)] for d in range(2)]
                    r_ut = [[Res(), Res()], [Res(), Res()]]
                    atmp = [sb("atmp%d" % i, [128, 4, 128], BF16) for i in range(2)]
                    r_at = [Res(), Res()]
                    Sbf = [sb("Sbf%d" % d, [128, 36, 128], BF16) for d in range(2)]
                    ATm = [sb("ATm%d" % d, [128, 18, 128], BF16) for d in range(2)]
                    vtok = sb("vtok", [128, 18, 128], BF16)
                    gg = sb("gg", [128, 18, 128], F32)
                    yTh = sb("yTh", [128, 2304], BF16)
                    Tst = [[sb("Tst%d_%d" % (d, i), [128, 128], F32) for i in range(2)] for d in range(2)]
                    ktok = [sb("ktok%d" % i, [128, 4, 128], BF16) for i in range(2)]
                    r_ktok = [Res(), Res()]
                    r_D, r_Sbf, r_AT = [Res(), Res()], [Res(), Res()], [Res(), Res()]
                    r_T = [[Res(), Res()], [Res(), Res()]]
                    r_vt, r_gg, r_yTh = Res(), Res(), Res()
                    qs2 = [sb("qs%d" % i, [128, 384], F32) for i in range(2)]
                    tF2 = [[sb("tF%d_%d" % (i, d), [128, 384], F32) for d in range(2)] for i in range(2)]
                    tK2 = [[sb("tK%d_%d" % (i, d), [128, 384], F32) for d in range(2)] for i in range(2)]
                    Bz2 = [[sb("Bz%d_%d" % (i, d), [128, 385], F32) for d in range(2)] for i in range(2)]
                    tE2 = [[sb("tE%d_%d" % (i, d), [128, 384], F32) for d in range(2)] for i in range(2)]
                    tX2 = [[sb("tX%d_%d" % (i, d), [128, 384], F32) for d in range(2)] for i in range(2)]
                    dD2 = [[sb("dD%d_%d" % (i, d), [128, 18], F32) for d in range(2)] for i in range(2)]
                    r_qs2 = [Res(), Res()]
                    r_tF2, r_tK2, r_Bz2, r_tE2, r_tX2, r_dD2 = ([[Res(), Res()], [Res(), Res()]] for _ in range(6))
                    for i in range(2):
                        for d in range(2):
                            MS("dve", Bz2[i][d][:, 0:1], 0.0, [r_Bz2[i][d]])
                    ost = sb("ost", [128, 4, 4], F32)
                    ojunk = sb("ojunk", [128, 128], BF16)
                    yb = [sb("yb%d" % i, [128, 128], BF16) for i in range(4)]
                    r_ost, r_oj = Res(), Res()
                    r_yb = [Res() for _ in range(4)]
                    zq_ = ps("zq", [128, 512], F32)
                    zq = zq_[:, 0:384]
                    zf_ = [ps("zf%d" % d, [128, 512], F32) for d in range(2)]
                    zf = [z[:, 0:384] for z in zf_]
                    vg = ps("vg", [128, 2, 256], F32)
                    pA = ps("pA", [128, 4, 128], F32)
                    pTr_ = ps("pTr", [128, 2, 4, 128], BF16)
                    pTr = [pTr_[:, i] for i in range(2)]
                    pU = [ps("pU%d" % i, [128, 4, 128], F32) for i in range(2)]
                    r_zq, r_pA = PRes(), PRes()
                    _rp = PRes()
                    r_pTr = [_rp, _rp]
                    r_zf = [PRes(), PRes()]
                    _rv = PRes()
                    r_vg = [_rv, _rv]
                    r_pU = [PRes(), PRes()]

                    nwh = 0
                    nhp = 0
                    nvg = 0
                    for h in range(16):
                        k = nwh % 2
                        nwh += 1
                        for si, part in enumerate((0, 2, 3, 1, 4)):
                            c0 = part * 2048 + h * 128
                            P.dma("pool", wh[k][:, :, si, :], win[j, :, c0:c0 + 128].rearrange("(c p) n -> p c n", p=128),
                                  writes=[r_wh[k]])
                        for pc in range(6):
                            kh = nhp % 2
                            nhp += 1
                            tiles = [pc * 3 + i for i in range(3)]
                            P.dma("sp", hp[kh][:], hT_d[:, :, pc * 384:(pc + 1) * 384].rearrange("c p t -> p c t"),
                                  reads=[r_hT[t] for t in tiles], writes=[r_hp[kh]])
                            for kc in range(16):
                                MM(zq, wh[k][:, kc, 0, :], hp[kh][:, kc, :], kc == 0, kc == 15, [r_wh[k], r_hp[kh]], [r_zq])
                            for d in range(2):
                                for kc in range(16):
                                    MM(zf[d], wh[k][:, kc, 1 + d, :], hp[kh][:, kc, :], kc == 0, kc == 15, [r_wh[k], r_hp[kh]],
                                       [r_zf[d]])
                            for i, t in enumerate(tiles):
                                a = nvg % 2
                                nvg += 1
                                for kc in range(16):
                                    MM(vg[:, a, :], hp[kh][:, kc, i * 128:(i + 1) * 128], wh[k][:, kc, 3:5, :], kc == 0, kc == 15,
                                       [r_wh[k], r_hp[kh]], [r_vg[a]])
                                CP("act", vtok[:, t, :], vg[:, a, 0:128], [r_vg[a]], [r_vt])
                                ACT(gg[:, t, :], vg[:, a, 128:256], AF.Silu, [r_vg[a]], [r_gg])
                                TT("pool", gg[:, t, :], gg[:, t, :], onb[:], ALU.mult, [r_gg, r_rc], [r_gg])
                            pp = pc % 2
                            qs, tF, tK, Bz, tE, tX, dD = qs2[pp], tF2[pp], tK2[pp], Bz2[pp], tE2[pp], tX2[pp], dD2[pp]
                            r_qs, r_tF, r_tK, r_Bz, r_tE, r_tX, r_dD = (r_qs2[pp], r_tF2[pp], r_tK2[pp], r_Bz2[pp], r_tE2[pp],
                                                                        r_tX2[pp], r_dD2[pp])
                            ACT(qs[:], zq, AF.Silu, [r_zq], [r_qs])
                            ch0 = pc * 6
                            for d in range(2):
                                ACT(tF[d][:], zf[d], AF.Sigmoid, [r_zf[d]], [r_tF[d]])
                            for d in range(2):
                                col = d * 16 + h
                                TS("dve", tK[d][:], tF[d][:], lbt[:, 2, col:col + 1], lbt[:, 1, col:col + 1], ALU.mult, ALU.add,
                                   [r_tF[d], r_lbt], [r_tK[d]])
                                TS("dve", tF[d][:], tF[d][:], lbt[:, 1, col:col + 1], lbt[:, 0, col:col + 1], ALU.mult, ALU.add,
                                   [r_tF[d], r_lbt], [r_tF[d]])
                            for d in range(2):
                                ACT(tF[d][:], tF[d][:], AF.Ln, [r_tF[d]], [r_tF[d]])
                            for d in range(2):
                                P.op("dve", lambda e, bo=Bz[d][:, 1:385], fi=tF[d][:]: e.tensor_tensor_scan(
                                    out=bo, data0=ones1[:].to_broadcast([128, 384]), data1=fi, initial=0.0, op0=ALU.mult, op1=ALU.add),
                                     [r_tF[d], r_const], [r_Bz[d]])
                                bzc = Bz[d][:, 0:384].rearrange("p (c j) -> p c j", j=64)
                                bze = Bz[d][:, 1:385].rearrange("p (c j) -> p c j", j=64)
                                in0 = bze if d == 0 else bzc
                                in1 = bzc[:, :, 32:33].to_broadcast([128, 6, 64])
                                TT("dve", tE[d][:].rearrange("p (c j) -> p c j", j=64), in0, in1, ALU.subtract, [r_Bz[d]], [r_tE[d]])
                                TT("dve", dD[d][:, 0:6].rearrange("p (c o) -> p c o", o=1), bzc[:, :, 32:33], bzc[:, :, 0:1],
                                   ALU.subtract, [r_Bz[d]], [r_dD[d]])
                                TT("dve", dD[d][:, 6:12].rearrange("p (c o) -> p c o", o=1), bze[:, :, 63:64], bzc[:, :, 32:33],
                                   ALU.subtract, [r_Bz[d]], [r_dD[d]])
                                TT("dve", dD[d][:, 12:18].rearrange("p (c o) -> p c o", o=1), bze[:, :, 63:64], bzc[:, :, 0:1],
                                   ALU.subtract, [r_Bz[d]], [r_dD[d]])
                            for d in range(2):
                                ACT(tX[d][:], tE[d][:], AF.Exp, [r_tE[d]], [r_tX[d]])
                                ACT(tE[d][:], tE[d][:], AF.Exp, [r_tE[d]], [r_tE[d]], scale=-1.0)
                                ha, ga = (0, 6) if d == 0 else (6, 0)
                                ACT(Hm[d][:, ch0:ch0 + 6], dD[d][:, ha:ha + 6], AF.Exp, [r_dD[d]], [r_D[d]])
                                ACT(Gm[d][:, ch0:ch0 + 6], dD[d][:, ga:ga + 6], AF.Exp, [r_dD[d]], [r_D[d]])
                                ACT(Dd[d][:, ch0:ch0 + 6], dD[d][:, 12:18], AF.Exp, [r_dD[d]], [r_D[d]])
                            for d in range(2):
                                qfac, kfac = (tX[d], tE[d]) if d == 0 else (tE[d], tX[d])
                                for par in range(2):
                                    o_q = qsp[d][:, ch0 + par:ch0 + 6:2, par * 64:par * 64 + 64]
                                    o_k = ksp[d][:, ch0 + par:ch0 + 6:2, par * 64:par * 64 + 64]
                                    v = lambda tl: tl[:].rearrange("p (c j) -> p c j", j=64)[:, par:6:2, :]
                                    TT("pool", o_q, v(qs), v(qfac), ALU.mult, [r_qs, r_tX[d], r_tE[d]], [r_qsp[d][pc]])
                                    TT("pool" if par else "dve", o_k, v(tK[d]), v(kfac), ALU.mult, [r_tK[d], r_tX[d], r_tE[d]], [r_ksp[d][pc]])
                        if stop == 80 and h == 0:
                            DUMP(0, qs[:], 384, [r_qs])
                            for d in range(2):
                                DUMP(1 + d * 5, tF[d][:], 384, [r_tF[d]])
                                DUMP(2 + d * 5, tK[d][:], 384, [r_tK[d]])
                                DUMP(3 + d * 5, Bz[d][:], 385, [r_Bz[d]])
                                DUMP(4 + d * 5, tE[d][:], 384, [r_tE[d]])
                                DUMP(5 + d * 5, tX[d][:], 384, [r_tX[d]])
                            DUMP(11, Dd[0][:], 36, [r_D[0]])
                            DUMP(12, Dd[1][:], 36, [r_D[1]])
                            DUMP(13, gg[:].rearrange("p t v -> p (t v)")[:, 0:2048], 2048, [r_gg])
                            DUMP(14, qsp[0][:].rearrange("p c j -> p (c j)")[:, 0:2048], 2048, r_qsp[0], bf=True)
                            DUMP(15, ksp[0][:].rearrange("p c j -> p (c j)")[:, 0:2048], 2048, r_ksp[0], bf=True)
                            DUMP(16, vtok[:].rearrange("p t v -> p (t v)")[:, 0:2048], 2048, [r_vt], bf=True)
                        ctiles = list(range(18)) if need_ctx else list(range(2, 18))
                        seqs = [list(range(36)), [3, 2, 1, 0] + list(range(35, 3, -1))]
                        for d in range(2):
                            MS("pool", Sbf[d][:, seqs[d][0], :], 0.0, [r_Sbf[d]])
                        for g4 in range(9):
                            for d in range(2):
                                cs = seqs[d][g4 * 4:(g4 + 1) * 4]
                                for i, c in enumerate(cs):
                                    TR(pTr[d][:, i, :], ksp[d][:, c, :], identb[:], [r_ksp[d][c // 6], r_const], [r_pTr[d]])
                                CP("act", ktok[d][:], pTr[d], [r_pTr[d]], [r_ktok[d]])
                                for i, c in enumerate(cs):
                                    MM(pU[d][:, i, :], ktok[d][:, i, :], vtok[:, c // 2, :], True, True, [r_ktok[d], r_vt],
                                       [r_pU[d]])
                            for i in range(4):
                                n = g4 * 4 + i
                                for d in range(2):
                                    c = seqs[d][n]
                                    Sc, Sn = Tst[d][n % 2], Tst[d][(n + 1) % 2]
                                    rc_, rn = r_T[d][n % 2], r_T[d][(n + 1) % 2]
                                    if n == 0:
                                        TS("dve", Sn[:], pU[d][:, i, :], Gm[d][:, c:c + 1], None, ALU.mult, None, [r_pU[d], r_D[d]], [rn])
                                    else:
                                        ut, rut = utmp[d][n % 2], r_ut[d][n % 2]
                                        TS("dve", ut[:], pU[d][:, i, :], Gm[d][:, c:c + 1], None, ALU.mult, None, [r_pU[d], r_D[d]], [rut])
                                        ACT(Sbf[d][:, c, :], Sc[:], AF.Identity, [rc_, r_D[d]], [r_Sbf[d]], scale=Hm[d][:, c:c + 1])
                                        if n < 35:
                                            STT("dve", Sn[:], Sc[:], Dd[d][:, c:c + 1], ut[:], ALU.mult, ALU.add, [rc_, r_D[d], rut], [rn])
                        for d in range(2):
                            for g4 in range(0, 18, 4):
                                ts_ = list(range(g4, min(g4 + 4, 18)))
                                for i, t in enumerate(ts_):
                                    for c in (2 * t, 2 * t + 1):
                                        MM(pA[:, i, :], ksp[d][:, c, :], qsp[d][:, c, :], c == 2 * t, c == 2 * t + 1,
                                           [r_ksp[d][c // 6], r_qsp[d][c // 6]], [r_pA])
                                n_ = len(ts_)
                                ka = (g4 // 4) % 2
                                CP("act", atmp[ka][:, 0:n_, :], pA[:, 0:n_, :], [r_pA], [r_at[ka]])
                                cm = -1 if d == 0 else 1
                                P.op("pool", lambda e, o=ATm[d][:, g4:g4 + n_, :], i_=atmp[ka][:, 0:n_, :], n_=n_, cm=cm:
                                     e.affine_select(out=o, in_=i_, pattern=[[0, n_], [-cm, 128]], compare_op=ALU.is_ge, fill=0.0,
                                                     base=0, channel_multiplier=cm), [r_at[ka]], [r_AT[d]])
                        for g4 in range(0, 18, 4):
                            ts_ = [t for t in range(g4, min(g4 + 4, 18))]
                            for i, t in enumerate(ts_):
                                if t not in ctiles:
                                    continue
                                mms = []
                                for d in range(2):
                                    mms.append((ATm[d][:, t, :], vtok[:, t, :], [r_AT[d], r_vt]))
                                    for c in (2 * t, 2 * t + 1):
                                        mms.append((qsp[d][:, c, :], Sbf[d][:, c, :], [r_qsp[d][c // 6], r_Sbf[d]]))
                                for mi, (lh, rh, rr) in enumerate(mms):
                                    MM(pA[:, i, :], lh, rh, mi == 0, mi == len(mms) - 1, rr, [r_pA])
                            for i, t in enumerate(ts_):
                                if t not in ctiles:
                                    continue
                                so = ost[:, i, :]
                                ACT(ojunk[:], pA[:, i, :], AF.Square, [r_pA], [r_oj, r_ost], accum=so[:, 0:1])
                                ACT(so[:, 1:2], so[:, 0:1], AF.Sqrt, [r_ost, r_const], [r_ost], scale=1.0 / 128, bias=epsT[:, 0:1])
                                P.op("dve", lambda e, so=so: e.reciprocal(out=so[:, 2:3], in_=so[:, 1:2]), [r_ost], [r_ost])
                                STT("dve", yb[i][:], pA[:, i, :], so[:, 2:3], gg[:, t, :], ALU.mult, ALU.mult, [r_pA, r_ost, r_gg],
                                    [r_yb[i]])
                            for i, t in enumerate(ts_):
                                if t not in ctiles:
                                    continue
                                TR(pTr[0][:, i, :], yb[i][:], identb[:], [r_yb[i], r_const], [r_pTr[0]])
                            live = [i for i, t in enumerate(ts_) if t in ctiles]
                            if live:
                                i0, i1 = live[0], live[-1] + 1
                                t0 = ts_[i0]
                                CP("act", yTh[:, t0 * 128:(t0 + i1 - i0) * 128].rearrange("p (i t) -> p i t", t=128), pTr[0][:, i0:i1, :],
                                   [r_pTr[0]], [r_yTh])
                        if stop == 80 and h == 0:
                            DUMP(17, yTh[:, 0:2048], 2048, [r_yTh], bf=True)
                            halt[0] = True
                            return
                        c_lo = ctiles[0] * 128
                        P.dma("sp", yT_d[h, :, c_lo:2304], yTh[:, c_lo:2304], reads=[r_yTh], writes=[r_yT[t] for t in ctiles])
                P.barrier()

            for l in range(n_layers):
                last = l == 3
                j = l // 2
                cur[0] = l
                load_mods(l)
                if CK(3):
                    return
                if l % 2 == 0:
                    phase_att(l, j, not last)
                    wo_src = wo_att[j]
                else:
                    phase_rec(l, j, not last)
                    wo_src = wo_rec[j]
                if CK(5):
                    return
                tiles = list(range(2, 18)) if last else list(range(18))
                if stop == 69:
                    tiles = [17]
                if stop == 70:
                    tiles = list(range(17, -1, -1))
                phase_wo(l, wo_src, tiles)
                if CK(6):
                    return
                if last:
                    groups = [[2, 3, 4, 5], list(range(6, 12)), list(range(12, 18))]
                else:
                    groups = [list(range(0, 6)), list(range(6, 12)), list(range(12, 18))]
                phase_mlp(l, groups, last)

            if n_layers == 4:
                with contextlib.ExitStack() as es:
                    fg = es.enter_context(SBT("fg", [128, 2048], F32))
                    r_fg = Res()
                    P.dma("sp", fg[:], fing_d.partition_broadcast(128), writes=[r_fg])
                    xf = [es.enter_context(SBT("xf%d" % i, [128, 2048], F32)) for i in range(2)]
                    fj = [es.enter_context(SBT("fj%d" % i, [128, 2048], BF16)) for i in range(2)]
                    fss = [es.enter_context(SBT("fss%d" % i, [128, 4], F32)) for i in range(2)]
                    r_xf = [Res(), Res()]
                    for t in range(2, 18):
                        k = t % 2
                        ss = fss[k]
                        P.dma("sp", xf[k][:], res[t * 128:(t + 1) * 128, :], reads=[r_res[t]], writes=[r_xf[k]])
                        ACT(fj[k][:], xf[k][:], AF.Square, [r_xf[k]], [r_xf[k]], accum=ss[:, 0:1])
                        ACT(ss[:, 1:2], ss[:, 0:1], AF.Sqrt, [r_xf[k], r_const], [r_xf[k]], scale=1.0 / 2048, bias=epsT[:, 0:1])
                        P.op("dve", lambda e, ss=ss: e.reciprocal(out=ss[:, 2:3], in_=ss[:, 1:2]), [r_xf[k]], [r_xf[k]])
                        STT("dve", xf[k][:], xf[k][:], ss[:, 2:3], fg[:], ALU.mult, ALU.mult, [r_xf[k], r_fg], [r_xf[k]])
                        P.dma("sp", out[(t - 2) * 128:(t - 1) * 128, :], xf[k][:], reads=[r_xf[k]], writes=[r_out])
        try:
            main_body()
        except _Stop:
            pass
        fin_reads = [r_out]
        if dbg and stop == 2:
            r_dbg = Res()
            P.dma("sp", dbg_o[0:128, 0:768], fmA[:].rearrange("p l j s -> p (l j s)"), reads=[r_fmA], writes=[r_dbg])
            fin_reads.append(r_dbg)
        elif dbg and dumped[0]:
            fin_reads.append(r_dump)
        elif dbg:
            r_dbg = Res()
            for t in range(18):
                P.dma("sp", dbg_o[t * 128:(t + 1) * 128, :], (xin if n_layers == 0 else res)[t * 128:(t + 1) * 128, :],
                      reads=[r_res[t]], writes=[r_dbg])
            fin_reads.append(r_dbg)
        P.op("sp", lambda e: e.nop(), fin_reads, ())
        P.emit(nc)
    return nc


_CONSTS = None


def make_in_maps(inputs, cores):
    global _CONSTS
    if _CONSTS is None:
        _CONSTS = make_consts()
    f = lambda a: np.ascontiguousarray(np.asarray(a, dtype=np.float32))
    x, c, ctx, c_ctx = f(inputs["x"]), f(inputs["c"]), f(inputs["ctx"]), f(inputs["c_ctx"])

    def fm(v):
        v = v.reshape(-1, 16, 128)
        return np.ascontiguousarray(v.transpose(2, 0, 1).reshape(128, -1))

    shared = {
        "w_ada": f(inputs["w_ada"]),
        "bada": np.ascontiguousarray(f(inputs["b_ada"]).reshape(4, 96, 128).transpose(2, 0, 1).reshape(128, 384)),
        "gmix": fm(f(inputs["norm_mix_g"])), "gmlp": fm(f(inputs["norm_mlp_g"])),
        "att_w_qkv": f(inputs["att_w_qkv"]), "att_w_o": f(inputs["att_w_o"]),
        "att_sink": f(inputs["att_sink"]).reshape(1, 32),
        "rec_w_in": f(inputs["rec_w_in"]), "rec_w_o": f(inputs["rec_w_o"]),
        "lbl": fm(f(inputs["rec_lb_logits"])), "rec_onorm_g": f(inputs["rec_onorm_g"]),
        "mlp_w_up": f(inputs["mlp_w_up"]), "mlp_w_down": f(inputs["mlp_w_down"]),
        "final_norm_g": f(inputs["final_norm_g"]).reshape(1, 2048),
    }
    shared.update(_CONSTS)
    maps = []
    for b in cores:
        m = dict(shared)
        m["xin"] = np.ascontiguousarray(np.concatenate([ctx[b], x[b]], axis=0))
        cc = np.stack([c[b], c_ctx], axis=0)
        m["ccl"] = np.ascontiguousarray(cc.reshape(2, 16, 128).transpose(2, 1, 0).reshape(128, 32))
        maps.append(m)
    return maps


def kernel(**inputs):
    nc = build(4, False)
    maps = make_in_maps(inputs, list(range(8)))
    r = run_bass_kernel_spmd(nc, maps, core_ids=list(range(8)))
    return np.stack([np.asarray(r.results[b]["out"], dtype=np.float32) for b in range(8)], axis=0)
```

```python
import contextlib
import numpy as np
import concourse.bass as bass
import concourse.mybir as mybir
from concourse.bass_utils import run_bass_kernel_spmd

F32 = mybir.dt.float32
BF16 = mybir.dt.bfloat16
AF = mybir.ActivationFunctionType
ALU = mybir.AluOpType
AX = mybir.AxisListType

ENGS = ("pe", "act", "dve", "pool", "sp")
DMA_SLOTS = 8


class Res:
    __slots__ = ("name", "last_w", "readers", "excl")

    def __init__(self, name="", excl=False):
        self.name = name
        self.last_w = None
        self.readers = []
        self.excl = excl


def PRes():
    return Res("psum", True)


class Op:
    __slots__ = ("eng", "fn", "deps", "needs_inc", "is_dma", "slot", "slot_n", "tok")

    def __init__(self, eng, fn):
        self.eng = eng
        self.fn = fn
        self.deps = []
        self.needs_inc = False
        self.is_dma = False
        self.slot = None
        self.slot_n = 0
        self.tok = None


class Prog:
    def __init__(self):
        self.ops = {e: [] for e in ENGS}
        self.dma_count = {e: 0 for e in ENGS}
        self.slot_last = {}
        self.barrier_deps = {e: [] for e in ENGS}

    def _add_dep(self, op, dep, kind):
        if dep is None or dep is op:
            return
        if not dep.is_dma and dep.eng == op.eng and not op.is_dma:
            if op.eng == "pe" or kind == "war" or kind == "excl":
                return
        for d in op.deps:
            if d is dep:
                return
        op.deps.append(dep)
        if not dep.is_dma:
            dep.needs_inc = True

    def _track(self, op, reads, writes):
        ex = [r for r in reads if r.excl] + [w for w in writes if w.excl]
        if ex:
            reads = [r for r in reads if not r.excl]
            writes = [w for w in writes if not w.excl]
            for x in ex:
                self._add_dep(op, x.last_w, "excl")
                x.last_w = op
        for r in reads:
            self._add_dep(op, r.last_w, "raw")
        for w in writes:
            self._add_dep(op, w.last_w, "waw")
            for rd in w.readers:
                self._add_dep(op, rd, "war")
        for r in reads:
            r.readers.append(op)
        for w in writes:
            w.last_w = op
            w.readers = []
        bd = self.barrier_deps[op.eng]
        if bd:
            for d in bd:
                if d is not op and not (d.eng == op.eng == "pe" and not d.is_dma and not op.is_dma):
                    if d not in op.deps:
                        op.deps.append(d)
                        if not d.is_dma:
                            d.needs_inc = True
            self.barrier_deps[op.eng] = []

    def op(self, eng, fn, reads=(), writes=()):
        o = Op(eng, fn)
        self._track(o, reads, writes)
        self.ops[eng].append(o)
        return o

    def dma(self, eng, out, in_, reads=(), writes=(), **kw):
        def fn(e):
            return e.dma_start(out=out, in_=in_, **kw)

        o = Op(eng, fn)
        o.is_dma = True
        n = self.dma_count[eng]
        self.dma_count[eng] = n + 1
        o.slot = n % DMA_SLOTS
        o.slot_n = n // DMA_SLOTS
        prev = self.slot_last.get((eng, o.slot))
        if prev is not None:
            o.deps.append(prev)
        self.slot_last[(eng, o.slot)] = o
        self._track(o, reads, writes)
        self.ops[eng].append(o)
        return o

    def barrier(self):
        lasts = []
        for e in ENGS:
            for o in reversed(self.ops[e]):
                if not o.is_dma:
                    lasts.append(o)
                    break
        for o in self.slot_last.values():
            lasts.append(o)
        for e in ENGS:
            self.barrier_deps[e] = list(lasts)

    def emit(self, nc):
        engmap = {"pe": "tensor", "act": "scalar", "dve": "vector", "pool": "gpsimd", "sp": "sync"}
        with contextlib.ExitStack() as es:
            sems = {e: es.enter_context(nc.semaphore("s_" + e)) for e in ENGS}
            dsems = {}
            for e in ENGS:
                if self.dma_count[e] > 0:
                    for s in range(DMA_SLOTS):
                        dsems[(e, s)] = es.enter_context(nc.semaphore("d_%s_%d" % (e, s)))
            for e in ENGS:
                cnt = 0
                for o in self.ops[e]:
                    if o.is_dma:
                        o.tok = (dsems[(e, o.slot)], 16 * (o.slot_n + 1))
                    elif o.needs_inc:
                        cnt += 1
                        o.tok = (sems[e], cnt)
            block = es.enter_context(nc.Block())

            def make(e):
                def body(eng):
                    seen = {}
                    for o in self.ops[e]:
                        for d in o.deps:
                            sem, val = d.tok
                            key = id(sem)
                            if seen.get(key, 0) < val:
                                eng.wait_ge(sem, val)
                                seen[key] = val
                        ins = o.fn(eng)
                        if o.is_dma:
                            ins.then_inc(o.tok[0], 16)
                        elif o.needs_inc:
                            ins.then_inc(o.tok[0], 1)

                return body

            for e in ENGS:
                if self.ops[e]:
                    getattr(block, engmap[e])(make(e))


NEG = -30000.0
SCALE = 128 ** -0.5


def make_consts():
    c = {}
    c["identf"] = np.eye(128, dtype=np.float32)
    rm = np.zeros((128, 128), np.float32)
    for m in range(128):
        a, p = (m // 64), (m % 64) // 32
        if p == 0:
            rm[m + 32, m] = -1.0
        else:
            rm[m - 32, m] = 1.0
    c["rotm"] = rm
    qi = np.arange(128)[:, None]
    kj = np.arange(128)[None, :]
    mb = np.zeros((3, 128, 384), np.float32)
    for v in range(3):
        mb[v, :, 0:128] = np.where(kj >= qi, 0.0, NEG)
        mb[v, :, 256:384] = np.where(kj <= qi, 0.0, NEG)
    mb[0, :, 0:128] = NEG
    mb[2, :, 256:384] = NEG
    c["amask"] = mb
    s = np.arange(128)[:, None]
    t = np.arange(128)[None, :]
    same = (s // 64) == (t // 64)
    rmask = np.zeros((2, 128, 128), np.float32)
    rmask[0] = (same & (s <= t)).astype(np.float32)
    rmask[1] = (same & (s >= t)).astype(np.float32)
    c["rmask"] = rmask
    pos = np.arange(2048)
    row = (pos // 64).astype(np.float32)
    col = (pos % 64).astype(np.float32)
    inv = (10000.0 ** (-np.arange(32, dtype=np.float32) / 32)).astype(np.float32)
    ar = row[:, None] * inv[None, :]
    ac = col[:, None] * inv[None, :]
    ang = np.concatenate([ar, ar, ac, ac], axis=-1).astype(np.float32)
    c["cosT"] = np.ascontiguousarray(np.cos(ang).T.astype(np.float32))
    c["sinT"] = np.ascontiguousarray(np.sin(ang).T.astype(np.float32))
    return c


class _Stop(Exception):
    pass


def build(n_layers=4, dbg=False, stop=0):
    nc = bass.Bass("TRN2", target_bir_lowering=False)
    P = Prog()

    _cnt = [0]

    def SBT(name, shape, dt):
        _cnt[0] += 1
        return nc.sbuf_tensor("%s_u%d" % (name, _cnt[0]), shape, dt)

    def PST(name, shape, dt):
        _cnt[0] += 1
        return nc.psum_tensor("%s_u%d" % (name, _cnt[0]), shape, dt)

    halt = [False]
    cur = [-1]

    def CK(n):
        if stop == n and (n <= 2 or cur[0] == n_layers - 1):
            halt[0] = True
        return halt[0]

    def din(name, shape, dt=F32):
        return nc.dram_tensor(name, list(shape), dt, kind="ExternalInput").ap()

    xin = din("xin", [2304, 2048])
    ccl = din("ccl", [128, 32])
    w_ada = din("w_ada", [4, 2048, 12288])
    bada_d = din("bada", [128, 384])
    gmix_d = din("gmix", [128, 64])
    gmlp_d = din("gmlp", [128, 64])
    wqkv = din("att_w_qkv", [2, 2048, 3072])
    wo_att = din("att_w_o", [2, 2048, 2048])
    sink_d = din("att_sink", [1, 32])
    win = din("rec_w_in", [2, 2048, 10240])
    wo_rec = din("rec_w_o", [2, 2048, 2048])
    lbl_d = din("lbl", [128, 64])
    onorm_d = din("rec_onorm_g", [2, 128])
    wup = din("mlp_w_up", [4, 2048, 8192])
    wdn = din("mlp_w_down", [4, 8192, 2048])
    fing_d = din("final_norm_g", [1, 2048])
    identf_d = din("identf", [128, 128])
    rotm_d = din("rotm", [128, 128])
    amask_d = din("amask", [3, 128, 384])
    rmask_d = din("rmask", [2, 128, 128])
    cosT_d = din("cosT", [128, 2048])
    sinT_d = din("sinT", [128, 2048])
    out = nc.dram_tensor("out", [2048, 2048], F32, kind="ExternalOutput").ap()
    res = nc.dram_tensor("res", [2304, 2048], F32).ap()
    yT_d = nc.dram_tensor("yT_d", [16, 128, 2304], BF16).ap()
    hT_d = nc.dram_tensor("hT_d", [16, 128, 2304], BF16).ap()
    if dbg:
        dbg_o = nc.dram_tensor("dbg", [2304, 2048], F32, kind="ExternalOutput").ap()

    r_res = [Res("res%d" % t) for t in range(18)]
    r_yT = [Res("yT%d" % t) for t in range(18)]
    r_hT = [Res("hT%d" % t) for t in range(18)]
    r_out = Res("out")

    r_dump = Res("dump")
    dumped = [False]

    def DUMP(i, ap, ncols, R, bf=False):
        if not dbg:
            return
        dumped[0] = True
        P.dma("pool" if bf else "sp", dbg_o[i * 128:(i + 1) * 128, 0:ncols], ap, reads=R, writes=[r_dump])

    def MM(o, lhsT, rhs, start, stop, R, W):
        P.op("pe", lambda e: e.matmul(o, lhsT=lhsT, rhs=rhs, start=start, stop=stop), R, W)

    def TR(o, i, ident, R, W):
        P.op("pe", lambda e: e.transpose(out=o, in_=i, identity=ident), R, W)

    def ACT(o, i, func, R, W, scale=1.0, bias=0.0, accum=None):
        if accum is None:
            P.op("act", lambda e: e.activation(out=o, in_=i, func=func, bias=bias, scale=scale), R, W)
        else:
            P.op("act", lambda e: e.activation(out=o, in_=i, func=func, bias=bias, scale=scale, accum_out=accum), R, W)

    def TS(eng, o, i, s1, s2, op0, op1, R, W):
        if s2 is None:
            P.op(eng, lambda e: e.tensor_scalar(out=o, in0=i, scalar1=s1, scalar2=None, op0=op0), R, W)
        else:
            P.op(eng, lambda e: e.tensor_scalar(out=o, in0=i, scalar1=s1, scalar2=s2, op0=op0, op1=op1), R, W)

    def TT(eng, o, a, b, op, R, W):
        P.op(eng, lambda e: e.tensor_tensor(out=o, in0=a, in1=b, op=op), R, W)

    def STT(eng, o, a, s, b, op0, op1, R, W):
        P.op(eng, lambda e: e.scalar_tensor_tensor(out=o, in0=a, scalar=s, in1=b, op0=op0, op1=op1), R, W)

    def CP(eng, o, i, R, W):
        if eng == "act":
            P.op("act", lambda e: e.copy(out=o, in_=i), R, W)
        else:
            P.op(eng, lambda e: e.tensor_copy(out=o, in_=i), R, W)

    def MS(eng, o, v, W):
        P.op(eng, lambda e: e.memset(o, v), (), W)

    with contextlib.ExitStack() as gs_:
        def gsb(name, shape, dt):
            return gs_.enter_context(SBT(name, shape, dt))

        identf = gsb("identf", [128, 128], F32)
        identb = gsb("identb", [128, 128], BF16)
        rotm = gsb("rotm", [128, 128], BF16)
        gmix = gsb("gmix", [128, 64], F32)
        gmlp = gsb("gmlp", [128, 64], F32)
        lbl = gsb("lbl", [128, 64], F32)
        lbt = gsb("lbt", [128, 3, 32], F32)
        epsT = gsb("epsT", [128, 1], F32)
        ones1 = gsb("ones1", [128, 1], F32)
        sinkr = gsb("sinkr", [128, 32], F32)
        fmA = gsb("fmA", [128, 4, 96, 2], F32)
        bada = gsb("bada", [128, 384], F32)
        r_fmA = Res("fmA")
        gsm = gsb("gsm", [128, 2, 2, 16], F32)
        r_const = Res("const")
        r_lbt = Res("lbt")
        r_fm = Res("fm")
        r_gate = Res("gate")

        P.dma("sp", identf[:], identf_d, writes=[r_const])
        P.dma("pool", identb[:], identf_d, writes=[r_const])
        P.dma("pool", rotm[:], rotm_d, writes=[r_const])
        P.dma("sp", gmix[:], gmix_d, writes=[r_const])
        P.dma("sp", gmlp[:], gmlp_d, writes=[r_const])
        P.dma("sp", lbl[:], lbl_d, writes=[r_const])
        P.dma("sp", bada[:], bada_d, writes=[r_const])
        P.dma("sp", sinkr[:], sink_d.partition_broadcast(128), writes=[r_const])
        MS("dve", epsT[:], 1e-6, [r_const])
        MS("dve", ones1[:], 1.0, [r_const])
        TS("dve", sinkr[:], sinkr[:], 1.0 / SCALE, None, ALU.mult, None, [r_const], [r_const])
        def main_body():
            if CK(1):
                return

            with contextlib.ExitStack() as ps_:
                def sb(name, shape, dt):
                    return ps_.enter_context(SBT(name, shape, dt))

                sil_f = sb("sil_f", [128, 32], F32)
                sil = sb("sil", [128, 16, 2], BF16)
                wb = [sb("wb%d" % i, [128, 16, 512], BF16) for i in range(3)]
                pm = ps_.enter_context(PST("pm", [128, 512], F32))
                r_sil = Res()
                r_wb = [Res() for _ in range(3)]
                r_pm = PRes()
                P.dma("sp", sil_f[:], ccl, writes=[r_sil])
                ACT(sil[:].rearrange("p c s -> p (c s)"), sil_f[:], AF.Silu, [r_sil], [r_sil])
                i = 0
                for l in range(n_layers):
                    for nb in range(24):
                        k = i % 3
                        i += 1
                        P.dma("pool", wb[k][:], w_ada[l, :, nb * 512:(nb + 1) * 512].rearrange("(c p) n -> p c n", p=128),
                              writes=[r_wb[k]])
                        for jj in range(4):
                            jc = nb * 4 + jj
                            for kc in range(16):
                                MM(pm[:, jc * 2:jc * 2 + 2], wb[k][:, kc, jj * 128:(jj + 1) * 128], sil[:, kc, :], kc == 0, kc == 15,
                                   [r_sil, r_wb[k]], [r_pm])
                    TT("dve", fmA[:, l, :, :], pm[:, 0:192].rearrange("p (j s) -> p j s", s=2),
                       bada[:, l * 96:(l + 1) * 96].rearrange("p (j o) -> p j o", o=1).to_broadcast([128, 96, 2]), ALU.add,
                       [r_pm, r_const], [r_fmA])
            P.barrier()
            if CK(2):
                return

            def stream_of(t):
                return 1 if t < 2 else 0

            def src_tile(l, t, mixer=True):
                s = xin if (l == 0 and mixer) else res
                return s[t * 128:(t + 1) * 128, :]

            class NormCtx:
                def __init__(self, es, tag):
                    def sb(name, shape, dt):
                        return es.enter_context(SBT(tag + name, shape, dt))

                    self.xt = [sb("xt%d" % i, [128, 2048], F32) for i in range(2)]
                    self.xn = [sb("xn%d" % i, [128, 2048], BF16) for i in range(2)]
                    self.ss = [sb("ss%d" % i, [128, 4], F32) for i in range(2)]
                    self.pT = es.enter_context(PST(tag + "pT", [128, 16, 128], BF16))
                    self.r_xt = [Res(), Res()]
                    self.r_xn = [Res(), Res()]
                    self.r_ss = [Res(), Res()]
                    self.r_pT = PRes()
                    self.n = 0

            def norm_tile(N, l, t, which, dst, r_dst):
                mixer = which == 0
                k = N.n % 2
                N.n += 1
                s = stream_of(t)
                xt, xn, ss = N.xt[k], N.xn[k], N.ss[k]
                P.dma("sp", xt[:], src_tile(l, t, mixer), reads=[r_res[t]], writes=[N.r_xt[k]])
                ACT(xn[:], xt[:], AF.Square, [N.r_xt[k]], [N.r_xn[k], N.r_ss[k]], accum=ss[:, 0:1])
                ACT(ss[:, 1:2], ss[:, 0:1], AF.Sqrt, [N.r_ss[k], r_const], [N.r_ss[k]], scale=1.0 / 2048, bias=epsT[:, 0:1])
                P.op("dve", lambda e: e.reciprocal(out=ss[:, 2:3], in_=ss[:, 1:2]), [N.r_ss[k]], [N.r_ss[k]])
                TS("dve", xn[:], xt[:], ss[:, 2:3], None, ALU.mult, None, [N.r_xt[k], N.r_ss[k]], [N.r_xn[k]])
                for kc in range(16):
                    TR(N.pT[:, kc, :], xn[:, kc * 128:(kc + 1) * 128], identb[:], [N.r_xn[k], r_const], [N.r_pT])
                sh = 0 if which == 0 else 48
                for kc in range(16):
                    sc_ap = gsm[:, s, which, kc:kc + 1]
                    bi_ap = fmA[:, l, sh + kc, s:s + 1]
                    if kc % 2 == 0:
                        ACT(dst[:, kc, :], N.pT[:, kc, :], AF.Identity, [N.r_pT, r_fm, r_fmA], [r_dst], scale=sc_ap, bias=bi_ap)
                    else:
                        TS("dve", dst[:, kc, :], N.pT[:, kc, :], sc_ap, bi_ap, ALU.mult, ALU.add, [N.r_pT, r_fm, r_fmA], [r_dst])

            def load_mods(l):
                for s in range(2):
                    STT("dve", gsm[:, s, 0, :], fmA[:, l, 16:32, s], 1.0, gmix[:, l * 16:(l + 1) * 16], ALU.add, ALU.mult,
                        [r_fmA, r_const], [r_fm])
                    STT("dve", gsm[:, s, 1, :], fmA[:, l, 64:80, s], 1.0, gmlp[:, l * 16:(l + 1) * 16], ALU.add, ALU.mult,
                        [r_fmA, r_const], [r_fm])

            def load_gate(es, l, m):
                gate_bc = [es.enter_context(SBT("gate%d" % s, [128, 2048], F32)) for s in range(2)]
                with contextlib.ExitStack() as eg:
                    rep = [eg.enter_context(SBT("rep%d" % i, [128, 128], F32)) for i in range(2)]
                    pg = eg.enter_context(PST("pg", [128, 512], F32))
                    r_rep = [Res(), Res()]
                    r_pg = PRes()
                    n = 0
                    for s in range(2):
                        for c4 in range(4):
                            for ci in range(4):
                                c = c4 * 4 + ci
                                k = n % 2
                                n += 1
                                CP("dve", rep[k][:], fmA[:, l, m * 16 + c, s:s + 1].to_broadcast([128, 128]), [r_fmA], [r_rep[k]])
                                MM(pg[:, ci * 128:(ci + 1) * 128], rep[k][:], identf[:], True, True, [r_rep[k], r_const], [r_pg])
                            CP("act", gate_bc[s][:, c4 * 512:(c4 + 1) * 512], pg[:], [r_pg], [r_gate])
                P.barrier()
                return gate_bc

            def phase_wo(l, wo_src, tiles):
                with contextlib.ExitStack() as es:
                    gate_bc = load_gate(es, l, 2)
                    if CK(60):
                        return
                    def sb(name, shape, dt):
                        return es.enter_context(SBT(name, shape, dt))

                    wo = sb("wo", [128, 16, 2048], BF16)
                    yt = [sb("yt%d" % i, [128, 16, 128], BF16) for i in range(2)]
                    xt = [sb("wxt%d" % i, [128, 2048], F32) for i in range(2)]
                    tmp = [sb("wtmp%d" % i, [128, 512], F32) for i in range(2)]
                    acc = [es.enter_context(PST("woacc%d" % i, [128, 512], F32)) for i in range(4)]
                    r_wo = [Res() for _ in range(4)]
                    r_yt, r_xt, r_tmp = [Res(), Res()], [Res(), Res()], [Res(), Res()]
                    r_acc = [PRes() for _ in range(4)]
                    for nb in range(4):
                        P.dma("pool", wo[:, :, nb * 512:(nb + 1) * 512],
                              wo_src[:, nb * 512:(nb + 1) * 512].rearrange("(c p) n -> p c n", p=128), writes=[r_wo[nb]])
                    if CK(61):
                        return
                    n = 0
                    for it, t in enumerate(tiles):
                        if it == 1 and CK(62):
                            return
                        if it == 2 and CK(63):
                            return
                        if it == 3 and CK(64):
                            return
                        if it == 8 and CK(65):
                            return
                        if it == 13 and CK(66):
                            return
                        if it == 17 and CK(67):
                            return
                        k = it % 2
                        s = stream_of(t)
                        P.dma("sp", yt[k][:], yT_d[:, :, t * 128:(t + 1) * 128].rearrange("c p t -> p c t"),
                              reads=[r_yT[t]], writes=[r_yt[k]])
                        P.dma("sp", xt[k][:], src_tile(l, t), reads=[r_res[t]], writes=[r_xt[k]])
                        for nb in range(4):
                            a = acc[n % 4]
                            ra = r_acc[n % 4]
                            for kc in range(16):
                                MM(a[:], yt[k][:, kc, :], wo[:, kc, nb * 512:(nb + 1) * 512], kc == 0, kc == 15,
                                   [r_yt[k], r_wo[nb]], [ra])
                            tm = tmp[n % 2]
                            TT("dve", tm[:], a[:], gate_bc[s][:, nb * 512:(nb + 1) * 512], ALU.mult, [ra, r_gate], [r_tmp[n % 2]])
                            TT("pool", xt[k][:, nb * 512:(nb + 1) * 512], tm[:], xt[k][:, nb * 512:(nb + 1) * 512], ALU.add,
                               [r_tmp[n % 2], r_xt[k]], [r_xt[k]])
                            n += 1
                        P.dma("sp", res[t * 128:(t + 1) * 128, :], xt[k][:], reads=[r_xt[k]], writes=[r_res[t]])
                    if CK(68) or CK(69) or CK(70):
                        return
                P.barrier()

            def phase_mlp(l, groups, last):
                with contextlib.ExitStack() as es:
                    gate_bc = load_gate(es, l, 5)
                    def sb(name, shape, dt):
                        return es.enter_context(SBT(name, shape, dt))

                    aT = sb("aT", [128, 64, 768], BF16)
                    r_aT = [Res() for _ in range(64)]
                    wu_i = 0
                    for g, tiles in enumerate(groups):
                        T = len(tiles) * 128
                        halves = [(0, T // 2), (T // 2, T)]
                        with contextlib.ExitStack() as es2:
                            def sb2(name, shape, dt):
                                return es2.enter_context(SBT(name, shape, dt))

                            hT = sb2("hT", [128, 16, 768], BF16)
                            r_h = Res()
                            wu = [sb2("wu%d" % i, [128, 16, 256], BF16) for i in range(2)]
                            r_wu = [Res(), Res()]
                            rl = [sb2("rl%d" % i, [128, 384], BF16) for i in range(2)]
                            r_rl = [Res(), Res()]
                            N = NormCtx(es2, "m")
                            up = [es2.enter_context(PST("up%d" % i, [128, 512], F32)) for i in range(4)]
                            r_up = [PRes() for _ in range(4)]
                            for it, t in enumerate(tiles):
                                norm_tile(N, l, t, 1, hT[:, :, it * 128:(it + 1) * 128], r_h)
                            nu = 0
                            for fb in range(32):
                                k = fb % 2
                                P.dma("pool", wu[k][:], wup[l, :, fb * 256:(fb + 1) * 256].rearrange("(c p) n -> p c n", p=128),
                                      writes=[r_wu[k]])
                                for fi in range(2):
                                    fc = fb * 2 + fi
                                    pa = [up[(nu * 2) % 4], up[(nu * 2 + 1) % 4]]
                                    rp = [r_up[(nu * 2) % 4], r_up[(nu * 2 + 1) % 4]]
                                    for kc in range(16):
                                        for hi, (c0, c1) in enumerate(halves):
                                            MM(pa[hi][:, 0:c1 - c0], wu[k][:, kc, fi * 128:(fi + 1) * 128], hT[:, kc, c0:c1],
                                               kc == 0, kc == 15, [r_wu[k], r_h], [rp[hi]])
                                    for hi, (c0, c1) in enumerate(halves):
                                        r_ = rl[(nu * 2 + hi) % 2]
                                        rr = r_rl[(nu * 2 + hi) % 2]
                                        ACT(r_[:, 0:c1 - c0], pa[hi][:, 0:c1 - c0], AF.Relu, [rp[hi]], [rr])
                                        TT("dve", aT[:, fc, c0:c1], r_[:, 0:c1 - c0], r_[:, 0:c1 - c0], ALU.mult, [rr], [r_aT[fc]])
                                    nu += 1
                        P.barrier()
                        with contextlib.ExitStack() as es2:
                            def sb2(name, shape, dt):
                                return es2.enter_context(SBT(name, shape, dt))

                            wd = [sb2("wd%d" % i, [128, 8, 512], BF16) for i in range(3)]
                            r_wd = [Res() for _ in range(3)]
                            xs = [sb2("xs%d" % i, [128, 512], F32) for i in range(6)]
                            r_xs = [Res() for _ in range(6)]
                            tm = [sb2("dtm%d" % i, [128, 512], F32) for i in range(2)]
                            r_tm = [Res(), Res()]
                            acc = [es2.enter_context(PST("dacc%d" % i, [128, 512], F32)) for i in range(6)]
                            r_acc = [PRes() for _ in range(6)]
                            wi = 0
                            ne = 0
                            for nb in range(4):
                                for it, t in enumerate(tiles):
                                    P.dma("sp", xs[it][:], src_tile(l, t, False)[:, nb * 512:(nb + 1) * 512], reads=[r_res[t]],
                                          writes=[r_xs[it]])
                                for fb in range(8):
                                    k = wi % 3
                                    wi += 1
                                    P.dma("pool", wd[k][:],
                                          wdn[l, fb * 1024:(fb + 1) * 1024, nb * 512:(nb + 1) * 512].rearrange("(c p) n -> p c n", p=128),
                                          writes=[r_wd[k]])
                                    for fi in range(8):
                                        fc = fb * 8 + fi
                                        for it in range(len(tiles)):
                                            MM(acc[it][:], aT[:, fc, it * 128:(it + 1) * 128], wd[k][:, fi, :], fc == 0, fc == 63,
                                               [r_aT[fc], r_wd[k]], [r_acc[it]])
                                for it, t in enumerate(tiles):
                                    s = stream_of(t)
                                    tt_ = tm[ne % 2]
                                    rt = r_tm[ne % 2]
                                    ne += 1
                                    TT("dve", tt_[:], acc[it][:], gate_bc[s][:, nb * 512:(nb + 1) * 512], ALU.mult,
                                       [r_acc[it], r_gate], [rt])
                                    TT("pool", xs[it][:], tt_[:], xs[it][:], ALU.add, [rt, r_xs[it]], [r_xs[it]])
                                    P.dma("sp", res[t * 128:(t + 1) * 128, nb * 512:(nb + 1) * 512], xs[it][:],
                                          reads=[r_xs[it]], writes=[r_res[t]])
                        P.barrier()

            def phase_att(l, j, need_ctx):
                with contextlib.ExitStack() as es:
                    def sb(name, shape, dt):
                        return es.enter_context(SBT(name, shape, dt))

                    qT = sb("qT", [128, 16, 2304], BF16)
                    kT = sb("kT", [128, 4, 2560], BF16)
                    V = sb("V", [128, 20, 512], BF16)
                    r_q = [Res() for _ in range(18)]
                    r_k = [Res() for _ in range(20)]
                    r_v = [Res() for _ in range(20)]
                    MS("pool", kT[:, :, 256:384], 0.0, [r_k[2]])
                    MS("pool", kT[:, :, 2432:2560], 0.0, [r_k[19]])
                    MS("pool", V[:, 2, :], 0.0, [r_v[2]])
                    MS("pool", V[:, 19, :], 0.0, [r_v[19]])

                    def kslot(t):
                        return t if t < 2 else t + 1

                    with contextlib.ExitStack() as es2:
                        def sb2(name, shape, dt):
                            return es2.enter_context(SBT(name, shape, dt))

                        hT = sb2("ahT", [128, 16, 768], BF16)
                        wq = [sb2("wq%d" % i, [128, 16, 256], BF16) for i in range(2)]
                        r_wq = [Res(), Res()]
                        cosT = sb2("cosT", [128, 768], F32)
                        sinT = sb2("sinT", [128, 768], F32)
                        r_tab = Res()
                        qb = [sb2("qb%d" % i, [128, 384], BF16) for i in range(2)]
                        t1 = [sb2("t1%d" % i, [128, 384], F32) for i in range(2)]
                        t2 = [sb2("t2%d" % i, [128, 384], F32) for i in range(2)]
                        r_qb, r_t1, r_t2 = [Res(), Res()], [Res(), Res()], [Res(), Res()]
                        N = NormCtx(es2, "a")
                        pq = [es2.enter_context(PST("pq%d" % i, [128, 512], F32)) for i in range(2)]
                        pr = [es2.enter_context(PST("pr%d" % i, [128, 512], F32)) for i in range(2)]
                        pv = [es2.enter_context(PST("pv%d" % i, [128, 512], F32)) for i in range(2)]
                        r_pq, r_pr, r_pv = [PRes(), PRes()], [PRes(), PRes()], [PRes(), PRes()]
                        wi = 0
                        nq = 0
                        nv = 0
                        pending = [None]
                        for g in range(3):
                            tiles = list(range(g * 6, g * 6 + 6))
                            r_h = Res()
                            if CK(40):
                                return
                            for it, t in enumerate(tiles):
                                norm_tile(N, l, t, 0, hT[:, :, it * 128:(it + 1) * 128], r_h)
                                if CK(41):
                                    return
                            if CK(42):
                                return
                            lat0 = g * 768 - 256
                            p0 = max(lat0, 0)
                            ncols = 768 - (p0 - lat0)
                            P.dma("sp", cosT[:, p0 - lat0:768], cosT_d[:, p0:p0 + ncols], writes=[r_tab])
                            P.dma("sp", sinT[:, p0 - lat0:768], sinT_d[:, p0:p0 + ncols], writes=[r_tab])
                            for blk in range(12):
                                if blk == 1 and CK(43):
                                    return
                                if blk == 9 and CK(44):
                                    return
                                if blk == 11 and CK(45):
                                    return
                                k = wi % 2
                                wi += 1
                                P.dma("pool", wq[k][:], wqkv[j, :, blk * 256:(blk + 1) * 256].rearrange("(c p) n -> p c n", p=128),
                                      writes=[r_wq[k]])
                                if blk < 10:
                                    for hh in range(2):
                                        for half in range(2):
                                            c0 = half * 384
                                            a = nq % 2
                                            nq += 1
                                            for kc in range(16):
                                                MM(pq[a][:, 0:384], wq[k][:, kc, hh * 128:(hh + 1) * 128], hT[:, kc, c0:c0 + 384],
                                                   kc == 0, kc == 15, [r_wq[k], r_h], [r_pq[a]])
                                            if pending[0] is not None:
                                                pending[0]()

                                            def epi(a=a, blk=blk, hh=hh, half=half, g=g, c0=c0, tiles=tiles):
                                                if blk < 8:
                                                    head = blk * 2 + hh

                                                    def dst(ca, cb):
                                                        return qT[:, head, g * 768 + ca:g * 768 + cb]
                                                    wres = [r_q[t] for t in tiles[half * 3:half * 3 + 3]]
                                                else:
                                                    kvh = (blk - 8) * 2 + hh

                                                    def dst(ca, cb):
                                                        sa = g * 768 + ca
                                                        off = 0 if sa < 256 else 128
                                                        return kT[:, kvh, sa + off:sa + off + (cb - ca)]
                                                    wres = [r_k[kslot(t)] for t in tiles[half * 3:half * 3 + 3]]
                                                if g == 0 and half == 0:
                                                    segs = [(0, 256, False), (256, 384, True)]
                                                else:
                                                    segs = [(c0, c0 + 384, True)]
                                                for (ca, cb, rope) in segs:
                                                    la, lb_ = ca - c0, cb - c0
                                                    if not rope:
                                                        CP("act", dst(ca, cb), pq[a][:, la:lb_], [r_pq[a]], wres)
                                                    else:
                                                        CP("act", qb[a][:, la:lb_], pq[a][:, la:lb_], [r_pq[a]], [r_qb[a]])
                                                        MM(pr[a][:, la:lb_], rotm[:], qb[a][:, la:lb_], True, True, [r_qb[a], r_const],
                                                           [r_pr[a]])
                                                        TT("dve", t1[a][:, la:lb_], pq[a][:, la:lb_], cosT[:, ca:cb], ALU.mult,
                                                           [r_pq[a], r_tab], [r_t1[a]])
                                                        TT("dve", t2[a][:, la:lb_], pr[a][:, la:lb_], sinT[:, ca:cb], ALU.mult,
                                                           [r_pr[a], r_tab], [r_t2[a]])
                                                        TT("pool", dst(ca, cb), t1[a][:, la:lb_], t2[a][:, la:lb_], ALU.add,
                                                           [r_t1[a], r_t2[a]], wres)
                                            pending[0] = epi
                                else:
                                    if pending[0] is not None:
                                        pending[0]()
                                        pending[0] = None
                                    vh = blk - 10
                                    for it, t in enumerate(tiles):
                                        a = nv % 2
                                        nv += 1
                                        for kc in range(16):
                                            MM(pv[a][:, 0:256], hT[:, kc, it * 128:(it + 1) * 128], wq[k][:, kc, :], kc == 0, kc == 15,
                                               [r_wq[k], r_h], [r_pv[a]])
                                        CP("act", V[:, kslot(t), vh * 256:(vh + 1) * 256], pv[a][:, 0:256], [r_pv[a]], [r_v[kslot(t)]])
                    P.barrier()
                    if CK(4):
                        return
                    with contextlib.ExitStack() as es2:
                        def sb2(name, shape, dt):
                            return es2.enter_context(SBT(name, shape, dt))

                        amask = sb2("amask", [128, 3, 384], BF16)
                        r_am = Res()
                        P.dma("pool", amask[:], amask_d.rearrange("v q k -> q v k"), writes=[r_am])
                        pb = [sb2("pb%d" % i, [128, 640], BF16) for i in range(3)]
                        pTs = [sb2("pTs%d" % i, [128, 5, 128], BF16) for i in range(2)]
                        st = [sb2("ast%d" % i, [128, 8], F32) for i in range(3)]
                        oTs = [sb2("oTs%d" % i, [128, 16, 128], BF16) for i in range(2)]
                        r_pb, r_pTs, r_st, r_oTs = [Res(), Res(), Res()], [Res(), Res()], [Res(), Res(), Res()], [Res(), Res()]
                        S = [es2.enter_context(PST("S%d" % i, [128, 1024], F32)) for i in range(2)]
                        pTp = [es2.enter_context(PST("pTp%d" % i, [128, 8, 128], BF16)) for i in range(2)]
                        oTp = [es2.enter_context(PST("oTp%d" % i, [128, 4, 128], F32)) for i in range(2)]
                        r_S, r_pTp, r_oTp = [PRes(), PRes()], [PRes(), PRes()], [PRes(), PRes()]
                        qtiles = list(range(18)) if need_ctx else list(range(2, 18))
                        units = [(iq, t, h) for iq, t in enumerate(qtiles) for h in range(16)]

                        def stage_a(u):
                            iq, t, h = units[u]
                            kv = h // 4
                            a = u % 2
                            a3 = u % 3
                            qap = qT[:, h, t * 128:(t + 1) * 128]
                            if t < 2:
                                lo, hi = 512, 768
                                MM(S[a][:, 512:768], qap, kT[:, kv, 0:256], True, True, [r_q[t], r_k[0], r_k[1]], [r_S[a]])
                            else:
                                b = t - 2
                                lo, hi = 128, 768
                                kc0 = 256 + b * 128
                                var = 0 if b == 0 else (2 if b == 15 else 1)
                                MM(S[a][:, 128:512], qap, kT[:, kv, kc0:kc0 + 384], True, False,
                                   [r_q[t], r_k[2 + b], r_k[3 + b], r_k[4 + b]], [r_S[a]])
                                MM(S[a][:, 128:512], identb[:], amask[:, var, :], False, True, [r_am, r_const], [r_S[a]])
                                MM(S[a][:, 512:768], qap, kT[:, kv, 0:256], True, True, [r_q[t], r_k[0], r_k[1]], [r_S[a]])
                            n = hi - lo
                            sa = st[a3]
                            P.op("dve", lambda e, sa=sa, Sa=S[a], lo=lo, hi=hi: e.reduce_max(out=sa[:, 0:1], in_=Sa[:, lo:hi], axis=AX.X),
                                 [r_S[a]], [r_st[a3]])
                            TS("dve", sa[:, 1:2], sa[:, 0:1], sinkr[:, j * 16 + h:j * 16 + h + 1], -SCALE, ALU.max, ALU.mult,
                               [r_st[a3], r_const], [r_st[a3]])
                            ACT(pb[a3][:, 0:n], S[a][:, lo:hi], AF.Exp, [r_S[a], r_st[a3]], [r_pb[a3], r_st[a3]], scale=SCALE,
                                bias=sa[:, 1:2], accum=sa[:, 2:3])
                            ACT(sa[:, 3:4], sinkr[:, j * 16 + h:j * 16 + h + 1], AF.Exp, [r_st[a3], r_const], [r_st[a3]], scale=SCALE,
                                bias=sa[:, 1:2])

                        def stage_a2(u):
                            iq, t, h = units[u]
                            a3 = u % 3
                            n = 256 if t < 2 else 640
                            sa = st[a3]
                            TT("dve", sa[:, 4:5], sa[:, 2:3], sa[:, 3:4], ALU.add, [r_st[a3]], [r_st[a3]])
                            P.op("dve", lambda e, sa=sa: e.reciprocal(out=sa[:, 5:6], in_=sa[:, 4:5]), [r_st[a3]], [r_st[a3]])
                            TS("dve", pb[a3][:, 0:n], pb[a3][:, 0:n], sa[:, 5:6], None, ALU.mult, None, [r_pb[a3], r_st[a3]], [r_pb[a3]])

                        def stage_b(u):
                            iq, t, h = units[u]
                            kv = h // 4
                            a = u % 2
                            a3 = u % 3
                            ob = iq % 2
                            if t < 2:
                                kslots = [0, 1]
                            else:
                                b = t - 2
                                kslots = [2 + b, 3 + b, 4 + b, 0, 1]
                            nk = len(kslots)
                            for jk in range(nk):
                                TR(pTp[a][:, jk, :], pb[a3][:, jk * 128:(jk + 1) * 128], identb[:], [r_pb[a3], r_const], [r_pTp[a]])
                            CP("act", pTs[a][:, 0:nk, :], pTp[a][:, 0:nk, :], [r_pTp[a]], [r_pTs[a]])
                            oa = (u // 4) % 2
                            hh = h % 4
                            for jk in range(nk):
                                MM(oTp[oa][:, hh, :], V[:, kslots[jk], kv * 128:(kv + 1) * 128], pTs[a][:, jk, :], jk == 0,
                                   jk == nk - 1, [r_pTs[a], r_v[kslots[jk]]], [r_oTp[oa]])
                            if hh == 3:
                                CP("dve", oTs[ob][:, h - 3:h + 1, :], oTp[oa][:], [r_oTp[oa]], [r_oTs[ob]])
                            if h == 15:
                                P.dma("sp", yT_d[:, :, t * 128:(t + 1) * 128].rearrange("c p t -> p c t"), oTs[ob][:],
                                      reads=[r_oTs[ob]], writes=[r_yT[t]])

                        nu_ = len(units)
                        for u in range(nu_ + 2):
                            if u < nu_:
                                stage_a(u)
                            if 1 <= u <= nu_:
                                stage_a2(u - 1)
                            if u >= 2:
                                stage_b(u - 2)
                P.barrier()

            def phase_rec(l, j, need_ctx):
                if j == 0:
                    MS("dve", lbt[:, 0, :], 0.0, [r_lbt])
                else:
                    TT("dve", lbt[:, 0, :], lbl[:, 32:64], lbl[:, 0:32], ALU.subtract, [r_const], [r_lbt])
                    ACT(lbt[:, 0, :], lbt[:, 0, :], AF.Sigmoid, [r_lbt], [r_lbt])
                TS("dve", lbt[:, 1, :], lbt[:, 0, :], -1.0, 1.0, ALU.mult, ALU.add, [r_lbt], [r_lbt])
                TS("dve", lbt[:, 2, :], lbt[:, 1, :], -1.0, None, ALU.mult, None, [r_lbt], [r_lbt])
                with contextlib.ExitStack() as es:
                    N = NormCtx(es, "r")
                    hs = [es.enter_context(SBT("hs%d" % i, [128, 16, 128], BF16)) for i in range(2)]
                    r_hs = [Res(), Res()]
                    for t in range(18):
                        k = t % 2
                        norm_tile(N, l, t, 0, hs[k][:], r_hs[k])
                        P.dma("sp", hT_d[:, :, t * 128:(t + 1) * 128].rearrange("c p t -> p c t"), hs[k][:], reads=[r_hs[k]],
                              writes=[r_hT[t]])
                P.barrier()
                with contextlib.ExitStack() as es:
                    def sb(name, shape, dt):
                        return es.enter_context(SBT(name, shape, dt))

                    def ps(name, shape, dt):
                        return es.enter_context(PST(name, shape, dt))

                    wh = [sb("wh%d" % i, [128, 16, 5, 128], BF16) for i in range(2)]
                    r_wh = [Res(), Res()]
                    hp = [sb("hp%d" % i, [128, 16, 384], BF16) for i in range(2)]
                    r_hp = [Res(), Res()]
                    rmask = sb("rmask", [128, 2, 128], BF16)
                    onb = sb("onb", [128, 128], F32)
                    r_rc = Res()
                    P.dma("pool", rmask[:], rmask_d.rearrange("v s t -> s v t"), writes=[r_rc])
                    P.dma("sp", onb[:], onorm_d[j:j + 1, :].partition_broadcast(128), writes=[r_rc])
                    qsp = [sb("qsp%d" % d, [128, 36, 128], BF16) for d in range(2)]
                    ksp = [sb("ksp%d" % d, [128, 36, 128], BF16) for d in range(2)]
                    r_qsp = [[Res() for _ in range(6)] for _ in range(2)]
                    r_ksp = [[Res() for _ in range(6)] for _ in range(2)]
                    for d in range(2):
                        MS("pool", qsp[d][:], 0.0, r_qsp[d])
                        MS("pool", ksp[d][:], 0.0, r_ksp[d])
                    Dd = [sb("Dd%d" % d, [128, 36], F32) for d in range(2)]
                    Hm = [sb("Hm%d" % d, [128, 36], F32) for d in range(2)]
                    Gm = [sb("Gm%d" % d, [128, 36], F32) for d in range(2)]
                    utmp = [[sb("utmp%d_%d" % (d, i), [128, 128], F32) for i in range(2)] for d in range(2)]
                    r_ut = [[Res(), Res()], [Res(), Res()]]
                    atmp = [sb("atmp%d" % i, [128, 4, 128], BF16) for i in range(2)]
                    r_at = [Res(), Res()]
                    Sbf = [sb("Sbf%d" % d, [128, 36, 128], BF16) for d in range(2)]
                    ATm = [sb("ATm%d" % d, [128, 18, 128], BF16) for d in range(2)]
                    vtok = sb("vtok", [128, 18, 128], BF16)
                    gg = sb("gg", [128, 18, 128], F32)
                    yTh = sb("yTh", [128, 2304], BF16)
                    Tst = [[sb("Tst%d_%d" % (d, i), [128, 128], F32) for i in range(2)] for d in range(2)]
                    ktok = [sb("ktok%d" % i, [128, 4, 128], BF16) for i in range(2)]
                    r_ktok = [Res(), Res()]
                    r_D, r_Sbf, r_AT = [Res(), Res()], [Res(), Res()], [Res(), Res()]
                    r_T = [[Res(), Res()], [Res(), Res()]]
                    r_vt, r_gg, r_yTh = Res(), Res(), Res()
                    qs2 = [sb("qs%d" % i, [128, 384], F32) for i in range(2)]
                    tF2 = [[sb("tF%d_%d" % (i, d), [128, 384], F32) for d in range(2)] for i in range(2)]
                    tK2 = [[sb("tK%d_%d" % (i, d), [128, 384], F32) for d in range(2)] for i in range(2)]
                    Bz2 = [[sb("Bz%d_%d" % (i, d), [128, 385], F32) for d in range(2)] for i in range(2)]
                    tE2 = [[sb("tE%d_%d" % (i, d), [128, 384], F32) for d in range(2)] for i in range(2)]
                    tX2 = [[sb("tX%d_%d" % (i, d), [128, 384], F32) for d in range(2)] for i in range(2)]
                    dD2 = [[sb("dD%d_%d" % (i, d), [128, 18], F32) for d in range(2)] for i in range(2)]
                    r_qs2 = [Res(), Res()]
                    r_tF2, r_tK2, r_Bz2, r_tE2, r_tX2, r_dD2 = ([[Res(), Res()], [Res(), Res()]] for _ in range(6))
                    for i in range(2):
                        for d in range(2):
                            MS("dve", Bz2[i][d][:, 0:1], 0.0, [r_Bz2[i][d]])
                    ost = sb("ost", [128, 4, 4], F32)
                    ojunk = sb("ojunk", [128, 128], BF16)
                    yb = [sb("yb%d" % i, [128, 128], BF16) for i in range(4)]
                    r_ost, r_oj = Res(), Res()
                    r_yb = [Res() for _ in range(4)]
                    zq_ = ps("zq", [128, 512], F32)
                    zq = zq_[:, 0:384]
                    zf_ = [ps("zf%d" % d, [128, 512], F32) for d in range(2)]
                    zf = [z[:, 0:384] for z in zf_]
                    vg = ps("vg", [128, 2, 256], F32)
                    pA = ps("pA", [128, 4, 128], F32)
                    pTr_ = ps("pTr", [128, 2, 4, 128], BF16)
                    pTr = [pTr_[:, i] for i in range(2)]
                    pU = [ps("pU%d" % i, [128, 4, 128], F32) for i in range(2)]
                    r_zq, r_pA = PRes(), PRes()
                    _rp = PRes()
                    r_pTr = [_rp, _rp]
                    r_zf = [PRes(), PRes()]
                    _rv = PRes()
                    r_vg = [_rv, _rv]
                    r_pU = [PRes(), PRes()]

                    nwh = 0
                    nhp = 0
                    nvg = 0
                    for h in range(16):
                        k = nwh % 2
                        nwh += 1
                        for si, part in enumerate((0, 2, 3, 1, 4)):
                            c0 = part * 2048 + h * 128
                            P.dma("pool", wh[k][:, :, si, :], win[j, :, c0:c0 + 128].rearrange("(c p) n -> p c n", p=128),
                                  writes=[r_wh[k]])
                        for pc in range(6):
                            kh = nhp % 2
                            nhp += 1
                            tiles = [pc * 3 + i for i in range(3)]
                            P.dma("sp", hp[kh][:], hT_d[:, :, pc * 384:(pc + 1) * 384].rearrange("c p t -> p c t"),
                                  reads=[r_hT[t] for t in tiles], writes=[r_hp[kh]])
                            for kc in range(16):
                                MM(zq, wh[k][:, kc, 0, :], hp[kh][:, kc, :], kc == 0, kc == 15, [r_wh[k], r_hp[kh]], [r_zq])
                            for d in range(2):
                                for kc in range(16):
                                    MM(zf[d], wh[k][:, kc, 1 + d, :], hp[kh][:, kc, :], kc == 0, kc == 15, [r_wh[k], r_hp[kh]],
                                       [r_zf[d]])
                            for i, t in enumerate(tiles):
                                a = nvg % 2
                                nvg += 1
                                for kc in range(16):
                                    MM(vg[:, a, :], hp[kh][:, kc, i * 128:(i + 1) * 128], wh[k][:, kc, 3:5, :], kc == 0, kc == 15,
                                       [r_wh[k], r_hp[kh]], [r_vg[a]])
                                CP("act", vtok[:, t, :], vg[:, a, 0:128], [r_vg[a]], [r_vt])
                                ACT(gg[:, t, :], vg[:, a, 128:256], AF.Silu, [r_vg[a]], [r_gg])
                                TT("pool", gg[:, t, :], gg[:, t, :], onb[:], ALU.mult, [r_gg, r_rc], [r_gg])
                            pp = pc % 2
                            qs, tF, tK, Bz, tE, tX, dD = qs2[pp], tF2[pp], tK2[pp], Bz2[pp], tE2[pp], tX2[pp], dD2[pp]
                            r_qs, r_tF, r_tK, r_Bz, r_tE, r_tX, r_dD = (r_qs2[pp], r_tF2[pp], r_tK2[pp], r_Bz2[pp], r_tE2[pp],
                                                                        r_tX2[pp], r_dD2[pp])
                            ACT(qs[:], zq, AF.Silu, [r_zq], [r_qs])
                            ch0 = pc * 6
                            for d in range(2):
                                ACT(tF[d][:], zf[d], AF.Sigmoid, [r_zf[d]], [r_tF[d]])
                            for d in range(2):
                                col = d * 16 + h
                                TS("dve", tK[d][:], tF[d][:], lbt[:, 2, col:col + 1], lbt[:, 1, col:col + 1], ALU.mult, ALU.add,
                                   [r_tF[d], r_lbt], [r_tK[d]])
                                TS("dve", tF[d][:], tF[d][:], lbt[:, 1, col:col + 1], lbt[:, 0, col:col + 1], ALU.mult, ALU.add,
                                   [r_tF[d], r_lbt], [r_tF[d]])
                            for d in range(2):
                                ACT(tF[d][:], tF[d][:], AF.Ln, [r_tF[d]], [r_tF[d]])
                            for d in range(2):
                                P.op("dve", lambda e, bo=Bz[d][:, 1:385], fi=tF[d][:]: e.tensor_tensor_scan(
                                    out=bo, data0=ones1[:].to_broadcast([128, 384]), data1=fi, initial=0.0, op0=ALU.mult, op1=ALU.add),
                                     [r_tF[d], r_const], [r_Bz[d]])
                                bzc = Bz[d][:, 0:384].rearrange("p (c j) -> p c j", j=64)
                                bze = Bz[d][:, 1:385].rearrange("p (c j) -> p c j", j=64)
                                in0 = bze if d == 0 else bzc
                                in1 = bzc[:, :, 32:33].to_broadcast([128, 6, 64])
                                TT("dve", tE[d][:].rearrange("p (c j) -> p c j", j=64), in0, in1, ALU.subtract, [r_Bz[d]], [r_tE[d]])
                                TT("dve", dD[d][:, 0:6].rearrange("p (c o) -> p c o", o=1), bzc[:, :, 32:33], bzc[:, :, 0:1],
                                   ALU.subtract, [r_Bz[d]], [r_dD[d]])
                                TT("dve", dD[d][:, 6:12].rearrange("p (c o) -> p c o", o=1), bze[:, :, 63:64], bzc[:, :, 32:33],
                                   ALU.subtract, [r_Bz[d]], [r_dD[d]])
                                TT("dve", dD[d][:, 12:18].rearrange("p (c o) -> p c o", o=1), bze[:, :, 63:64], bzc[:, :, 0:1],
                                   ALU.subtract, [r_Bz[d]], [r_dD[d]])
                            for d in range(2):
                                ACT(tX[d][:], tE[d][:], AF.Exp, [r_tE[d]], [r_tX[d]])
                                ACT(tE[d][:], tE[d][:], AF.Exp, [r_tE[d]], [r_tE[d]], scale=-1.0)
                                ha, ga = (0, 6) if d == 0 else (6, 0)
                                ACT(Hm[d][:, ch0:ch0 + 6], dD[d][:, ha:ha + 6], AF.Exp, [r_dD[d]], [r_D[d]])
                                ACT(Gm[d][:, ch0:ch0 + 6], dD[d][:, ga:ga + 6], AF.Exp, [r_dD[d]], [r_D[d]])
                                ACT(Dd[d][:, ch0:ch0 + 6], dD[d][:, 12:18], AF.Exp, [r_dD[d]], [r_D[d]])
                            for d in range(2):
                                qfac, kfac = (tX[d], tE[d]) if d == 0 else (tE[d], tX[d])
                                for par in range(2):
                                    o_q = qsp[d][:, ch0 + par:ch0 + 6:2, par * 64:par * 64 + 64]
                                    o_k = ksp[d][:, ch0 + par:ch0 + 6:2, par * 64:par * 64 + 64]
                                    v = lambda tl: tl[:].rearrange("p (c j) -> p c j", j=64)[:, par:6:2, :]
                                    TT("pool", o_q, v(qs), v(qfac), ALU.mult, [r_qs, r_tX[d], r_tE[d]], [r_qsp[d][pc]])
                                    TT("pool" if par else "dve", o_k, v(tK[d]), v(kfac), ALU.mult, [r_tK[d], r_tX[d], r_tE[d]], [r_ksp[d][pc]])
                        if stop == 80 and h == 0:
                            DUMP(0, qs[:], 384, [r_qs])
                            for d in range(2):
                                DUMP(1 + d * 5, tF[d][:], 384, [r_tF[d]])
                                DUMP(2 + d * 5, tK[d][:], 384, [r_tK[d]])
                                DUMP(3 + d * 5, Bz[d][:], 385, [r_Bz[d]])
                                DUMP(4 + d * 5, tE[d][:], 384, [r_tE[d]])
                                DUMP(5 + d * 5, tX[d][:], 384, [r_tX[d]])
                            DUMP(11, Dd[0][:], 36, [r_D[0]])
                            DUMP(12, Dd[1][:], 36, [r_D[1]])
                            DUMP(13, gg[:].rearrange("p t v -> p (t v)")[:, 0:2048], 2048, [r_gg])
                            DUMP(14, qsp[0][:].rearrange("p c j -> p (c j)")[:, 0:2048], 2048, r_qsp[0], bf=True)
                            DUMP(15, ksp[0][:].rearrange("p c j -> p (c j)")[:, 0:2048], 2048, r_ksp[0], bf=True)
                            DUMP(16, vtok[:].rearrange("p t v -> p (t v)")[:, 0:2048], 2048, [r_vt], bf=True)
                        ctiles = list(range(18)) if need_ctx else list(range(2, 18))
                        seqs = [list(range(36)), [3, 2, 1, 0] + list(range(35, 3, -1))]
                        for d in range(2):
                            MS("pool", Sbf[d][:, seqs[d][0], :], 0.0, [r_Sbf[d]])
                        for g4 in range(9):
                            for d in range(2):
                                cs = seqs[d][g4 * 4:(g4 + 1) * 4]
                                for i, c in enumerate(cs):
                                    TR(pTr[d][:, i, :], ksp[d][:, c, :], identb[:], [r_ksp[d][c // 6], r_const], [r_pTr[d]])
                                CP("act", ktok[d][:], pTr[d], [r_pTr[d]], [r_ktok[d]])
                                for i, c in enumerate(cs):
                                    MM(pU[d][:, i, :], ktok[d][:, i, :], vtok[:, c // 2, :], True, True, [r_ktok[d], r_vt],
                                       [r_pU[d]])
                            for i in range(4):
                                n = g4 * 4 + i
                                for d in range(2):
                                    c = seqs[d][n]
                                    Sc, Sn = Tst[d][n % 2], Tst[d][(n + 1) % 2]
                                    rc_, rn = r_T[d][n % 2], r_T[d][(n + 1) % 2]
                                    if n == 0:
                                        TS("dve", Sn[:], pU[d][:, i, :], Gm[d][:, c:c + 1], None, ALU.mult, None, [r_pU[d], r_D[d]], [rn])
                                    else:
                                        ut, rut = utmp[d][n % 2], r_ut[d][n % 2]
                                        TS("dve", ut[:], pU[d][:, i, :], Gm[d][:, c:c + 1], None, ALU.mult, None, [r_pU[d], r_D[d]], [rut])
                                        TT("pool", Sbf[d][:, c, :], Sc[:], Hm[d][:, c:c + 1].to_broadcast([128, 128]), ALU.mult, [rc_, r_D[d]], [r_Sbf[d]])
                                        if n < 35:
                                            STT("dve", Sn[:], Sc[:], Dd[d][:, c:c + 1], ut[:], ALU.mult, ALU.add, [rc_, r_D[d], rut], [rn])
                        for d in range(2):
                            for g4 in range(0, 18, 4):
                                ts_ = list(range(g4, min(g4 + 4, 18)))
                                for i, t in enumerate(ts_):
                                    for c in (2 * t, 2 * t + 1):
                                        MM(pA[:, i, :], ksp[d][:, c, :], qsp[d][:, c, :], c == 2 * t, c == 2 * t + 1,
                                           [r_ksp[d][c // 6], r_qsp[d][c // 6]], [r_pA])
                                n_ = len(ts_)
                                ka = (g4 // 4) % 2
                                CP("act", atmp[ka][:, 0:n_, :], pA[:, 0:n_, :], [r_pA], [r_at[ka]])
                                cm = -1 if d == 0 else 1
                                P.op("pool", lambda e, o=ATm[d][:, g4:g4 + n_, :], i_=atmp[ka][:, 0:n_, :], n_=n_, cm=cm:
                                     e.affine_select(out=o, in_=i_, pattern=[[0, n_], [-cm, 128]], compare_op=ALU.is_ge, fill=0.0,
                                                     base=0, channel_multiplier=cm), [r_at[ka]], [r_AT[d]])
                        for g4 in range(0, 18, 4):
                            ts_ = [t for t in range(g4, min(g4 + 4, 18))]
                            for i, t in enumerate(ts_):
                                if t not in ctiles:
                                    continue
                                mms = []
                                for d in range(2):
                                    mms.append((ATm[d][:, t, :], vtok[:, t, :], [r_AT[d], r_vt]))
                                    for c in (2 * t, 2 * t + 1):
                                        mms.append((qsp[d][:, c, :], Sbf[d][:, c, :], [r_qsp[d][c // 6], r_Sbf[d]]))
                                for mi, (lh, rh, rr) in enumerate(mms):
                                    MM(pA[:, i, :], lh, rh, mi == 0, mi == len(mms) - 1, rr, [r_pA])
                            for i, t in enumerate(ts_):
                                if t not in ctiles:
                                    continue
                                ACT(ojunk[:], pA[:, i, :], AF.Square, [r_pA], [r_oj, r_ost], accum=ost[:, i, 0:1])
                            if any(t in ctiles for t in ts_):
                                ACT(ost[:, :, 1], ost[:, :, 0], AF.Sqrt, [r_ost, r_const], [r_ost], scale=1.0 / 128, bias=epsT[:, 0:1])
                                P.op("dve", lambda e: e.reciprocal(out=ost[:, :, 2], in_=ost[:, :, 1]), [r_ost], [r_ost])
                            for i, t in enumerate(ts_):
                                if t not in ctiles:
                                    continue
                                STT("dve", yb[i][:], pA[:, i, :], ost[:, i, 2:3], gg[:, t, :], ALU.mult, ALU.mult, [r_pA, r_ost, r_gg],
                                    [r_yb[i]])
                            for i, t in enumerate(ts_):
                                if t not in ctiles:
                                    continue
                                TR(pTr[0][:, i, :], yb[i][:], identb[:], [r_yb[i], r_const], [r_pTr[0]])
                            live = [i for i, t in enumerate(ts_) if t in ctiles]
                            if live:
                                i0, i1 = live[0], live[-1] + 1
                                t0 = ts_[i0]
                                CP("act", yTh[:, t0 * 128:(t0 + i1 - i0) * 128].rearrange("p (i t) -> p i t", t=128), pTr[0][:, i0:i1, :],
                                   [r_pTr[0]], [r_yTh])
                        if stop == 80 and h == 0:
                            DUMP(17, yTh[:, 0:2048], 2048, [r_yTh], bf=True)
                            halt[0] = True
                            return
                        c_lo = ctiles[0] * 128
                        P.dma("sp", yT_d[h, :, c_lo:2304], yTh[:, c_lo:2304], reads=[r_yTh], writes=[r_yT[t] for t in ctiles])
                P.barrier()

            for l in range(n_layers):
                last = l == 3
                j = l // 2
                cur[0] = l
                load_mods(l)
                if CK(3):
                    return
                if l % 2 == 0:
                    phase_att(l, j, not last)
                    wo_src = wo_att[j]
                else:
                    phase_rec(l, j, not last)
                    wo_src = wo_rec[j]
                if CK(5):
                    return
                tiles = list(range(2, 18)) if last else list(range(18))
                if stop == 69:
                    tiles = [17]
                if stop == 70:
                    tiles = list(range(17, -1, -1))
                phase_wo(l, wo_src, tiles)
                if CK(6):
                    return
                if last:
                    groups = [[2, 3, 4, 5], list(range(6, 12)), list(range(12, 18))]
                else:
                    groups = [list(range(0, 6)), list(range(6, 12)), list(range(12, 18))]
                phase_mlp(l, groups, last)

            if n_layers == 4:
                with contextlib.ExitStack() as es:
                    fg = es.enter_context(SBT("fg", [128, 2048], F32))
                    r_fg = Res()
                    P.dma("sp", fg[:], fing_d.partition_broadcast(128), writes=[r_fg])
                    xf = [es.enter_context(SBT("xf%d" % i, [128, 2048], F32)) for i in range(2)]
                    fj = [es.enter_context(SBT("fj%d" % i, [128, 2048], BF16)) for i in range(2)]
                    fss = [es.enter_context(SBT("fss%d" % i, [128, 4], F32)) for i in range(2)]
                    r_xf = [Res(), Res()]
                    for t in range(2, 18):
                        k = t % 2
                        ss = fss[k]
                        P.dma("sp", xf[k][:], res[t * 128:(t + 1) * 128, :], reads=[r_res[t]], writes=[r_xf[k]])
                        ACT(fj[k][:], xf[k][:], AF.Square, [r_xf[k]], [r_xf[k]], accum=ss[:, 0:1])
                        ACT(ss[:, 1:2], ss[:, 0:1], AF.Sqrt, [r_xf[k], r_const], [r_xf[k]], scale=1.0 / 2048, bias=epsT[:, 0:1])
                        P.op("dve", lambda e, ss=ss: e.reciprocal(out=ss[:, 2:3], in_=ss[:, 1:2]), [r_xf[k]], [r_xf[k]])
                        STT("dve", xf[k][:], xf[k][:], ss[:, 2:3], fg[:], ALU.mult, ALU.mult, [r_xf[k], r_fg], [r_xf[k]])
                        P.dma("sp", out[(t - 2) * 128:(t - 1) * 128, :], xf[k][:], reads=[r_xf[k]], writes=[r_out])
        try:
            main_body()
        except _Stop:
            pass
        fin_reads = [r_out]
        if dbg and stop == 2:
            r_dbg = Res()
            P.dma("sp", dbg_o[0:128, 0:768], fmA[:].rearrange("p l j s -> p (l j s)"), reads=[r_fmA], writes=[r_dbg])
            fin_reads.append(r_dbg)
        elif dbg and dumped[0]:
            fin_reads.append(r_dump)
        elif dbg:
            r_dbg = Res()
            for t in range(18):
                P.dma("sp", dbg_o[t * 128:(t + 1) * 128, :], (xin if n_layers == 0 else res)[t * 128:(t + 1) * 128, :],
                      reads=[r_res[t]], writes=[r_dbg])
            fin_reads.append(r_dbg)
        P.op("sp", lambda e: e.nop(), fin_reads, ())
        P.emit(nc)
    return nc


_CONSTS = None


def make_in_maps(inputs, cores):
    global _CONSTS
    if _CONSTS is None:
        _CONSTS = make_consts()
    f = lambda a: np.ascontiguousarray(np.asarray(a, dtype=np.float32))
    x, c, ctx, c_ctx = f(inputs["x"]), f(inputs["c"]), f(inputs["ctx"]), f(inputs["c_ctx"])

    def fm(v):
        v = v.reshape(-1, 16, 128)
        return np.ascontiguousarray(v.transpose(2, 0, 1).reshape(128, -1))

    shared = {
        "w_ada": f(inputs["w_ada"]),
        "bada": np.ascontiguousarray(f(inputs["b_ada"]).reshape(4, 96, 128).transpose(2, 0, 1).reshape(128, 384)),
        "gmix": fm(f(inputs["norm_mix_g"])), "gmlp": fm(f(inputs["norm_mlp_g"])),
        "att_w_qkv": f(inputs["att_w_qkv"]), "att_w_o": f(inputs["att_w_o"]),
        "att_sink": f(inputs["att_sink"]).reshape(1, 32),
        "rec_w_in": f(inputs["rec_w_in"]), "rec_w_o": f(inputs["rec_w_o"]),
        "lbl": fm(f(inputs["rec_lb_logits"])), "rec_onorm_g": f(inputs["rec_onorm_g"]),
        "mlp_w_up": f(inputs["mlp_w_up"]), "mlp_w_down": f(inputs["mlp_w_down"]),
        "final_norm_g": f(inputs["final_norm_g"]).reshape(1, 2048),
    }
    shared.update(_CONSTS)
    maps = []
    for b in cores:
        m = dict(shared)
        m["xin"] = np.ascontiguousarray(np.concatenate([ctx[b], x[b]], axis=0))
        cc = np.stack([c[b], c_ctx], axis=0)
        m["ccl"] = np.ascontiguousarray(cc.reshape(2, 16, 128).transpose(2, 1, 0).reshape(128, 32))
        maps.append(m)
    return maps


def kernel(**inputs):
    nc = build(4, False)
    maps = make_in_maps(inputs, list(range(8)))
    r = run_bass_kernel_spmd(nc, maps, core_ids=list(range(8)))
    return np.stack([np.asarray(r.results[b]["out"], dtype=np.float32) for b in range(8)], axis=0)
```
